# Optimizing a Trainium2 kernel written in Bass

```python
import math
import jax, jax.numpy as jnp
from jax import lax
import numpy as np

D_MODEL = 1024
BATCH = 8
SEQ = 2048
DEPTH = 1
DEC_BATCH = 128
DEC_SEQ = 1
PAST_LEN = 16384
PAGE_SIZE = 128

PLE_DIM = 256
CONV_WIDTH = D_MODEL
CONV_K = 31
SSM_WIDTH = D_MODEL // 2
SSM_GROUP = 16
SSM_GROUPS = SSM_WIDTH // SSM_GROUP
SSM_STATE = 64
EPS = 1e-6
DT_MIN = 1e-3
DT_MAX = 1e-1
IN_COLS = 3 * CONV_WIDTH + 2 * SSM_WIDTH + 2 * D_MODEL

kernel_name = "conformer_s5_gated_hybrid_step"


def rmsnorm(x, g):
    xf = x.astype(jnp.float32)
    y = xf * lax.rsqrt(jnp.mean(xf * xf, axis=-1, keepdims=True) + EPS)
    return (y * g.astype(jnp.float32)).astype(x.dtype)


def layernorm(x, g, b):
    xf = x.astype(jnp.float32)
    mu = jnp.mean(xf, axis=-1, keepdims=True)
    xc = xf - mu
    y = xc * lax.rsqrt(jnp.mean(xc * xc, axis=-1, keepdims=True) + EPS)
    return (y * g.astype(jnp.float32) + b.astype(jnp.float32)).astype(x.dtype)


def causal_depthwise_conv(u, buf, w, b):
    ext = jnp.concatenate([buf.astype(u.dtype), u], axis=1)
    out = lax.conv_general_dilated(
        ext, w.astype(u.dtype)[:, None, :], window_strides=(1,), padding='VALID',
        dimension_numbers=('NWC', 'WIO', 'NWC'), feature_group_count=u.shape[-1])
    return out + b.astype(u.dtype), ext[:, -(CONV_K - 1):]


def ssm_discretise(a_re, a_im, log_dt, b_re, b_im):
    dt = jnp.exp(log_dt.astype(jnp.float32))[:, None]
    a_re = a_re.astype(jnp.float32)
    a_im = a_im.astype(jnp.float32)
    mag = jnp.exp(a_re * dt)
    lb_re = mag * jnp.cos(a_im * dt)
    lb_im = mag * jnp.sin(a_im * dt)
    n_re, n_im = lb_re - 1.0, lb_im
    den = a_re * a_re + a_im * a_im
    f_re = (n_re * a_re + n_im * a_im) / den
    f_im = (n_im * a_re - n_re * a_im) / den
    b_re = b_re.astype(jnp.float32)
    b_im = b_im.astype(jnp.float32)
    bb_re = f_re[:, :, None] * b_re - f_im[:, :, None] * b_im
    bb_im = f_re[:, :, None] * b_im + f_im[:, :, None] * b_re
    return lb_re, lb_im, bb_re, bb_im


def _ssm_combine(e1, e2):
    a1r, a1i, b1r, b1i = e1
    a2r, a2i, b2r, b2i = e2
    ar = a1r * a2r - a1i * a2i
    ai = a1r * a2i + a1i * a2r
    br = a2r * b1r - a2i * b1i + b2r
    bi = a2r * b1i + a2i * b1r + b2i
    return ar, ai, br, bi


def ssm_branch(u, s0_re, s0_im, a_re, a_im, log_dt, b_re, b_im, c_re, c_im, d):
    bsz, L, _ = u.shape
    uf = u.astype(jnp.float32)
    ug = uf.reshape(bsz, L, SSM_GROUPS, SSM_GROUP)
    lb_re, lb_im, bb_re, bb_im = ssm_discretise(a_re, a_im, log_dt, b_re, b_im)
    bu_re = jnp.einsum('blgc,gpc->blgp', ug, bb_re)
    bu_im = jnp.einsum('blgc,gpc->blgp', ug, bb_im)
    full = (bsz, L, SSM_GROUPS, SSM_STATE)
    ar = jnp.broadcast_to(lb_re, full)
    ai = jnp.broadcast_to(lb_im, full)
    pr, pi_, sr, si = lax.associative_scan(_ssm_combine, (ar, ai, bu_re, bu_im), axis=1)
    s0r = s0_re.astype(jnp.float32)[:, None]
    s0i = s0_im.astype(jnp.float32)[:, None]
    s_re = sr + pr * s0r - pi_ * s0i
    s_im = si + pr * s0i + pi_ * s0r
    y = (jnp.einsum('blgp,gcp->blgc', s_re, c_re.astype(jnp.float32))
         - jnp.einsum('blgp,gcp->blgc', s_im, c_im.astype(jnp.float32)))
    y = y.reshape(bsz, L, SSM_WIDTH) + d.astype(jnp.float32) * uf
    return y.astype(u.dtype), s_re[:, -1], s_im[:, -1]


def mixer(h, conv_buf, s0_re, s0_im, w_in, conv_w, conv_b, ln_g, ln_b, ssm_a_re, ssm_a_im,
          ssm_log_dt, ssm_b_re, ssm_b_im, ssm_c_re, ssm_c_im, ssm_d, w_glu, b_glu, w_pc, w_ps, w_out):
    proj = h @ w_in
    o = 0
    c_a = proj[..., o:o + CONV_WIDTH]; o += CONV_WIDTH
    c_b = proj[..., o:o + CONV_WIDTH]; o += CONV_WIDTH
    c_z = proj[..., o:o + CONV_WIDTH]; o += CONV_WIDTH
    s_x = proj[..., o:o + SSM_WIDTH]; o += SSM_WIDTH
    s_z = proj[..., o:o + SSM_WIDTH]; o += SSM_WIDTH
    g_c = jax.nn.sigmoid(proj[..., o:o + D_MODEL]); o += D_MODEL
    g_s = jax.nn.sigmoid(proj[..., o:o + D_MODEL])
    u = c_a * jax.nn.sigmoid(c_b)
    v, new_buf = causal_depthwise_conv(u, conv_buf, conv_w, conv_b)
    v = jax.nn.silu(layernorm(v, ln_g, ln_b)) * jax.nn.silu(c_z)
    y_c = v @ w_pc
    ys, sr, si = ssm_branch(s_x, s0_re, s0_im, ssm_a_re, ssm_a_im, ssm_log_dt,
                            ssm_b_re, ssm_b_im, ssm_c_re, ssm_c_im, ssm_d)
    ys = jax.nn.gelu(ys)
    ys = ys * jax.nn.sigmoid(ys @ w_glu + b_glu)
    ys = ys * jax.nn.silu(s_z)
    y_s = ys @ w_ps
    m = g_c * y_c + g_s * y_s
    return m @ w_out, new_buf, sr, si


def trunk(x, p, conv_bufs, s_res, s_ims, W):
    (g_pre, w_in, conv_w, conv_b, ln_g, ln_b, ssm_a_re, ssm_a_im, ssm_log_dt, ssm_b_re, ssm_b_im,
     ssm_c_re, ssm_c_im, ssm_d, w_glu, b_glu, w_pc, w_ps, w_out, g_post, w_ple, g_ple, w_pg, b_pg) = W
    bufs, srs, sis = [], [], []
    for i in range(DEPTH):
        h = rmsnorm(x, g_pre[i])
        o, nb, sr, si = mixer(h, conv_bufs[i], s_res[i], s_ims[i], w_in[i], conv_w[i], conv_b[i],
                              ln_g[i], ln_b[i], ssm_a_re[i], ssm_a_im[i], ssm_log_dt[i], ssm_b_re[i],
                              ssm_b_im[i], ssm_c_re[i], ssm_c_im[i], ssm_d[i], w_glu[i], b_glu[i],
                              w_pc[i], w_ps[i], w_out[i])
        x = x + rmsnorm(o, g_post[i])
        e = rmsnorm(p[i].astype(x.dtype) @ w_ple[i], g_ple[i])
        x = x + e * jax.nn.sigmoid(x @ w_pg[i] + b_pg[i])
        bufs.append(nb); srs.append(sr); sis.append(si)
    return x, jnp.stack(bufs), jnp.stack(srs), jnp.stack(sis)


def setup_inputs(seed: int = 0) -> dict:
    key = jax.random.key(seed)
    ks = iter(jax.random.split(key, 40))
    nrm = lambda shape, s: jax.random.normal(next(ks), shape, jnp.float32) * s
    L = DEPTH
    n_idx = jnp.arange(SSM_STATE, dtype=jnp.float32)
    inp = {
        "x_prompt": nrm((BATCH, SEQ, D_MODEL), 1.0),
        "x_sample": nrm((DEC_BATCH, DEC_SEQ, D_MODEL), 1.0),
        "cache_conv": nrm((L, DEC_BATCH, CONV_K - 1, CONV_WIDTH), 0.5),
        "state_ssm_re": nrm((L, DEC_BATCH, SSM_GROUPS, SSM_STATE), 0.1),
        "state_ssm_im": nrm((L, DEC_BATCH, SSM_GROUPS, SSM_STATE), 0.1),
        "p_prompt": nrm((L, BATCH, SEQ, PLE_DIM), 1.0),
        "p_sample": nrm((L, DEC_BATCH, DEC_SEQ, PLE_DIM), 1.0),
        "g_pre": 1.0 + nrm((L, D_MODEL), 0.02),
        "w_in": nrm((L, D_MODEL, IN_COLS), D_MODEL ** -0.5),
        "conv_w": nrm((L, CONV_K, CONV_WIDTH), CONV_K ** -0.5),
        "conv_b": nrm((L, CONV_WIDTH), 0.02),
        "ln_g": 1.0 + nrm((L, CONV_WIDTH), 0.02),
        "ln_b": nrm((L, CONV_WIDTH), 0.02),
        "ssm_a_re": -0.5 + nrm((L, SSM_GROUPS, SSM_STATE), 0.01),
        "ssm_a_im": math.pi * n_idx + nrm((L, SSM_GROUPS, SSM_STATE), 0.01),
        "ssm_log_dt": jax.random.uniform(next(ks), (L, SSM_GROUPS), jnp.float32,
                                         math.log(DT_MIN), math.log(DT_MAX)),
        "ssm_b_re": nrm((L, SSM_GROUPS, SSM_STATE, SSM_GROUP), (2.0 * SSM_GROUP) ** -0.5),
        "ssm_b_im": nrm((L, SSM_GROUPS, SSM_STATE, SSM_GROUP), (2.0 * SSM_GROUP) ** -0.5),
        "ssm_c_re": nrm((L, SSM_GROUPS, SSM_GROUP, SSM_STATE), (2.0 * SSM_STATE) ** -0.5),
        "ssm_c_im": nrm((L, SSM_GROUPS, SSM_GROUP, SSM_STATE), (2.0 * SSM_STATE) ** -0.5),
        "ssm_d": 1.0 + nrm((L, SSM_WIDTH), 0.1),
        "w_glu": nrm((L, SSM_WIDTH, SSM_WIDTH), SSM_WIDTH ** -0.5),
        "b_glu": nrm((L, SSM_WIDTH), 0.02),
        "w_pc": nrm((L, CONV_WIDTH, D_MODEL), CONV_WIDTH ** -0.5),
        "w_ps": nrm((L, SSM_WIDTH, D_MODEL), SSM_WIDTH ** -0.5),
        "w_out": nrm((L, D_MODEL, D_MODEL), D_MODEL ** -0.5),
        "g_post": 1.0 + nrm((L, D_MODEL), 0.02),
        "w_ple": nrm((L, PLE_DIM, D_MODEL), PLE_DIM ** -0.5),
        "g_ple": 1.0 + nrm((L, D_MODEL), 0.02),
        "w_pg": nrm((L, D_MODEL, D_MODEL), D_MODEL ** -0.5),
        "b_pg": nrm((L, D_MODEL), 0.02),
    }
    return inp


def reference(x_prompt, x_sample, cache_conv, state_ssm_re, state_ssm_im, p_prompt, p_sample,
              g_pre, w_in, conv_w, conv_b, ln_g, ln_b, ssm_a_re, ssm_a_im, ssm_log_dt,
              ssm_b_re, ssm_b_im, ssm_c_re, ssm_c_im, ssm_d, w_glu, b_glu, w_pc, w_ps, w_out,
              g_post, w_ple, g_ple, w_pg, b_pg):
    W = (g_pre, w_in, conv_w, conv_b, ln_g, ln_b, ssm_a_re, ssm_a_im, ssm_log_dt, ssm_b_re, ssm_b_im,
         ssm_c_re, ssm_c_im, ssm_d, w_glu, b_glu, w_pc, w_ps, w_out, g_post, w_ple, g_ple, w_pg, b_pg)
    bp = x_prompt.shape[0]
    zero_buf = jnp.zeros((DEPTH, bp, CONV_K - 1, CONV_WIDTH), x_prompt.dtype)
    zero_s = jnp.zeros((DEPTH, bp, SSM_GROUPS, SSM_STATE), jnp.float32)
    y_prompt, conv_prompt, ssm_re_prompt, ssm_im_prompt = trunk(
        x_prompt, p_prompt, zero_buf, zero_s, zero_s, W)
    y_sample, conv_sample, ssm_re_sample, ssm_im_sample = trunk(
        x_sample, p_sample, cache_conv, state_ssm_re, state_ssm_im, W)
    return (y_prompt, y_sample, conv_prompt, conv_sample,
            ssm_re_prompt, ssm_im_prompt, ssm_re_sample, ssm_im_sample)
```

```python
import math
from contextlib import ExitStack
import numpy as np
import concourse.bass as bass
import concourse.mybir as mybir
from concourse.bass_utils import run_bass_kernel_spmd

F32 = mybir.dt.float32
BF16 = mybir.dt.bfloat16
AF = mybir.ActivationFunctionType
ALU = mybir.AluOpType

D = 1024
NS = 16
CK = 31
EPS = 1e-6
PI = math.pi
TWO_PI = 2.0 * math.pi
MAGIC = 12582912.0
SHR = 1.0 - 2e-6


def V(t, p0, np_, f0, dims):
    F = 1
    for s in t.shape[1:]:
        F *= s
    return bass.AP(t, p0 * F + f0, [[F, np_]] + [list(d) for d in dims])


def DR(t, off, dims):
    return bass.AP(t, off, [list(d) for d in dims])


class Prog:
    ENGS = ["tensor", "vector", "scalar", "gpsimd", "sync"]
    NDS = 8

    def __init__(self, nc, es):
        self.nc = nc
        self.q = {e: [] for e in self.ENGS}
        self.cnt = {e: 0 for e in self.ENGS}
        self.sem = {e: es.enter_context(nc.semaphore("s_" + e)) for e in self.ENGS}
        self.dsem = {}
        self.dcnt = {}
        self.dnext = {}
        for qn in ["sync", "gpsimd", "scalar"]:
            self.dsem[qn] = [es.enter_context(nc.semaphore("d_%s%d" % (qn, i))) for i in range(self.NDS)]
            self.dcnt[qn] = [0] * self.NDS
            self.dnext[qn] = 0
        self.last_w = {}
        self.readers = {}
        self.waited = {e: {} for e in self.ENGS}
        self.all_events = []

    def _deps(self, eng, R, W):
        deps = {}

        def add(ev, war=False):
            if ev is None:
                return
            key, val, src = ev[0], ev[1], ev[2]
            if src == eng and eng == "tensor":
                return
            if deps.get(key, (0, None))[0] < val:
                deps[key] = (val, ev[3])
        for r in R:
            add(self.last_w.get(r))
        for w in W:
            add(self.last_w.get(w))
            for ev in self.readers.get(w, []):
                add(ev, war=True)
        out = []
        for key, (val, semh) in deps.items():
            if self.waited[eng].get(key, 0) >= val:
                continue
            self.waited[eng][key] = val
            out.append((semh, val))
        return out

    def _commit(self, ev, R, W):
        for w in W:
            self.last_w[w] = ev
            self.readers[w] = []
        for r in R:
            self.readers.setdefault(r, []).append(ev)

    def op(self, eng, fn, R=(), W=()):
        waits = self._deps(eng, R, W)
        self.cnt[eng] += 1
        ev = ("e_" + eng, self.cnt[eng], eng, self.sem[eng])
        self.q[eng].append((waits, fn, self.sem[eng], 1))
        self._commit(ev, R, W)
        return ev

    def dma(self, qn, fn, R=(), W=()):
        waits = self._deps(qn, R, W)
        j = self.dnext[qn]
        self.dnext[qn] = (j + 1) % self.NDS
        n = self.dcnt[qn][j]
        key = "d_%s%d" % (qn, j)
        semh = self.dsem[qn][j]
        if n > 0 and self.waited[qn].get(key, 0) < 16 * n:
            self.waited[qn][key] = 16 * n
            waits.append((semh, 16 * n))
        self.dcnt[qn][j] = n + 1
        ev = (key, 16 * (n + 1), "dma_" + qn, semh)
        self.q[qn].append((waits, fn, semh, 16))
        self._commit(ev, R, W)
        self.all_events.append(ev)
        return ev

    def barrier(self, final=False):
        evs = []
        for e in self.ENGS:
            if self.cnt[e] > 0:
                evs.append(("e_" + e, self.cnt[e], e, self.sem[e]))
        for qn in self.dsem:
            for j in range(self.NDS):
                if self.dcnt[qn][j] > 0:
                    evs.append(("d_%s%d" % (qn, j), 16 * self.dcnt[qn][j], "dma_" + qn, self.dsem[qn][j]))
        for e in self.ENGS:
            if e == "tensor" and not final:
                continue
            waits = []
            for (key, val, src, semh) in evs:
                if src == e:
                    continue
                if self.waited[e].get(key, 0) >= val:
                    continue
                self.waited[e][key] = val
                waits.append((semh, val))
            if waits:
                self.q[e].append((waits, None, None, 0))

    def replay(self, block):
        def mk(e):
            def body(eng):
                for (waits, fn, semh, inc) in self.q[e]:
                    for (s, v) in waits:
                        eng.wait_ge(s, v)
                    if fn is not None:
                        fn(eng).then_inc(semh, inc)
            return body
        block.tensor(mk("tensor"))
        block.vector(mk("vector"))
        block.scalar(mk("scalar"))
        block.gpsimd(mk("gpsimd"))
        block.sync(mk("sync"))


class _Stop(Exception):
    pass


def build_program(SEQ, stop=None):
    TT = SEQ + NS
    NTB = SEQ // 512
    tblocks = [(i * 512, 512) for i in range(NTB)] + [(SEQ, NS)]
    ttiles = [(i * 128, 128) for i in range(SEQ // 128)] + [(SEQ, NS)]

    nc = bass.Bass("TRN2", target_bir_lowering=False)
    es = ExitStack()

    def din(name, shape):
        return nc.dram_tensor(name, list(shape), F32, kind="ExternalInput")

    def dout(name, shape):
        return nc.dram_tensor(name, list(shape), F32, kind="ExternalOutput")

    xp = din("xp", [SEQ, D]); xs = din("xs", [NS, D])
    pp = din("pp", [SEQ, 256]); psm = din("psm", [NS, 256])
    cache = din("cache", [NS * 30, D])
    st_re = din("st_re", [NS, 2048]); st_im = din("st_im", [NS, 2048])
    w_in = din("w_in", [D, 6144]); w_pc = din("w_pc", [D, D]); w_ps = din("w_ps", [512, D])
    w_glu = din("w_glu", [512, 512]); w_out = din("w_out", [D, D]); w_pg = din("w_pg", [D, D])
    w_ple = din("w_ple", [256, D])
    rows = din("rows", [4, D])
    colv8 = din("colv8", [128, 3, 8])
    colv4 = din("colv4", [128, 2, 4])
    convw = din("convw", [128, 8, CK])
    PA = din("PA", [128, 3, 256])
    BA = din("BA", [128, 2, 256])
    MA = din("MA", [128, 8])
    PB = din("PB", [128, 3, 16])
    CB = din("CB", [128, 2, 256])
    BB = din("BB", [128, 2, 256])
    MB = din("MB", [128, 4, 8])
    ident_d = din("ident", [128, 128])
    iota_d = din("iota", [128, SEQ])

    y_p = dout("y_p", [SEQ, D]); y_s = dout("y_s", [NS, D])
    conv_p = dout("conv_p", [30, D]); conv_s = dout("conv_s", [NS * 30, D])
    sre_p = dout("sre_p", [32, 64]); sim_p = dout("sim_p", [32, 64])
    sre_s = dout("sre_s", [NS, 2048]); sim_s = dout("sim_s", [NS, 2048])

    P = Prog(nc, es)
    stacks = []

    def ck(n):
        if stop is not None and n == stop:
            raise _Stop()

    def sb(name, free, dt=F32, stack=es):
        return stack.enter_context(nc.sbuf_tensor(name, [128] + list(free), dt))

    PS = [es.enter_context(nc.psum_tensor("ps%d" % i, [128, 512], F32)) for i in range(8)]
    psi = [0]

    def nextps():
        i = psi[0]
        psi[0] = (i + 1) % 8
        return i

    ident_b = sb("ident_b", [128], BF16)
    ident_f = sb("ident_f", [128], F32)
    ones_b = sb("ones_b", [128], BF16)
    cv8 = sb("cv8", [3, 8], F32)
    cv4 = sb("cv4", [2, 4], F32)
    m_t = sb("m_t", [8, TT], BF16)
    stW = ExitStack(); stacks.append(stW)
    NWB = 4
    WB = [sb("wb%d" % i, [8, 512], BF16, stW) for i in range(NWB)]

    def pipeline(stages, n, between=None):
        for t in range(n + len(stages) - 1):
            if between is not None:
                between()
            for si, st_fn in enumerate(stages):
                i = t - si
                if 0 <= i < n:
                    st_fn(i)
    wbi = [0]

    P.dma("sync", lambda e: e.dma_start(out=ident_f[:], in_=ident_d.ap()), W=["ident_f"])
    P.dma("gpsimd", lambda e: e.dma_start(out=ident_b[:], in_=ident_d.ap()), W=["ident_b"])
    P.dma("sync", lambda e: e.dma_start(out=cv8[:], in_=colv8.ap()), W=["cv8"])
    P.dma("sync", lambda e: e.dma_start(out=cv4[:], in_=colv4.ap()), W=["cv4"])
    P.op("gpsimd", lambda e: e.memset(ones_b[:], 1.0), W=["ones_b"])

    def load_w(wd, ncols_total, kc, c0, ncols):
        i = wbi[0]
        wbi[0] = (i + 1) % NWB
        buf = WB[i]
        P.dma("gpsimd", lambda e: e.dma_start(
            out=V(buf, 0, 128, 0, [[512, kc], [1, ncols]]),
            in_=DR(wd, c0, [[ncols_total, 128], [128 * ncols_total, kc], [1, ncols]])),
            W=[("wb", i)])
        return i

    pending = {}

    def prefetch(key, *args):
        pending[key] = load_w(*args)

    def getw(key, *args):
        if key in pending:
            return pending.pop(key)
        return load_w(*args)

    def mm_fm(wi, kc, col_in_buf, act, act_key, cs, cn, pbank):
        buf = WB[wi]
        actF = act.shape[2]
        for k in range(kc):
            P.op("tensor", lambda e, k=k: e.matmul(
                V(PS[pbank], 0, 128, 0, [[1, cn]]),
                V(buf, 0, 128, k * 512 + col_in_buf, [[1, 128]]),
                V(act, 0, 128, k * actF + cs, [[1, cn]]),
                start=(k == 0), stop=(k == kc - 1)),
                R=[("wb", wi), act_key], W=[("ps", pbank)])

    try:
        stH = ExitStack(); stacks.append(stH)
        hT = sb("hT", [8, TT], BF16, stH)
        stC = ExitStack(); stacks.append(stC)
        sx = sb("sx", [4, TT], BF16, stC)
        wi = load_w(w_in, 6144, 8, 3072, 512)
        TC = SEQ // 8
        stS = ExitStack(); stacks.append(stS)
        PBt = sb("PBt", [3, 16], F32, stS)
        tB = [sb("tB%d" % i, [16], F32, stS) for i in range(13)]
        LpB = sb("LpB", [2, 9, 16], F32, stS)
        Gc = sb("Gc", [8, 2, 256], F32, stS)
        Hc = sb("Hc", [2, 9, 256], BF16, stS)
        BbB = sb("BbB", [2, 256], BF16, stS)
        mA = sb("mA", [8], F32, stS)
        mB = sb("mB", [4, 8], BF16, stS)
        fin = sb("fin", [2, 16], F32, stS)
        sS = sb("sS", [2, 256], F32, stS)
        sN = sb("sN", [2, 256], F32, stS)
        sNb = sb("sNb", [2, 256], BF16, stS)
        sT1 = sb("sT1", [256], F32, stS)
        sT2 = sb("sT2", [256], F32, stS)
        P.dma("sync", lambda e: e.dma_start(out=PBt[:], in_=PB.ap()), W=["PBt"])
        P.dma("sync", lambda e: e.dma_start(out=mA[:], in_=MA.ap()), W=["mA"])
        P.dma("gpsimd", lambda e: e.dma_start(out=mB[:], in_=MB.ap()), W=["mB"])
        stP = ExitStack(); stacks.append(stP)
        PAt = sb("PAt", [3, 256], F32, stP)
        BAt = sb("BAt", [2, 256], F32, stP)
        CBt = sb("CBt", [2, 256], F32, stP)
        BBt = sb("BBt", [2, 256], F32, stP)
        tA = [sb("tA%d" % i, [256], F32, stP) for i in range(9)]
        big0 = sb("big0", [9 * 256], F32, stP)
        stage = big0
        big1 = sb("big1", [9 * 256], F32, stP)
        P.dma("sync", lambda e: e.dma_start(out=PAt[:], in_=PA.ap()), W=["PAt"])
        P.dma("sync", lambda e: e.dma_start(out=BAt[:], in_=BA.ap()), W=["BAt"])
        P.dma("sync", lambda e: e.dma_start(out=CBt[:], in_=CB.ap()), W=["CBt"])
        P.dma("sync", lambda e: e.dma_start(out=BBt[:], in_=BB.ap()), W=["BBt"])

        Gkeys = rho8 = tau8 = None

        def prep_gen():
            nonlocal Gkeys, rho8, tau8
            def lam_prep(par, n, T, pk, pre):
                a_re = V(par, 0, 128, 0, [[1, n]]); a_im = V(par, 0, 128, n, [[1, n]]); ldt = V(par, 0, 128, 2 * n, [[1, n]])
                dt_, mag, th, cs_, sn_, tmp = T[0], T[1], T[2], T[3], T[4], T[5]
                P.op("scalar", lambda e: e.activation(out=dt_[:], in_=ldt, func=AF.Exp), R=[pk], W=[pre + "dt"])
                P.op("vector", lambda e: e.tensor_tensor(mag[:], a_re, dt_[:], ALU.mult), R=[pk, pre + "dt"], W=[pre + "mag"])
                P.op("scalar", lambda e: e.activation(out=mag[:], in_=mag[:], func=AF.Exp), R=[pre + "mag"], W=[pre + "mag"])
                P.op("vector", lambda e: e.scalar_tensor_tensor(th[:], a_im, 1.0 / TWO_PI, dt_[:], ALU.mult, ALU.mult), R=[pk, pre + "dt"], W=[pre + "th"])
                P.op("vector", lambda e: e.tensor_scalar(tmp[:], th[:], MAGIC, -MAGIC, ALU.add, ALU.add), R=[pre + "th"], W=[pre + "tmp"])
                P.op("vector", lambda e: e.tensor_tensor(th[:], th[:], tmp[:], ALU.subtract), R=[pre + "th", pre + "tmp"], W=[pre + "th"])
                P.op("scalar", lambda e: e.activation(out=sn_[:], in_=th[:], func=AF.Sin, scale=TWO_PI * SHR), R=[pre + "th"], W=[pre + "sin"])
                P.op("scalar", lambda e: e.activation(out=tmp[:], in_=th[:], func=AF.Sin, scale=PI * SHR), R=[pre + "th"], W=[pre + "tmp"])
                P.op("scalar", lambda e: e.activation(out=tmp[:], in_=tmp[:], func=AF.Square, scale=math.sqrt(2.0)), R=[pre + "tmp"], W=[pre + "tmp"])
                P.op("scalar", lambda e: e.activation(out=cs_[:], in_=tmp[:], func=AF.Identity, scale=-1.0, bias=1.0), R=[pre + "tmp"], W=[pre + "cos"])
                return mag, th, cs_, sn_

            def f_prep(par, n, T, mag, cs_, sn_, pk, pre):
                a_re = V(par, 0, 128, 0, [[1, n]]); a_im = V(par, 0, 128, n, [[1, n]])
                lr, li, den, t7, nr = T[0], T[5], T[6], T[7], T[8]
                P.op("vector", lambda e: e.tensor_tensor(lr[:], mag[:], cs_[:], ALU.mult), R=[pre + "mag", pre + "cos", pre + "dt"], W=[pre + "dt"])
                P.op("vector", lambda e: e.tensor_tensor(li[:], mag[:], sn_[:], ALU.mult), R=[pre + "mag", pre + "sin"], W=[pre + "tmp"])
                P.op("vector", lambda e: e.tensor_scalar(nr[:], lr[:], -1.0, None, ALU.add), R=[pre + "dt"], W=[pre + "nr"])
                P.op("vector", lambda e: e.tensor_tensor(den[:], a_re, a_re, ALU.mult), R=[pk], W=[pre + "den"])
                P.op("vector", lambda e: e.tensor_tensor(t7[:], a_im, a_im, ALU.mult), R=[pk], W=[pre + "t7"])
                P.op("vector", lambda e: e.tensor_tensor(den[:], den[:], t7[:], ALU.add), R=[pre + "den", pre + "t7"], W=[pre + "den"])
                P.op("vector", lambda e: e.reciprocal(den[:], den[:]), R=[pre + "den"], W=[pre + "den"])
                fr, fi = T[3], T[4]
                P.op("vector", lambda e: e.tensor_tensor(fr[:], nr[:], a_re, ALU.mult), R=[pre + "nr", pk], W=[pre + "cos"])
                P.op("vector", lambda e: e.tensor_tensor(t7[:], li[:], a_im, ALU.mult), R=[pre + "tmp", pk, pre + "den"], W=[pre + "t7"])
                P.op("vector", lambda e: e.tensor_tensor(fr[:], fr[:], t7[:], ALU.add), R=[pre + "cos", pre + "t7"], W=[pre + "cos"])
                P.op("vector", lambda e: e.tensor_tensor(fr[:], fr[:], den[:], ALU.mult), R=[pre + "cos", pre + "den"], W=[pre + "cos"])
                P.op("vector", lambda e: e.tensor_tensor(fi[:], li[:], a_re, ALU.mult), R=[pre + "tmp", pk], W=[pre + "sin"])
                P.op("vector", lambda e: e.tensor_tensor(t7[:], nr[:], a_im, ALU.mult), R=[pre + "nr", pk, pre + "cos"], W=[pre + "t7"])
                P.op("vector", lambda e: e.tensor_tensor(fi[:], fi[:], t7[:], ALU.subtract), R=[pre + "sin", pre + "t7"], W=[pre + "sin"])
                P.op("vector", lambda e: e.tensor_tensor(fi[:], fi[:], den[:], ALU.mult), R=[pre + "sin", pre + "den"], W=[pre + "sin"])
                return lr, li, fr, fi

            def cmul(o_re, o_im, a_re, a_im, b_re, b_im, t0, t1, R, Wre, Wim, neg_im=False):
                P.op("vector", lambda e: e.tensor_tensor(t0, a_re, b_re, ALU.mult), R=R, W=["cm_t0"])
                P.op("vector", lambda e: e.tensor_tensor(t1, a_im, b_im, ALU.mult), R=R, W=["cm_t1"])
                P.op("vector", lambda e: e.tensor_tensor(o_re, t0, t1, ALU.subtract), R=["cm_t0", "cm_t1"], W=Wre)
                P.op("vector", lambda e: e.tensor_tensor(t0, a_re, b_im, ALU.mult), R=R + Wre, W=["cm_t0"])
                P.op("vector", lambda e: e.tensor_tensor(t1, a_im, b_re, ALU.mult), R=R + Wre, W=["cm_t1"])
                if neg_im:
                    P.op("vector", lambda e: e.scalar_tensor_tensor(o_im, t0, -1.0, t1, ALU.mult, ALU.subtract), R=["cm_t0", "cm_t1"], W=Wim)
                else:
                    P.op("vector", lambda e: e.tensor_tensor(o_im, t0, t1, ALU.add), R=["cm_t0", "cm_t1"], W=Wim)

            magA, tauA, cosA, sinA = lam_prep(PAt, 256, tA, "PAt", "A_")
            yield
            lrA, liA, frA, fiA = f_prep(PAt, 256, tA, magA, cosA, sinA, "PAt", "A_")
            yield

            def gk(k, comp):
                return V(Gc, 0, 128, (k * 2 + comp) * 256, [[1, 256]])
            b0 = V(big0, 0, 128, 0, [[1, 256]]); b1 = V(big1, 0, 128, 0, [[1, 256]])
            cmul(gk(0, 0), gk(0, 1), frA[:], fiA[:], V(BAt, 0, 128, 0, [[1, 256]]), V(BAt, 0, 128, 256, [[1, 256]]), b0, b1,
                 ["A_cos", "A_sin", "BAt"], [("Gc", 0, 0)], [("Gc", 0, 1)])
            for k in range(1, 8):
                cmul(gk(k, 0), gk(k, 1), gk(k - 1, 0), gk(k - 1, 1), lrA[:], liA[:], b0, b1,
                     [("Gc", k - 1, 0), ("Gc", k - 1, 1), "A_dt", "A_tmp"], [("Gc", k, 0)], [("Gc", k, 1)])
                yield
            Gkeys = [("Gc", k, c) for k in range(8) for c in range(2)]
            magB, tauB, cosB, sinB = lam_prep(PBt, 16, tB, "PBt", "B_")
            yield
            lrB, liB, frB, fiB = f_prep(PBt, 16, tB, magB, cosB, sinB, "PBt", "B_")
            yield

            def lp(k, comp):
                return V(LpB, 0, 128, (comp * 9 + k) * 16, [[1, 16]])
            P.op("vector", lambda e: e.memset(lp(0, 0), 1.0), W=[("LpB", 0)])
            P.op("vector", lambda e: e.memset(lp(0, 1), 0.0), R=[("LpB", 0)], W=[("LpB", 0)])
            P.op("vector", lambda e: e.tensor_copy(lp(1, 0), lrB[:]), R=["B_dt"], W=[("LpB", 1)])
            P.op("vector", lambda e: e.tensor_copy(lp(1, 1), liB[:]), R=["B_tmp", ("LpB", 1)], W=[("LpB", 1)])
            tb0 = tB[9][:]; tb1 = tB[10][:]
            for k in range(2, 9):
                cmul(lp(k, 0), lp(k, 1), lp(k - 1, 0), lp(k - 1, 1), lrB[:], liB[:], tb0, tb1,
                     [("LpB", k - 1), "B_dt", "B_tmp"], [("LpB", k)], [("LpB", k)])
                yield
            LpKeys = [("LpB", k) for k in range(9)]
            rho8 = tB[11]; tau8 = tB[12]
            P.op("vector", lambda e: e.tensor_tensor(rho8[:], magB[:], magB[:], ALU.mult), R=["B_mag"], W=["rho8"])
            P.op("vector", lambda e: e.tensor_tensor(rho8[:], rho8[:], rho8[:], ALU.mult), R=["rho8"], W=["rho8"])
            P.op("vector", lambda e: e.tensor_tensor(rho8[:], rho8[:], rho8[:], ALU.mult), R=["rho8"], W=["rho8"])
            P.op("vector", lambda e: e.tensor_scalar(tau8[:], tauB[:], 8.0, None, ALU.mult), R=["B_th"], W=["tau8"])
            P.op("vector", lambda e: e.tensor_scalar(tb0, tau8[:], MAGIC, -MAGIC, ALU.add, ALU.add), R=["tau8"] + LpKeys, W=["cm_t0"])
            P.op("vector", lambda e: e.tensor_tensor(tau8[:], tau8[:], tb0, ALU.subtract), R=["tau8", "cm_t0"], W=["tau8"])
            def bc16(t):
                return V(t, 0, 128, 0, [[1, 16], [0, 16]])

            def q16(t, comp):
                return V(t, 0, 128, comp * 256, [[16, 16], [1, 16]])
            g0 = V(big0, 0, 128, 0, [[16, 16], [1, 16]]); g1 = V(big1, 0, 128, 0, [[16, 16], [1, 16]])
            cmul(q16(BbB, 0), q16(BbB, 1), bc16(frB), bc16(fiB), q16(BBt, 0), q16(BBt, 1), g0, g1,
                 ["B_cos", "B_sin", "BBt"], ["BbB0"], ["BbB1"])
            def lpb(comp):
                return V(LpB, 0, 128, comp * 144, [[16, 9], [1, 16], [0, 16]])

            def cbb(comp):
                return V(CBt, 0, 128, comp * 256, [[0, 9], [16, 16], [1, 16]])

            def hcv(comp):
                return V(Hc, 0, 128, comp * 2304, [[256, 9], [16, 16], [1, 16]])
            h0 = V(big0, 0, 128, 0, [[256, 9], [16, 16], [1, 16]]); h1 = V(big1, 0, 128, 0, [[256, 9], [16, 16], [1, 16]])
            cmul(hcv(0), hcv(1), cbb(0), cbb(1), lpb(0), lpb(1), h0, h1, ["CBt"] + LpKeys, ["Hc0"], ["Hc1"], neg_im=True)
            yield

            for comp, sd in ((0, st_re), (1, st_im)):
                P.dma("sync", lambda e, sd=sd: e.dma_start(out=V(stage, 0, NS, 0, [[1, 2048]]), in_=sd.ap()), W=["cm_t0"])
                pb = nextps()
                for qg in range(16):
                    P.op("tensor", lambda e, qg=qg, pb=pb: e.matmul(
                        V(PS[pb], 0, 128, qg * NS, [[1, NS]]),
                        V(stage, 0, NS, qg * 128, [[1, 128]]),
                        V(ident_f, 0, NS, 0, [[1, NS]]), start=True, stop=True),
                        R=["cm_t0", "ident_f"], W=[("ps", pb)])
                P.op("vector", lambda e, comp=comp, pb=pb: e.tensor_copy(
                    V(sS, 0, 128, comp * 256, [[1, 256]]), V(PS[pb], 0, 128, 0, [[1, 256]])),
                    R=[("ps", pb)], W=["sS%d" % comp])

            def bcB(t):
                return V(t, 0, 128, 0, [[1, 16], [0, NS]])

            def s3(t, comp):
                return V(t, 0, 128, comp * 256, [[NS, 16], [1, NS]])

            def s3t(t):
                return V(t, 0, 128, 0, [[NS, 16], [1, NS]])
            cmul(s3(sN, 0), s3(sN, 1), s3(sS, 0), s3(sS, 1), bcB(lrB), bcB(liB), s3t(sT1), s3t(sT2),
                 ["sS0", "sS1", "B_dt", "B_tmp"], ["sN0"], ["sN1"])

        pgen = prep_gen()
        stA = ExitStack(); stacks.append(stA)
        xt = [sb("xt%d" % i, [D], F32, stA) for i in range(3)]
        hb = [sb("hb%d" % i, [D], BF16, stA) for i in range(2)]
        junk = sb("junk", [D], BF16, stA)
        ssA = sb("ssA", [40], F32, stA)
        gpre = sb("gpre", [D], F32, stA)
        P.dma("sync", lambda e: e.dma_start(out=gpre[:], in_=DR(rows, 0, [[0, 128], [1, D]])), W=["gpre"])

        def sA1(ti):
            r0, nt = ttiles[ti]; b = ti % 3
            src = DR(xp, r0 * D, [[D, nt], [1, D]]) if r0 < SEQ else DR(xs, 0, [[D, nt], [1, D]])
            P.dma("sync", lambda e: e.dma_start(out=V(xt[b], 0, nt, 0, [[1, D]]), in_=src), W=[("xt", b)])
            P.op("scalar", lambda e: e.activation(
                out=V(junk, 0, nt, 0, [[1, D]]), in_=V(xt[b], 0, nt, 0, [[1, D]]), func=AF.Square,
                accum_out=V(ssA, 0, nt, ti, [[1, 1]])), R=[("xt", b)], W=["junk", ("ssA", ti)])
            P.op("scalar", lambda e: e.activation(
                out=V(ssA, 0, nt, ti, [[1, 1]]), in_=V(ssA, 0, nt, ti, [[1, 1]]), func=AF.Ln, scale=1.0 / D, bias=EPS),
                R=[("ssA", ti)], W=[("ssA", ti)])

        def sA2(ti):
            r0, nt = ttiles[ti]; b = ti % 3; bh = ti % 2
            P.op("scalar", lambda e: e.activation(
                out=V(ssA, 0, nt, ti, [[1, 1]]), in_=V(ssA, 0, nt, ti, [[1, 1]]), func=AF.Exp, scale=-0.5),
                R=[("ssA", ti)], W=[("ssA", ti)])
            P.op("vector", lambda e: e.scalar_tensor_tensor(
                V(hb[bh], 0, nt, 0, [[1, D]]), V(xt[b], 0, nt, 0, [[1, D]]), V(ssA, 0, nt, ti, [[1, 1]]),
                V(gpre, 0, nt, 0, [[1, D]]), ALU.mult, ALU.mult),
                R=[("xt", b), ("ssA", ti), "gpre"], W=[("hb", bh)])

        def sA3(ti):
            r0, nt = ttiles[ti]; b = ti % 2
            for half in range(2):
                pb = nextps()
                for kk in range(4):
                    k = half * 4 + kk
                    P.op("tensor", lambda e, k=k, kk=kk, pb=pb: e.matmul(
                        V(PS[pb], 0, 128, kk * nt, [[1, nt]]),
                        V(hb[b], 0, nt, k * 128, [[1, 128]]),
                        V(ident_b, 0, nt, 0, [[1, nt]]), start=True, stop=True),
                        R=[("hb", b), "ident_b"], W=[("ps", pb)])
                P.op("scalar", lambda e, half=half, pb=pb: e.activation(
                    out=V(hT, 0, 128, half * 4 * TT + r0, [[TT, 4], [1, nt]]),
                    in_=V(PS[pb], 0, 128, 0, [[nt, 4], [1, nt]]), func=AF.Copy),
                    R=[("ps", pb)], W=[("hT", half, ti)])
        pipeline([sA1, sA2, sA3], len(ttiles), between=lambda: next(pgen, None))
        for _ in pgen:
            pass
        ck(1)

        for (cs, cn) in tblocks:
            tis = [ti for ti, (r0, nt) in enumerate(ttiles) if cs <= r0 < cs + cn]
            hkeys = [("hT", h, ti) for h in range(2) for ti in tis]
            for oc in range(4):
                pb = nextps()
                for k in range(8):
                    mov = V(hT, 0, 128, k * TT + cs, [[1, cn]])
                    P.op("tensor", lambda e, k=k, oc=oc, cn=cn, pb=pb, mov=mov: e.matmul(
                        V(PS[pb], 0, 128, 0, [[1, cn]]),
                        V(WB[wi], 0, 128, k * 512 + oc * 128, [[1, 128]]),
                        mov, start=(k == 0), stop=(k == 7)),
                        R=[("wb", wi)] + hkeys, W=[("ps", pb)])
                dst = (V(sx, 0, 128, oc * TT + cs // 8, [[SEQ // 8, 8], [1, cn // 8]]) if cs < SEQ
                       else V(sx, 0, 128, oc * TT + cs, [[1, cn]]))
                src_ = (V(PS[pb], 0, 128, 0, [[1, 8], [8, cn // 8]]) if cs < SEQ else V(PS[pb], 0, 128, 0, [[1, cn]]))
                P.op("scalar", lambda e, dst=dst, src_=src_: e.activation(out=dst, in_=src_, func=AF.Copy),
                     R=[("ps", pb)], W=[("sx", oc) if cs < SEQ else ("sxs", oc)])
        ck(2)
        ck(3)
        P.barrier()
        stA.close()
        stP.close()

        stL = ExitStack(); stacks.append(stL)
        SLi = sb("SLi", [8, 2, 512], BF16, stL)
        YSk = sb("YSk", [4, 2, 9, 128], BF16, stL)
        BBs = sb("BBs", [4, 2, 128], BF16, stL)
        BDk = sb("BDk", [8, 128], BF16, stL)
        Spv = sb("Spv", [2, 4, 2, TC], BF16, stL)
        iot = sb("iot", [TC], F32, stL)
        tcs = [sb("tcos%d" % i, [TC], F32, stL) for i in range(2)]
        tsn = [sb("tsin%d" % i, [TC], F32, stL) for i in range(2)]
        Tg = [sb("Tg%d" % i, [TC], F32, stL) for i in range(2)]
        Wk = [sb("Wk%d" % i, [TC], F32, stL) for i in range(5)]
        Pk = [Wk[0], Wk[1]]
        ytmp = sb("ytmp", [NS], F32, stL)
        P.dma("sync", lambda e: e.dma_start(out=iot[:], in_=DR(iota_d, 0, [[SEQ, 128], [1, TC]])), W=["iot"])
        P.op("gpsimd", lambda e: e.memset(V(Spv, 0, 128, 0, [[TC, 16], [1, 1]]), 0.0), W=["Spv_z"])
        P.op("gpsimd", lambda e: e.memset(YSk[:], 0.0), W=["YSk_z"])
        P.op("gpsimd", lambda e: e.memset(BBs[:], 0.0), W=["BBs_z"])
        ysg = sx
        YB = [0, 1, 2, 3]; SLB = [4, 5]; BUS = 6; YSB = 7
        slit = [0]
        def gen_tables(qg):
            sl = qg % 2
            t8col = V(tau8, 0, 128, qg, [[1, 1]])
            P.op("vector", lambda e: e.tensor_scalar(Tg[1][:], iot[:], t8col, None, ALU.mult), R=["iot", "tau8"], W=["tg1"])
            P.op("vector", lambda e: e.tensor_scalar(Tg[0][:], Tg[1][:], MAGIC, -MAGIC, ALU.add, ALU.add), R=["tg1"], W=["tg0"])
            P.op("vector", lambda e: e.tensor_tensor(Tg[1][:], Tg[1][:], Tg[0][:], ALU.subtract), R=["tg1", "tg0"], W=["tg1"])
            P.op("scalar", lambda e: e.activation(out=tsn[sl][:], in_=Tg[1][:], func=AF.Sin, scale=TWO_PI * SHR), R=["tg1"], W=[("tsin", sl)])
            P.op("scalar", lambda e: e.activation(out=Tg[0][:], in_=Tg[1][:], func=AF.Sin, scale=PI * SHR), R=["tg1"], W=["tg0"])
            P.op("scalar", lambda e: e.activation(out=Tg[0][:], in_=Tg[0][:], func=AF.Square, scale=math.sqrt(2.0)), R=["tg0"], W=["tg0"])
            P.op("scalar", lambda e: e.activation(out=tcs[sl][:], in_=Tg[0][:], func=AF.Identity, scale=-1.0, bias=1.0), R=["tg0"], W=[("tcos", sl)])

        def expand_sli(blk):
            for g2 in range(8):
                P.op("scalar", lambda e, g2=g2: e.activation(
                    out=V(SLi, 0, 128, g2 * 64, [[512, 16], [1, 64]]),
                    in_=V(Gc, 0, 128, blk * 64, [[256, 16], [1, 64]]), func=AF.Identity,
                    scale=V(mA, 0, 128, g2, [[1, 1]])),
                    R=Gkeys + ["mA"], W=[("SLi", k) for k in range(8)])

        slb_of = {}

        def s_local(blk, q):
            pbs = SLB[slit[0] % 2]; slit[0] += 1
            slb_of[(blk, q)] = pbs
            for comp in range(2):
                for i in range(8):
                    P.op("tensor", lambda e, comp=comp, i=i: e.matmul(
                        V(PS[pbs], 0, 128, comp * TC, [[1, TC]]),
                        V(SLi, 0, 128, ((7 - i) * 2 + comp) * 512 + q * 128, [[1, 128]]),
                        V(sx, 0, 128, blk * TT + i * TC, [[1, TC]]), start=(i == 0), stop=(i == 7)),
                        R=[("SLi", 7 - i), ("sx", blk)], W=[("ps", pbs)])

        def sample_bu(blk):
            for q in range(4):
                for comp in range(2):
                    P.op("tensor", lambda e, comp=comp, q=q: e.matmul(
                        V(PS[BUS], 0, 128, (blk % 2) * 128 + (q * 2 + comp) * NS, [[1, NS]]),
                        V(SLi, 0, 128, (0 * 2 + comp) * 512 + q * 128, [[1, 128]]),
                        V(sx, 0, 128, blk * TT + SEQ, [[1, NS]]), start=True, stop=True),
                        R=[("SLi", 0), ("sxs", blk)], W=[("ps", BUS)])

        expand_sli(0)
        sample_bu(0)
        s_local(0, 0)
        for blk in range(4):
            for q in range(4):
                qg = blk * 4 + q
                for comp in range(2):
                    for par in range(2):
                        P.op("gpsimd", lambda e, q=q, qg=qg, comp=comp, par=par: e.tensor_copy(
                            V(YSk, 64 * par, 64, (q * 2 + comp) * 9 * 128 + (2 * q + par) * 16, [[128, 9], [1, 16]]),
                            V(Hc, 64 * par, 64, comp * 2304 + qg * 16, [[256, 9], [1, 16]])),
                            R=["Hc%d" % comp, "YSk_z"], W=["YSk"])
                        P.op("gpsimd", lambda e, q=q, qg=qg, comp=comp, par=par: e.tensor_copy(
                            V(BBs, 64 * par, 64, (q * 2 + comp) * 128 + (2 * q + par) * 16, [[1, 16]]),
                            V(BbB, 64 * par, 64, comp * 256 + qg * 16, [[1, 16]])),
                            R=["BbB%d" % comp, "BBs_z"], W=["BBs"])
            ck(4 if blk == 0 else -1)
            dcol = V(cv4, 0, 128, 4 + blk, [[1, 1]])
            ck(5 if blk == 0 else -1)
            for q in range(4):
                qg = blk * 4 + q
                pbs = slb_of[(blk, q)]
                if q < 3:
                    s_local(blk, q + 1)
                elif blk < 3:
                    expand_sli(blk + 1)
                    sample_bu(blk + 1)
                    s_local(blk + 1, 0)
                if qg == 0:
                    gen_tables(0)
                if qg + 1 < 16:
                    gen_tables(qg + 1)
                sl = qg % 2
                tcos = tcs[sl]; tsin = tsn[sl]
                kc_ = ("tcos", sl); ks_ = ("tsin", sl)
                br = V(PS[pbs], 0, 128, 0, [[1, TC]]); bi = V(PS[pbs], 0, 128, TC, [[1, TC]])
                kp = ("ps", pbs)
                P.op("vector", lambda e, br=br, tcos=tcos, tsin=tsin: e.tensor_tensor(Wk[0][:], tcos[:], br, ALU.mult), R=[kc_, kp], W=["w0"])
                P.op("vector", lambda e, bi=bi, tcos=tcos, tsin=tsin: e.tensor_tensor(Wk[1][:], tsin[:], bi, ALU.mult), R=[ks_, kp], W=["w1"])
                P.op("vector", lambda e: e.tensor_tensor(Wk[0][:], Wk[0][:], Wk[1][:], ALU.add), R=["w0", "w1"], W=["w0"])
                P.op("vector", lambda e, bi=bi, tcos=tcos, tsin=tsin: e.tensor_tensor(Wk[2][:], tcos[:], bi, ALU.mult), R=[kc_, kp], W=["w2"])
                P.op("vector", lambda e, br=br, tcos=tcos, tsin=tsin: e.tensor_tensor(Wk[1][:], tsin[:], br, ALU.mult), R=[ks_, kp], W=["w1"])
                P.op("vector", lambda e: e.tensor_tensor(Wk[2][:], Wk[2][:], Wk[1][:], ALU.subtract), R=["w2", "w1"], W=["w2"])
                rbc = V(rho8, 0, 128, qg, [[0, TC]])
                P.op("vector", lambda e, rbc=rbc: e.tensor_tensor_scan(Wk[3][:], rbc, Wk[0][:], 0.0, ALU.mult, ALU.add), R=["w0", "rho8"], W=["wk3"])
                P.op("vector", lambda e, rbc=rbc: e.tensor_tensor_scan(Wk[4][:], rbc, Wk[2][:], 0.0, ALU.mult, ALU.add), R=["w2", "rho8"], W=["wk4"])
                P.op("vector", lambda e, tcos=tcos: e.tensor_tensor(Wk[0][:], tcos[:], Wk[3][:], ALU.mult), R=[kc_, "wk3"], W=["w0"])
                P.op("vector", lambda e, tsin=tsin: e.tensor_tensor(Wk[1][:], tsin[:], Wk[4][:], ALU.mult), R=[ks_, "wk4"], W=["w1"])
                P.op("vector", lambda e, q=q, blk=blk: e.tensor_tensor(V(Spv, 0, 128, ((blk % 2) * 8 + q * 2 + 0) * TC + 1, [[1, TC - 1]]),
                                                              V(Wk[0], 0, 128, 0, [[1, TC - 1]]), V(Wk[1], 0, 128, 0, [[1, TC - 1]]), ALU.subtract),
                     R=["w0", "w1", "Spv_z"], W=[("Spv", blk % 2, q, 0)])
                P.op("vector", lambda e, qg=qg: e.tensor_tensor(V(fin, 0, 128, qg, [[1, 1]]), V(Wk[0], 0, 128, TC - 1, [[1, 1]]),
                                                                V(Wk[1], 0, 128, TC - 1, [[1, 1]]), ALU.subtract),
                     R=["w0", "w1"], W=[("fin", 0, qg)])
                P.op("vector", lambda e, tsin=tsin: e.tensor_tensor(Pk[0][:], tsin[:], Wk[3][:], ALU.mult), R=[ks_, "wk3"], W=["w0"])
                P.op("vector", lambda e, tcos=tcos: e.tensor_tensor(Pk[1][:], tcos[:], Wk[4][:], ALU.mult), R=[kc_, "wk4"], W=["w1"])
                P.op("vector", lambda e, q=q, blk=blk: e.tensor_tensor(V(Spv, 0, 128, ((blk % 2) * 8 + q * 2 + 1) * TC + 1, [[1, TC - 1]]),
                                                              V(Pk[0], 0, 128, 0, [[1, TC - 1]]), V(Pk[1], 0, 128, 0, [[1, TC - 1]]), ALU.add),
                     R=["w0", "w1", "Spv_z"], W=[("Spv", blk % 2, q, 1)])
                P.op("vector", lambda e, qg=qg: e.tensor_tensor(V(fin, 0, 128, 16 + qg, [[1, 1]]), V(Pk[0], 0, 128, TC - 1, [[1, 1]]),
                                                                V(Pk[1], 0, 128, TC - 1, [[1, 1]]), ALU.add),
                     R=["w0", "w1"], W=[("fin", 1, qg)])
                ck(6 if (blk == 0 and q == 0) else -1)
            for comp in range(2):
                P.op("vector", lambda e, comp=comp, blk=blk: e.tensor_tensor(
                    V(sN, 0, 128, comp * 256 + blk * 4 * NS, [[NS, 4], [1, NS]]),
                    V(sN, 0, 128, comp * 256 + blk * 4 * NS, [[NS, 4], [1, NS]]),
                    V(PS[BUS], 0, 128, (blk % 2) * 128 + comp * NS, [[2 * NS, 4], [1, NS]]), ALU.add),
                    R=["sN%d" % comp, ("ps", BUS)], W=["sN%d" % comp])
                P.op("vector", lambda e, comp=comp, blk=blk: e.tensor_copy(
                    V(sNb, 0, 128, comp * 256 + blk * 4 * NS, [[1, 4 * NS]]), V(sN, 0, 128, comp * 256 + blk * 4 * NS, [[1, 4 * NS]])),
                    R=["sN%d" % comp], W=["sNb%d" % comp])
            for q in range(4):
                qg = blk * 4 + q
                for comp in range(2):
                    P.op("tensor", lambda e, comp=comp, qg=qg, q=q: e.matmul(
                        V(PS[YSB], 0, 128, 0, [[1, NS]]),
                        V(YSk, 0, 128, ((q * 2 + comp) * 9 + 0) * 128, [[1, 128]]),
                        V(sNb, 0, 128, comp * 256 + qg * NS, [[1, NS]]),
                        start=(q == 0 and comp == 0), stop=(q == 3 and comp == 1)),
                        R=["YSk", "sNb%d" % comp], W=[("ps", YSB)])
            for hb_ in range(2):
                ck(43 if (blk == 0 and hb_ == 1) else -1)
                pbk = YB[hb_]
                for k4 in range(4):
                    k = hb_ * 4 + k4
                    for q in range(4):
                        for comp in range(2):
                            P.op("tensor", lambda e, k=k, k4=k4, q=q, comp=comp, pbk=pbk: e.matmul(
                                V(PS[pbk], 0, 128, k4 * 128, [[1, 128]]),
                                V(BBs, 0, 128, (q * 2 + comp) * 128, [[1, 128]]),
                                V(YSk, 0, 128, ((q * 2 + comp) * 9 + k) * 128, [[1, 128]]),
                                start=(q == 0 and comp == 0), stop=(q == 3 and comp == 1)),
                                R=["BBs", "YSk"], W=[("ps", pbk)])
                ck(41 if (blk == 0 and hb_ == 0) else -1)
                if hb_ == 0:
                    P.op("vector", lambda e, pbk=pbk, dcol=dcol: e.scalar_tensor_tensor(
                        V(BDk, 0, 128, 0, [[1, 128]]), ident_f[:], dcol, V(PS[pbk], 0, 128, 0, [[1, 128]]), ALU.mult, ALU.add),
                        R=[("ps", pbk), "ident_f", "cv4"], W=["BDk0"])
                    ck(42 if blk == 0 else -1)
                    P.op("vector", lambda e, pbk=pbk: e.tensor_copy(
                        V(BDk, 0, 128, 128, [[1, 384]]), V(PS[pbk], 0, 128, 128, [[1, 384]])),
                        R=[("ps", pbk)], W=["BDk1"])
                else:
                    ck(44 if blk == 0 else -1)
                    P.op("vector", lambda e, pbk=pbk: e.tensor_copy(
                        V(BDk, 0, 128, 512, [[1, 512]]), V(PS[pbk], 0, 128, 0, [[1, 512]])),
                        R=[("ps", pbk)], W=["BDk2"])
            ck(7 if blk == 0 else -1)
            spk = [("Spv", blk % 2, q, comp) for q in range(4) for comp in range(2)]
            for j in range(8):
                yb = YB[j // 2]
                nmm = (j + 1) + 8
                cnt = 0
                for i in range(j + 1):
                    P.op("tensor", lambda e, i=i, j=j, yb=yb, blk=blk, cnt=cnt, nmm=nmm: e.matmul(
                        V(PS[yb], 0, 128, (j % 2) * TC, [[1, TC]]),
                        V(BDk, 0, 128, (j - i) * 128, [[1, 128]]),
                        V(sx, 0, 128, blk * TT + i * TC, [[1, TC]]), start=(cnt == 0), stop=(cnt == nmm - 1)),
                        R=["BDk0", "BDk1", "BDk2", ("sx", blk)], W=[("ps", yb)])
                    cnt += 1
                for q in range(4):
                    for comp in range(2):
                        P.op("tensor", lambda e, q=q, comp=comp, j=j, yb=yb, cnt=cnt, nmm=nmm, blk=blk: e.matmul(
                            V(PS[yb], 0, 128, (j % 2) * TC, [[1, TC]]),
                            V(YSk, 0, 128, ((q * 2 + comp) * 9 + j + 1) * 128, [[1, 128]]),
                            V(Spv, 0, 128, ((blk % 2) * 8 + q * 2 + comp) * TC, [[1, TC]]), start=(cnt == 0), stop=(cnt == nmm - 1)),
                            R=["YSk"] + spk, W=[("ps", yb)])
                        cnt += 1
            ck(8 if blk == 0 else -1)
            for b4 in range(4):
                yb = YB[b4]
                P.op("scalar", lambda e, b4=b4, yb=yb, blk=blk: e.activation(
                    out=V(ysg, 0, 128, blk * TT + 2 * b4, [[1, 2], [8, TC]]), in_=V(PS[yb], 0, 128, 0, [[TC, 2], [1, TC]]), func=AF.Gelu),
                    R=[("ps", yb)], W=[("sx", blk)])
            P.op("vector", lambda e, blk=blk, dcol=dcol: e.scalar_tensor_tensor(
                V(ytmp, 0, 128, 0, [[1, NS]]), V(sx, 0, 128, blk * TT + SEQ, [[1, NS]]), dcol,
                V(PS[YSB], 0, 128, 0, [[1, NS]]), ALU.mult, ALU.add),
                R=[("sxs", blk), "cv4", ("ps", YSB)], W=["ytmp"])
            P.op("scalar", lambda e, blk=blk: e.activation(
                out=V(ysg, 0, 128, blk * TT + SEQ, [[1, NS]]), in_=V(ytmp, 0, 128, 0, [[1, NS]]), func=AF.Gelu),
                R=["ytmp"], W=[("sxs", blk)])
        ck(9)
        prefetch("glu", w_glu, 512, 4, 0, 512)
        prefetch("sz", w_in, 6144, 8, 3584, 512)
        for comp, dd in ((0, sre_p), (1, sim_p)):
            P.dma("sync", lambda e, comp=comp, dd=dd: e.dma_start(
                out=DR(dd, 0, [[1, 128], [128, 16], [1, 1]]), in_=V(fin, 0, 128, comp * 16, [[1, 16], [1, 1]]),
                allow_slow_non_contiguous=True),
                R=[("fin", comp, qg) for qg in range(16)], W=["out_fin%d" % comp])
        for comp, dd in ((0, sre_s), (1, sim_s)):
            pbs = [nextps() for _ in range(4)]
            for qg in range(16):
                pb = pbs[qg // 4]
                P.op("tensor", lambda e, comp=comp, qg=qg, pb=pb: e.matmul(
                    V(PS[pb], 0, NS, (qg % 4) * 128, [[1, 128]]),
                    V(sN, 0, 128, comp * 256 + qg * NS, [[1, NS]]),
                    V(ident_f, 0, 128, 0, [[1, 128]]), start=True, stop=True),
                    R=["sN%d" % comp, "ident_f"], W=[("ps", pb)])
            for i4 in range(4):
                P.op("scalar", lambda e, comp=comp, i4=i4, pb=pbs[i4]: e.activation(
                    out=V(Gc, 0, NS, comp * 2048 + i4 * 512, [[1, 512]]), in_=V(PS[pb], 0, NS, 0, [[1, 512]]), func=AF.Copy),
                    R=[("ps", pbs[i4])], W=Gkeys)
            P.dma("sync", lambda e, comp=comp, dd=dd: e.dma_start(out=dd.ap(), in_=V(Gc, 0, NS, comp * 2048, [[1, 2048]])),
                  R=Gkeys, W=["out_soT%d" % comp])
        P.barrier()
        stL.close()
        stS.close()

        ys2 = sb("ys2", [4, TT], BF16, stC)
        gt1 = [sb("gt1_%d" % i, [512], F32, stC) for i in range(2)]
        gt2 = [sb("gt2_%d" % i, [512], F32, stC) for i in range(2)]
        wi_glu = getw("glu", w_glu, 512, 4, 0, 512)
        wi_sz = getw("sz", w_in, 6144, 8, 3584, 512)
        it = 0
        for oc in range(4):
            for (cs, cn) in tblocks:
                b = it % 2; it += 1
                pb = nextps()
                mm_fm(wi_glu, 4, oc * 128, ysg, "ysgall", cs, cn, pb)
                P.op("scalar", lambda e, oc=oc, cn=cn, pb=pb, b=b: e.activation(
                    out=V(gt1[b], 0, 128, 0, [[1, cn]]), in_=V(PS[pb], 0, 128, 0, [[1, cn]]), func=AF.Sigmoid,
                    bias=V(cv4, 0, 128, oc, [[1, 1]])), R=[("ps", pb), "cv4"], W=[("gt1", b)])
                pb2 = nextps()
                mm_fm(wi_sz, 8, oc * 128, hT, "hTall", cs, cn, pb2)
                P.op("scalar", lambda e, cn=cn, pb2=pb2, b=b: e.activation(
                    out=V(gt2[b], 0, 128, 0, [[1, cn]]), in_=V(PS[pb2], 0, 128, 0, [[1, cn]]), func=AF.Sigmoid),
                    R=[("ps", pb2)], W=[("gt2", b)])
                P.op("vector", lambda e, cn=cn, pb2=pb2, b=b: e.tensor_tensor(
                    V(gt2[b], 0, 128, 0, [[1, cn]]), V(gt2[b], 0, 128, 0, [[1, cn]]), V(PS[pb2], 0, 128, 0, [[1, cn]]), ALU.mult),
                    R=[("gt2", b), ("ps", pb2)], W=[("gt2", b)])
                P.op("vector", lambda e, cn=cn, b=b, oc=oc, cs=cs: e.tensor_tensor(
                    V(gt1[b], 0, 128, 0, [[1, cn]]), V(gt1[b], 0, 128, 0, [[1, cn]]), V(ysg, 0, 128, oc * TT + cs, [[1, cn]]), ALU.mult),
                    R=[("gt1", b), "ysgall"], W=[("gt1", b)])
                P.op("vector", lambda e, cn=cn, b=b, oc=oc, cs=cs: e.tensor_tensor(
                    V(ys2, 0, 128, oc * TT + cs, [[1, cn]]), V(gt1[b], 0, 128, 0, [[1, cn]]), V(gt2[b], 0, 128, 0, [[1, cn]]), ALU.mult),
                    R=[("gt1", b), ("gt2", b)], W=["ys2all"])
        for half in range(2):
            wi_ps = load_w(w_ps, D, 4, half * 512, 512)
            wi_gs = load_w(w_in, 6144, 8, 5120 + half * 512, 512)
            for o4 in range(4):
                oc = half * 4 + o4
                for (cs, cn) in tblocks:
                    b = it % 2; it += 1
                    pb = nextps()
                    mm_fm(wi_gs, 8, o4 * 128, hT, "hTall", cs, cn, pb)
                    P.op("scalar", lambda e, cn=cn, pb=pb, b=b: e.activation(
                        out=V(gt1[b], 0, 128, 0, [[1, cn]]), in_=V(PS[pb], 0, 128, 0, [[1, cn]]), func=AF.Sigmoid),
                        R=[("ps", pb)], W=[("gt1", b)])
                    pb2 = nextps()
                    mm_fm(wi_ps, 4, o4 * 128, ys2, "ys2all", cs, cn, pb2)
                    P.op("vector", lambda e, cn=cn, pb2=pb2, b=b, oc=oc, cs=cs: e.tensor_tensor(
                        V(m_t, 0, 128, oc * TT + cs, [[1, cn]]), V(gt1[b], 0, 128, 0, [[1, cn]]), V(PS[pb2], 0, 128, 0, [[1, cn]]), ALU.mult),
                        R=[("gt1", b), ("ps", pb2)], W=[("m", oc)])
        prefetch("a0", w_in, 6144, 8, 0, 512)
        prefetch("b0", w_in, 6144, 8, 1024, 512)
        P.barrier()
        stC.close()

        stB = ExitStack(); stacks.append(stB)
        UW = 30 + SEQ
        u_t = sb("u_t", [8, UW], BF16, stB)
        vs_b = sb("vs_b", [8, NS], BF16, stB)
        us_f = sb("us_f", [8, NS], F32, stB)
        ul_f = sb("ul_f", [8, 30], F32, stB)
        vs_f = sb("vs_f", [8, NS], F32, stB)
        cwt = sb("cwt", [8, CK], F32, stB)
        bt1 = [sb("bt1_%d" % i, [512], F32, stB) for i in range(2)]
        bt2 = [sb("bt2_%d" % i, [512], F32, stB) for i in range(2)]
        acc1 = sb("acc1", [TT], F32, stB)
        acc2 = sb("acc2", [TT], F32, stB)
        bt3 = [sb("bt3_%d" % i, [512], F32, stB) for i in range(3)]
        cz1 = [sb("cz1_%d" % i, [512], F32, stB) for i in range(3)]

        def vcol(c, cs, cn):
            if cs < SEQ:
                return V(u_t, 0, 128, c * UW + 30 + cs, [[1, cn]])
            return V(vs_b, 0, 128, c * NS, [[1, cn]])
        P.dma("sync", lambda e: e.dma_start(out=cwt[:], in_=convw.ap()), W=["cwt"])
        P.op("gpsimd", lambda e: e.memset(V(u_t, 0, 128, 0, [[UW, 8], [1, 30]]), 0.0), W=[("u", c, -1) for c in range(8)])
        P.op("gpsimd", lambda e: e.memset(acc1[:], 0.0), W=["acc1"])
        P.op("gpsimd", lambda e: e.memset(acc2[:], 0.0), W=["acc2"])
        for half in range(2):
            wi_a = getw("a%d" % half, w_in, 6144, 8, half * 512, 512)
            wi_b = getw("b%d" % half, w_in, 6144, 8, 1024 + half * 512, 512)
            for o4 in range(4):
                c = half * 4 + o4
                for tbi, (cs, cn) in enumerate(tblocks):
                    b = it % 2; it += 1
                    pb = nextps()
                    mm_fm(wi_b, 8, o4 * 128, hT, "hTall", cs, cn, pb)
                    P.op("scalar", lambda e, cn=cn, pb=pb, b=b: e.activation(
                        out=V(bt1[b], 0, 128, 0, [[1, cn]]), in_=V(PS[pb], 0, 128, 0, [[1, cn]]), func=AF.Sigmoid),
                        R=[("ps", pb)], W=[("bt1", b)])
                    pb2 = nextps()
                    mm_fm(wi_a, 8, o4 * 128, hT, "hTall", cs, cn, pb2)
                    if cs < SEQ:
                        P.op("vector", lambda e, cn=cn, pb2=pb2, b=b, c=c, cs=cs: e.tensor_tensor(
                            V(u_t, 0, 128, c * UW + 30 + cs, [[1, cn]]), V(bt1[b], 0, 128, 0, [[1, cn]]), V(PS[pb2], 0, 128, 0, [[1, cn]]), ALU.mult),
                            R=[("bt1", b), ("ps", pb2)], W=[("u", c, tbi)])
                        if tbi == NTB - 1:
                            P.op("vector", lambda e, pb2=pb2, b=b, c=c: e.tensor_tensor(
                                V(ul_f, 0, 128, c * 30, [[1, 30]]), V(bt1[b], 0, 128, 482, [[1, 30]]), V(PS[pb2], 0, 128, 482, [[1, 30]]), ALU.mult),
                                R=[("bt1", b), ("ps", pb2)], W=[("ul", c)])
                    else:
                        P.op("vector", lambda e, cn=cn, pb2=pb2, b=b, c=c: e.tensor_tensor(
                            V(us_f, 0, 128, c * NS, [[1, NS]]), V(bt1[b], 0, 128, 0, [[1, cn]]), V(PS[pb2], 0, 128, 0, [[1, cn]]), ALU.mult),
                            R=[("bt1", b), ("ps", pb2)], W=[("us", c)])
        stB1 = ExitStack(); stacks.append(stB1)
        nrows = NS * 30
        cacheT = sb("cacheT", [8, nrows], F32, stB1)
        crow = [sb("crow%d" % i, [D], F32, stB1) for i in range(2)]
        ctmp = sb("ctmp", [nrows], F32, stB1)
        otm = sb("otm", [D], F32, stB1)
        for half in range(2):
            pb = nextps()
            for kk in range(4):
                c = half * 4 + kk
                P.op("tensor", lambda e, c=c, kk=kk, pb=pb: e.matmul(
                    V(PS[pb], 0, 30, kk * 128, [[1, 128]]), V(ul_f, 0, 128, c * 30, [[1, 30]]),
                    V(ident_f, 0, 128, 0, [[1, 128]]), start=True, stop=True),
                    R=[("ul", c), "ident_f"], W=[("ps", pb)])
            P.op("scalar", lambda e, half=half, pb=pb: e.activation(
                out=V(otm, 0, 30, half * 512, [[1, 512]]), in_=V(PS[pb], 0, 30, 0, [[1, 512]]), func=AF.Copy),
                R=[("ps", pb)], W=[("otm", half)])
        P.dma("sync", lambda e: e.dma_start(out=conv_p.ap(), in_=V(otm, 0, 30, 0, [[1, D]])),
              R=[("otm", 0), ("otm", 1)], W=["otm_out"])
        P.dma("sync", lambda e: e.dma_start(out=DR(conv_s, 0, [[30 * D, NS], [1, 29 * D]]),
                                            in_=DR(cache, D, [[30 * D, NS], [1, 29 * D]])), W=["conv_s_a"])
        for half in range(2):
            pb = nextps()
            for kk in range(4):
                c = half * 4 + kk
                P.op("tensor", lambda e, c=c, kk=kk, pb=pb: e.matmul(
                    V(PS[pb], 0, NS, kk * 128, [[1, 128]]), V(us_f, 0, 128, c * NS, [[1, NS]]),
                    V(ident_f, 0, 128, 0, [[1, 128]]), start=True, stop=True),
                    R=[("us", c), "ident_f"], W=[("ps", pb)])
            P.op("scalar", lambda e, half=half, pb=pb: e.activation(
                out=V(otm, 32, NS, half * 512, [[1, 512]]), in_=V(PS[pb], 0, NS, 0, [[1, 512]]), func=AF.Copy),
                R=[("ps", pb)], W=[("otm2", half)])
        P.dma("sync", lambda e: e.dma_start(out=DR(conv_s, 29 * D, [[30 * D, NS], [1, D]]), in_=V(otm, 32, NS, 0, [[1, D]])),
              R=[("otm2", 0), ("otm2", 1)], W=["conv_s_b"])
        rtiles = [(r, min(128, nrows - r)) for r in range(0, nrows, 128)]
        for ri, (r0, nr) in enumerate(rtiles):
            b = ri % 2
            P.dma("sync", lambda e, r0=r0, nr=nr, b=b: e.dma_start(out=V(crow[b], 0, nr, 0, [[1, D]]),
                                                                  in_=DR(cache, r0 * D, [[D, nr], [1, D]])), W=[("crow", b)])
            for c in range(8):
                pb = nextps()
                P.op("tensor", lambda e, c=c, pb=pb, nr=nr, b=b: e.matmul(
                    V(PS[pb], 0, 128, 0, [[1, nr]]), V(crow[b], 0, nr, c * 128, [[1, 128]]),
                    V(ident_f, 0, nr, 0, [[1, nr]]), start=True, stop=True),
                    R=[("crow", b), "ident_f"], W=[("ps", pb)])
                P.op("scalar", lambda e, c=c, pb=pb, nr=nr, r0=r0: e.activation(
                    out=V(cacheT, 0, 128, c * nrows + r0, [[1, nr]]), in_=V(PS[pb], 0, 128, 0, [[1, nr]]), func=AF.Copy),
                    R=[("ps", pb)], W=[("cacheT", c)])
        for c in range(8):
            P.op("vector", lambda e, c=c: e.tensor_tensor(
                V(ctmp, 0, 128, 0, [[30, NS], [1, 30]]), V(cacheT, 0, 128, c * nrows, [[30, NS], [1, 30]]),
                V(cwt, 0, 128, c * CK, [[0, NS], [1, 30]]), ALU.mult), R=[("cacheT", c), "cwt"], W=["ctmp"])
            P.op("vector", lambda e, c=c: e.tensor_reduce(
                V(vs_f, 0, 128, c * NS, [[1, NS]]), V(ctmp, 0, 128, 0, [[30, NS], [1, 30]]), mybir.AxisListType.X, ALU.add),
                R=["ctmp"], W=[("vs", c)])
            P.op("vector", lambda e, c=c: e.scalar_tensor_tensor(
                V(vs_f, 0, 128, c * NS, [[1, NS]]), V(us_f, 0, 128, c * NS, [[1, NS]]), V(cwt, 0, 128, c * CK + 30, [[1, 1]]),
                V(vs_f, 0, 128, c * NS, [[1, NS]]), ALU.mult, ALU.add), R=[("vs", c), ("us", c), "cwt"], W=[("vs", c)])
        P.barrier()
        stB1.close()
        stB2 = ExitStack(); stacks.append(stB2)
        dg = [sb("dg%d" % i, [CK, 128], BF16, stB2) for i in range(2)]
        vsq = [sb("vsq%d" % i, [512], BF16, stB2) for i in range(2)]
        msq = sb("msq", [512], F32, stB2)
        ND = 5
        accD = [sb("accD%d" % i, [SEQ], F32, stB2) for i in range(2)]
        conv_items = []
        for c in range(8):
            for tbi, (cs, cn) in reversed(list(enumerate(tblocks))):
                conv_items.append((c, tbi, cs, cn))
        conv_state = {}

        def build_dg(c):
            d_ = dg[c % 2]
            for k in range(ND, CK):
                P.op("scalar", lambda e, k=k: e.activation(
                    out=V(d_, 0, 128, k * 128, [[1, 128]]), in_=ident_f[:], func=AF.Identity, scale=V(cwt, 0, 128, c * CK + k, [[1, 1]])),
                    R=["ident_f", "cwt"], W=[("dg", c % 2, k)])

        def dve_taps(c, ks):
            a_ = accD[c % 2]
            ukeys = [("u", c, t) for t in range(-1, NTB)]
            for k in ks:
                srcu = V(u_t, 0, 128, c * UW + k, [[1, SEQ]])
                wcol = V(cwt, 0, 128, c * CK + k, [[1, 1]])
                if k == 0:
                    P.op("vector", lambda e, srcu=srcu, wcol=wcol: e.tensor_scalar(a_[:], srcu, wcol, None, ALU.mult),
                         R=ukeys + ["cwt"], W=[("accD", c % 2)])
                else:
                    P.op("vector", lambda e, srcu=srcu, wcol=wcol: e.scalar_tensor_tensor(a_[:], srcu, wcol, a_[:], ALU.mult, ALU.add),
                         R=ukeys + ["cwt", ("accD", c % 2)], W=[("accD", c % 2)])

        def sC1(ii):
            c, tbi, cs, cn = conv_items[ii]
            d_ = dg[c % 2]
            if ii == 0:
                build_dg(0)
                dve_taps(0, range(ND))
            if ii % len(tblocks) == 1 and c + 1 < 8:
                build_dg(c + 1)
            slots = list(range(1, len(tblocks)))[:3]
            per = -(-ND // len(slots))
            if c + 1 < 8 and (ii % len(tblocks)) in slots:
                j = slots.index(ii % len(tblocks))
                dve_taps(c + 1, range(per * j, min(ND, per * j + per)))
            b = ii % 2
            bias = V(cv8, 0, 128, c, [[1, 1]])
            if cs < SEQ:
                pb = nextps()
                for k in range(ND, CK):
                    P.op("tensor", lambda e, k=k, pb=pb: e.matmul(
                        V(PS[pb], 0, 128, 0, [[1, 512]]), V(d_, 0, 128, k * 128, [[1, 128]]),
                        V(u_t, 0, 128, c * UW + cs + k, [[1, 512]]), start=(k == ND), stop=(k == CK - 1)),
                        R=[("dg", c % 2, k), ("u", c, tbi), ("u", c, tbi - 1)], W=[("ps", pb)])
                P.op("vector", lambda e, pb=pb: e.tensor_tensor(
                    V(bt2[b], 0, 128, 0, [[1, cn]]), V(PS[pb], 0, 128, 0, [[1, cn]]), V(accD[c % 2], 0, 128, cs, [[1, cn]]), ALU.add),
                    R=[("ps", pb), ("accD", c % 2)], W=[("bt2", b)])
                src = V(bt2[b], 0, 128, 0, [[1, cn]]); rk = ("bt2", b)
            else:
                src = V(vs_f, 0, 128, c * NS, [[1, NS]]); rk = ("vs", c)
            P.op("scalar", lambda e: e.activation(
                out=vcol(c, cs, cn), in_=src, func=AF.Identity, bias=bias),
                R=[rk, "cv8"], W=[("u", c, tbi)])
            P.op("scalar", lambda e: e.activation(
                out=V(vsq[b], 0, 128, 0, [[1, cn]]), in_=src, func=AF.Square, bias=bias),
                R=[rk, "cv8"], W=[("vsq", b)])

        def sC2(ii):
            c, tbi, cs, cn = conv_items[ii]
            b = ii % 2
            p1 = nextps(); p2 = nextps()
            P.op("tensor", lambda e: e.matmul(
                V(PS[p1], 0, 128, 0, [[1, cn]]), ones_b[:], vcol(c, cs, cn), start=True, stop=True),
                R=["ones_b", ("u", c, tbi)], W=[("ps", p1)])
            P.op("tensor", lambda e: e.matmul(
                V(PS[p2], 0, 128, 0, [[1, cn]]), ones_b[:], V(vsq[b], 0, 128, 0, [[1, cn]]), start=True, stop=True),
                R=["ones_b", ("vsq", b)], W=[("ps", p2)])
            P.op("vector", lambda e: e.tensor_tensor(
                V(acc1, 0, 128, cs, [[1, cn]]), V(acc1, 0, 128, cs, [[1, cn]]), V(PS[p1], 0, 128, 0, [[1, cn]]), ALU.add),
                R=[("ps", p1), ("acc1", tbi)], W=[("acc1", tbi)])
            P.op("vector", lambda e: e.tensor_tensor(
                V(acc2, 0, 128, cs, [[1, cn]]), V(acc2, 0, 128, cs, [[1, cn]]), V(PS[p2], 0, 128, 0, [[1, cn]]), ALU.add),
                R=[("ps", p2), ("acc2", tbi)], W=[("acc2", tbi)])
        pipeline([sC1, sC2], len(conv_items))
        for tbi, (cs, cn) in enumerate(tblocks):
            a1 = V(acc1, 0, 128, cs, [[1, cn]]); a2 = V(acc2, 0, 128, cs, [[1, cn]]); mq = V(msq, 0, 128, 0, [[1, cn]])
            k1 = ("acc1", tbi); k2 = ("acc2", tbi)
            P.op("vector", lambda e, a1=a1: e.tensor_scalar(a1, a1, 1.0 / D, None, ALU.mult), R=[k1, "acc1"], W=[k1])
            P.op("vector", lambda e, a2=a2: e.tensor_scalar(a2, a2, 1.0 / D, EPS, ALU.mult, ALU.add), R=[k2, "acc2"], W=[k2])
            P.op("vector", lambda e, a1=a1, mq=mq: e.tensor_tensor(mq, a1, a1, ALU.mult), R=[k1], W=["msq"])
            P.op("vector", lambda e, a2=a2, mq=mq: e.tensor_tensor(a2, a2, mq, ALU.subtract), R=[k2, "msq"], W=[k2])
            P.op("scalar", lambda e, a2=a2: e.activation(out=a2, in_=a2, func=AF.Ln), R=[k2], W=[k2])
            P.op("scalar", lambda e, a2=a2: e.activation(out=a2, in_=a2, func=AF.Exp, scale=-0.5), R=[k2], W=[k2])
            P.op("vector", lambda e, a1=a1, a2=a2: e.scalar_tensor_tensor(a1, a1, -1.0, a2, ALU.mult, ALU.mult), R=[k1, k2], W=[k1])
        v2_items = []
        for half in range(2):
            for o4 in range(4):
                for tbi, (cs, cn) in enumerate(tblocks):
                    v2_items.append((half, o4, tbi, cs, cn))
        wz = {}

        def sV1(ii):
            half, o4, tbi, cs, cn = v2_items[ii]
            c = half * 4 + o4
            if o4 == 0 and tbi == 0:
                wz[half] = getw("z%d" % half, w_in, 6144, 8, 2048 + half * 512, 512)
            b = ii % 3
            pb = nextps()
            mm_fm(wz[half], 8, o4 * 128, hT, "hTall", cs, cn, pb)
            P.op("scalar", lambda e: e.activation(
                out=V(cz1[b], 0, 128, 0, [[1, cn]]), in_=V(PS[pb], 0, 128, 0, [[1, cn]]), func=AF.Silu),
                R=[("ps", pb)], W=[("cz1", b)])
            P.op("vector", lambda e: e.tensor_tensor(
                V(bt3[b], 0, 128, 0, [[1, cn]]), vcol(c, cs, cn), V(acc2, 0, 128, cs, [[1, cn]]), ALU.mult),
                R=[("u", c, tbi), ("acc2", tbi)], W=[("bt3", b)])
            P.op("vector", lambda e: e.tensor_tensor(
                V(bt3[b], 0, 128, 0, [[1, cn]]), V(bt3[b], 0, 128, 0, [[1, cn]]), V(acc1, 0, 128, cs, [[1, cn]]), ALU.add),
                R=[("bt3", b), ("acc1", tbi)], W=[("bt3", b)])

        def sV2(ii):
            half, o4, tbi, cs, cn = v2_items[ii]
            c = half * 4 + o4
            b = ii % 3
            P.op("scalar", lambda e: e.activation(
                out=V(bt3[b], 0, 128, 0, [[1, cn]]), in_=V(bt3[b], 0, 128, 0, [[1, cn]]), func=AF.Silu,
                scale=V(cv8, 0, 128, 8 + c, [[1, 1]]), bias=V(cv8, 0, 128, 16 + c, [[1, 1]])),
                R=[("bt3", b), "cv8"], W=[("bt3", b)])
            P.op("gpsimd", lambda e: e.tensor_tensor(
                vcol(c, cs, cn), V(bt3[b], 0, 128, 0, [[1, cn]]), V(cz1[b], 0, 128, 0, [[1, cn]]), ALU.mult),
                R=[("bt3", b), ("cz1", b)], W=[("u", c, tbi)])
        pipeline([sV1, sV2], len(v2_items))
        v2keys_all = {tbi: [("u", c, tbi) for c in range(8)] for tbi in range(len(tblocks))}
        for half in range(2):
            wi_pc = load_w(w_pc, D, 8, half * 512, 512)
            wi_gc = load_w(w_in, 6144, 8, 4096 + half * 512, 512)
            for o4 in range(4):
                oc = half * 4 + o4
                for tbi_, (cs, cn) in enumerate(tblocks):
                    v2keys = v2keys_all[tbi_]
                    b = it % 2; it += 1
                    pb = nextps()
                    mm_fm(wi_gc, 8, o4 * 128, hT, "hTall", cs, cn, pb)
                    P.op("scalar", lambda e, cn=cn, pb=pb, b=b: e.activation(
                        out=V(bt1[b], 0, 128, 0, [[1, cn]]), in_=V(PS[pb], 0, 128, 0, [[1, cn]]), func=AF.Sigmoid),
                        R=[("ps", pb)], W=[("bt1", b)])
                    pb2 = nextps()
                    buf = WB[wi_pc]
                    for k in range(8):
                        P.op("tensor", lambda e, k=k, buf=buf, o4=o4, cs=cs, cn=cn, pb2=pb2: e.matmul(
                            V(PS[pb2], 0, 128, 0, [[1, cn]]), V(buf, 0, 128, k * 512 + o4 * 128, [[1, 128]]),
                            vcol(k, cs, cn), start=(k == 0), stop=(k == 7)),
                            R=[("wb", wi_pc)] + v2keys, W=[("ps", pb2)])
                    P.op("vector", lambda e, cn=cn, pb2=pb2, b=b: e.tensor_tensor(
                        V(bt1[b], 0, 128, 0, [[1, cn]]), V(bt1[b], 0, 128, 0, [[1, cn]]), V(PS[pb2], 0, 128, 0, [[1, cn]]), ALU.mult),
                        R=[("bt1", b), ("ps", pb2)], W=[("bt1", b)])
                    P.op("vector", lambda e, cn=cn, b=b, oc=oc, cs=cs: e.tensor_tensor(
                        V(m_t, 0, 128, oc * TT + cs, [[1, cn]]), V(m_t, 0, 128, oc * TT + cs, [[1, cn]]), V(bt1[b], 0, 128, 0, [[1, cn]]), ALU.add),
                        R=[("bt1", b), ("m", oc)], W=[("m", oc)])
        wo = [load_w(w_out, D, 8, h2 * 512, 512) for h2 in range(2)]
        wg = [load_w(w_pg, D, 8, h2 * 512, 512) for h2 in range(2)]
        P.barrier()
        stB2.close()
        stB.close()

        stH.close()
        stD = ExitStack(); stacks.append(stD)
        wple_b = sb("wple_b", [2, D], BF16, stD)
        bpg_b = sb("bpg_b", [D], BF16, stD)
        P.dma("gpsimd", lambda e: e.dma_start(out=V(bpg_b, 0, 1, 0, [[1, D]]),
                                              in_=DR(rows, 3 * D, [[0, 1], [1, D]])), W=["bpg_b"])
        rows2 = sb("rows2", [2, D], F32, stD)
        P.dma("sync", lambda e: e.dma_start(out=V(rows2, 0, 128, 0, [[1, 2 * D]]), in_=DR(rows, D, [[0, 128], [1, 2 * D]])), W=["rows2"])
        for (dst, wd, kc, nm) in ((wple_b, w_ple, 2, "wple"),):
            for h2 in range(2):
                P.dma("gpsimd", lambda e, dst=dst, wd=wd, kc=kc, h2=h2: e.dma_start(
                    out=V(dst, 0, 128, h2 * 512, [[D, kc], [1, 512]]),
                    in_=DR(wd, h2 * 512, [[D, 128], [128 * D, kc], [1, 512]])), W=[(nm, h2)])
        RX, RE, R1, RB = 4, 5, 3, 3
        xd = [sb("xd%d" % i, [D], F32, stD) for i in range(RX)]
        osb = [sb("osb%d" % i, [D], F32, stD) for i in range(RX)]
        en = [sb("en%d" % i, [D], F32, stD) for i in range(RE)]
        x1 = [sb("x1_%d" % i, [D], F32, stD) for i in range(R1)]
        pd = [sb("pd%d" % i, [256], BF16, stD) for i in range(RB)]
        pT = [sb("pT%d" % i, [2, 128], BF16, stD) for i in range(RB)]
        x1b = [sb("x1b%d" % i, [D], BF16, stD) for i in range(RB)]
        x1T = [sb("x1T%d" % i, [8, 128], BF16, stD) for i in range(RB)]
        gsig = [sb("gsig%d" % i, [D], F32, stD) for i in range(2)]
        yo = [sb("yo%d" % i, [D], F32, stD) for i in range(2)]
        jks = [sb("jk%d" % i, [512], BF16, stD) for i in range(4)]
        ssD = sb("ssD", [len(ttiles), 8], F32, stD)
        mkeys = [("m", oc) for oc in range(8)]
        tst = {}

        def col(ti, nt, c0):
            return V(ssD, 0, nt, ti * 8 + c0, [[1, 1]])

        def tl1(ti):
            r0, nt = ttiles[ti]
            srcx = DR(xp, r0 * D, [[D, nt], [1, D]]) if r0 < SEQ else DR(xs, 0, [[D, nt], [1, D]])
            srcp = DR(pp, r0 * 256, [[256, nt], [1, 256]]) if r0 < SEQ else DR(psm, 0, [[256, nt], [1, 256]])
            bx = ti % RX; bb = ti % RB
            P.dma("sync", lambda e: e.dma_start(out=V(xd[bx], 0, nt, 0, [[1, D]]), in_=srcx), W=[("xd", bx)])
            P.dma("gpsimd", lambda e: e.dma_start(out=V(pd[bb], 0, nt, 0, [[1, 256]]), in_=srcp), W=[("pd", bb)])
            ob = [nextps(), nextps()]
            for h2 in range(2):
                for k in range(8):
                    P.op("tensor", lambda e, k=k, h2=h2: e.matmul(
                        V(PS[ob[h2]], 0, nt, 0, [[1, 512]]), V(m_t, 0, 128, k * TT + r0, [[1, nt]]),
                        V(WB[wo[h2]], 0, 128, k * 512, [[1, 512]]), start=(k == 0), stop=(k == 7)),
                        R=mkeys + [("wb", wo[h2])], W=[("ps", ob[h2])])
                P.op("scalar", lambda e, h2=h2: e.activation(
                    out=V(jks[h2], 0, nt, 0, [[1, 512]]), in_=V(PS[ob[h2]], 0, nt, 0, [[1, 512]]), func=AF.Square,
                    accum_out=col(ti, nt, h2)), R=[("ps", ob[h2])], W=[("jk", h2), ("ssD", ti, h2)])
                P.op("scalar", lambda e, h2=h2: e.activation(
                    out=V(osb[bx], 0, nt, h2 * 512, [[1, 512]]), in_=V(PS[ob[h2]], 0, nt, 0, [[1, 512]]), func=AF.Copy),
                    R=[("ps", ob[h2])], W=[("osb", bx, h2)])
            pb = nextps()
            for k in range(2):
                P.op("tensor", lambda e, k=k: e.matmul(
                    V(PS[pb], 0, 128, k * nt, [[1, nt]]), V(pd[bb], 0, nt, k * 128, [[1, 128]]),
                    V(ident_b, 0, nt, 0, [[1, nt]]), start=True, stop=True),
                    R=[("pd", bb), "ident_b"], W=[("ps", pb)])
            P.op("vector", lambda e: e.tensor_copy(
                V(pT[bb], 0, 128, 0, [[128, 2], [1, nt]]), V(PS[pb], 0, 128, 0, [[nt, 2], [1, nt]])),
                R=[("ps", pb)], W=[("pT", bb)])

        def tl2(ti):
            r0, nt = ttiles[ti]
            bb = ti % RB; be = ti % RE
            eb = [nextps(), nextps()]
            for h2 in range(2):
                for k in range(2):
                    P.op("tensor", lambda e, k=k, h2=h2: e.matmul(
                        V(PS[eb[h2]], 0, nt, 0, [[1, 512]]), V(pT[bb], 0, 128, k * 128, [[1, nt]]),
                        V(wple_b, 0, 128, k * D + h2 * 512, [[1, 512]]), start=(k == 0), stop=(k == 1)),
                        R=[("pT", bb), ("wple", h2)], W=[("ps", eb[h2])])
                P.op("scalar", lambda e, h2=h2: e.activation(
                    out=V(jks[2 + h2], 0, nt, 0, [[1, 512]]), in_=V(PS[eb[h2]], 0, nt, 0, [[1, 512]]), func=AF.Square,
                    accum_out=col(ti, nt, 2 + h2)), R=[("ps", eb[h2])], W=[("jk", 2 + h2), ("ssD", ti, 2 + h2)])
                P.op("scalar", lambda e, h2=h2: e.activation(
                    out=V(en[be], 0, nt, h2 * 512, [[1, 512]]), in_=V(PS[eb[h2]], 0, nt, 0, [[1, 512]]), func=AF.Copy),
                    R=[("ps", eb[h2])], W=[("en", be, h2)])
            a = col(ti, nt, 0); a2 = col(ti, nt, 1)
            P.op("vector", lambda e: e.tensor_tensor(a, a, a2, ALU.add), R=[("ssD", ti, 0), ("ssD", ti, 1)], W=[("ssD", ti, 0)])

        def tl3(ti):
            r0, nt = ttiles[ti]
            a = col(ti, nt, 0)
            P.op("scalar", lambda e: e.activation(out=a, in_=a, func=AF.Ln, scale=1.0 / D, bias=EPS), R=[("ssD", ti, 0)], W=[("ssD", ti, 0)])
            P.op("scalar", lambda e: e.activation(out=a, in_=a, func=AF.Exp, scale=-0.5), R=[("ssD", ti, 0)], W=[("ssD", ti, 0)])
            c = col(ti, nt, 2); c2 = col(ti, nt, 3)
            P.op("vector", lambda e: e.tensor_tensor(c, c, c2, ALU.add), R=[("ssD", ti, 2), ("ssD", ti, 3)], W=[("ssD", ti, 2)])

        def tl4(ti):
            r0, nt = ttiles[ti]
            bx = ti % RX; b1 = ti % R1; bb = ti % RB
            a = col(ti, nt, 0); c = col(ti, nt, 2)
            P.op("scalar", lambda e: e.activation(out=c, in_=c, func=AF.Ln, scale=1.0 / D, bias=EPS), R=[("ssD", ti, 2)], W=[("ssD", ti, 2)])
            P.op("scalar", lambda e: e.activation(out=c, in_=c, func=AF.Exp, scale=-0.5), R=[("ssD", ti, 2)], W=[("ssD", ti, 2)])
            P.op("vector", lambda e: e.scalar_tensor_tensor(
                V(x1[b1], 0, nt, 0, [[1, D]]), V(osb[bx], 0, nt, 0, [[1, D]]), a,
                V(rows2, 0, nt, 0, [[1, D]]), ALU.mult, ALU.mult),
                R=[("osb", bx, 0), ("osb", bx, 1), ("ssD", ti, 0), "rows2"], W=[("x1", b1)])
            P.op("vector", lambda e: e.tensor_tensor(
                V(x1[b1], 0, nt, 0, [[1, D]]), V(x1[b1], 0, nt, 0, [[1, D]]), V(xd[bx], 0, nt, 0, [[1, D]]), ALU.add),
                R=[("x1", b1), ("xd", bx)], W=[("x1", b1)])
            P.op("gpsimd", lambda e: e.tensor_copy(V(x1b[bb], 0, nt, 0, [[1, D]]), V(x1[b1], 0, nt, 0, [[1, D]])),
                 R=[("x1", b1)], W=[("x1b", bb)])

        def tl5(ti):
            r0, nt = ttiles[ti]
            bb = ti % RB; be = ti % RE
            for half in range(2):
                pb = nextps()
                for kk in range(4):
                    k = half * 4 + kk
                    P.op("tensor", lambda e, k=k, kk=kk, pb=pb: e.matmul(
                        V(PS[pb], 0, 128, kk * nt, [[1, nt]]), V(x1b[bb], 0, nt, k * 128, [[1, 128]]),
                        V(ident_b, 0, nt, 0, [[1, nt]]), start=True, stop=True),
                        R=[("x1b", bb), "ident_b"], W=[("ps", pb)])
                P.op("scalar", lambda e, half=half, pb=pb: e.activation(
                    out=V(x1T[bb], 0, 128, half * 4 * 128, [[128, 4], [1, nt]]),
                    in_=V(PS[pb], 0, 128, 0, [[nt, 4], [1, nt]]), func=AF.Copy),
                    R=[("ps", pb)], W=[("x1T", bb, half)])
            c = col(ti, nt, 2)
            P.op("vector", lambda e: e.scalar_tensor_tensor(
                V(en[be], 0, nt, 0, [[1, D]]), V(en[be], 0, nt, 0, [[1, D]]), c,
                V(rows2, 0, nt, D, [[1, D]]), ALU.mult, ALU.mult),
                R=[("en", be, 0), ("en", be, 1), ("ssD", ti, 2), "rows2"], W=[("en", be, 0), ("en", be, 1)])

        def tl6(ti):
            r0, nt = ttiles[ti]
            bb = ti % RB; be = ti % RE; b1 = ti % R1; b2 = ti % 2
            gb = [nextps(), nextps()]
            for h2 in range(2):
                P.op("tensor", lambda e, h2=h2: e.matmul(
                    V(PS[gb[h2]], 0, nt, 0, [[1, 512]]), V(ones_b, 0, 1, 0, [[1, nt]]),
                    V(bpg_b, 0, 1, h2 * 512, [[1, 512]]), start=True, stop=False),
                    R=["ones_b", "bpg_b"], W=[("ps", gb[h2])])
                for k in range(8):
                    P.op("tensor", lambda e, k=k, h2=h2: e.matmul(
                        V(PS[gb[h2]], 0, nt, 0, [[1, 512]]), V(x1T[bb], 0, 128, k * 128, [[1, nt]]),
                        V(WB[wg[h2]], 0, 128, k * 512, [[1, 512]]), start=False, stop=(k == 7)),
                        R=[("x1T", bb, 0), ("x1T", bb, 1), ("wb", wg[h2])], W=[("ps", gb[h2])])
                P.op("scalar", lambda e, h2=h2: e.activation(
                    out=V(gsig[b2], 0, nt, h2 * 512, [[1, 512]]), in_=V(PS[gb[h2]], 0, nt, 0, [[1, 512]]), func=AF.Sigmoid),
                    R=[("ps", gb[h2])], W=[("gsig", b2, h2)])
            P.op("vector", lambda e: e.tensor_tensor(
                V(en[be], 0, nt, 0, [[1, D]]), V(en[be], 0, nt, 0, [[1, D]]), V(gsig[b2], 0, nt, 0, [[1, D]]), ALU.mult),
                R=[("en", be, 0), ("en", be, 1), ("gsig", b2, 0), ("gsig", b2, 1)], W=[("en", be, 0), ("en", be, 1)])
            P.op("vector", lambda e: e.tensor_tensor(
                V(yo[b2], 0, nt, 0, [[1, D]]), V(en[be], 0, nt, 0, [[1, D]]), V(x1[b1], 0, nt, 0, [[1, D]]), ALU.add),
                R=[("en", be, 0), ("en", be, 1), ("x1", b1)], W=[("yo", b2)])
            dsty = DR(y_p, r0 * D, [[D, nt], [1, D]]) if r0 < SEQ else DR(y_s, 0, [[D, nt], [1, D]])
            P.dma("sync", lambda e: e.dma_start(out=dsty, in_=V(yo[b2], 0, nt, 0, [[1, D]])),
                  R=[("yo", b2)], W=[("yout", ti)])
        pipeline([tl1, tl2, tl3, tl4, tl5, tl6], len(ttiles))
        P.barrier(final=True)


    except _Stop:
        for st_ in reversed(stacks):
            st_.close()
        P.barrier(final=True)
    with nc.Block() as block:
        P.replay(block)
    for st_ in reversed(stacks):
        st_.close()
    es.close()
    return nc


def _host_layouts(inp, SEQ):
    f = lambda a: np.ascontiguousarray(np.asarray(a, dtype=np.float32))
    out = {}
    out["w_in"] = f(inp["w_in"][0]); out["w_pc"] = f(inp["w_pc"][0]); out["w_ps"] = f(inp["w_ps"][0])
    out["w_glu"] = f(inp["w_glu"][0]); out["w_out"] = f(inp["w_out"][0]); out["w_pg"] = f(inp["w_pg"][0])
    out["w_ple"] = f(inp["w_ple"][0])
    out["rows"] = f(np.stack([inp["g_pre"][0], inp["g_post"][0], inp["g_ple"][0], inp["b_pg"][0]]))
    col8 = lambda v: np.asarray(v).reshape(8, 128).T
    col4 = lambda v: np.asarray(v).reshape(4, 128).T
    out["colv8"] = f(np.stack([col8(inp["conv_b"][0]), col8(inp["ln_g"][0]), col8(inp["ln_b"][0])], axis=1))
    out["colv4"] = f(np.stack([col4(inp["b_glu"][0]), col4(inp["ssm_d"][0])], axis=1))
    out["convw"] = f(np.asarray(inp["conv_w"][0]).T.reshape(8, 128, CK).transpose(1, 0, 2))
    a_re = np.asarray(inp["ssm_a_re"][0]); a_im = np.asarray(inp["ssm_a_im"][0]); ldt = np.asarray(inp["ssm_log_dt"][0])
    b_re = np.asarray(inp["ssm_b_re"][0]); b_im = np.asarray(inp["ssm_b_im"][0])
    c_re = np.asarray(inp["ssm_c_re"][0]); c_im = np.asarray(inp["ssm_c_im"][0])
    PA = np.zeros((128, 3, 4, 64), np.float32)
    BA = np.zeros((128, 2, 4, 64), np.float32)
    MA = np.zeros((128, 8), np.float32)
    PB = np.zeros((128, 3, 16), np.float32)
    CBm = np.zeros((128, 2, 16, 16), np.float32)
    BBm = np.zeros((128, 2, 16, 16), np.float32)
    MB = np.zeros((128, 4, 8), np.float32)
    for blk in range(4):
        for gl in range(8):
            g = blk * 8 + gl
            q, par = gl // 2, gl % 2
            rows = slice(gl * 16, gl * 16 + 16)
            PA[rows, 0, blk, :] = a_re[g][None, :]
            PA[rows, 1, blk, :] = a_im[g][None, :]
            PA[rows, 2, blk, :] = ldt[g]
            BA[rows, 0, blk, :] = b_re[g].T
            BA[rows, 1, blk, :] = b_im[g].T
            MA[rows, gl] = 1.0
            qg = blk * 4 + q
            prow = slice(par * 64, (par + 1) * 64)
            PB[prow, 0, qg] = a_re[g]; PB[prow, 1, qg] = a_im[g]; PB[prow, 2, qg] = ldt[g]
            CBm[prow, 0, qg, :] = c_re[g].T
            CBm[prow, 1, qg, :] = c_im[g].T
            BBm[prow, 0, qg, :] = b_re[g]
            BBm[prow, 1, qg, :] = b_im[g]
            MB[prow, q, gl] = 1.0
    out["PA"] = PA.reshape(128, 3, 256); out["BA"] = BA.reshape(128, 2, 256); out["MA"] = MA
    out["PB"] = PB; out["CB"] = CBm.reshape(128, 2, 256); out["BB"] = BBm.reshape(128, 2, 256); out["MB"] = MB
    out["ident"] = np.eye(128, dtype=np.float32)
    out["iota"] = np.ascontiguousarray(np.broadcast_to(np.arange(SEQ, dtype=np.float32), (128, SEQ)))
    return out


def make_in_maps(inp, SEQ, ncores):
    shared = _host_layouts(inp, SEQ)
    f = lambda a: np.ascontiguousarray(np.asarray(a, dtype=np.float32))
    maps = []
    for i in range(ncores):
        m = dict(shared)
        m["xp"] = f(inp["x_prompt"][i]); m["xs"] = f(inp["x_sample"][i * NS:(i + 1) * NS, 0])
        m["pp"] = f(inp["p_prompt"][0, i]); m["psm"] = f(inp["p_sample"][0, i * NS:(i + 1) * NS, 0])
        m["cache"] = f(inp["cache_conv"][0, i * NS:(i + 1) * NS]).reshape(NS * 30, D)
        m["st_re"] = f(inp["state_ssm_re"][0, i * NS:(i + 1) * NS]).reshape(NS, 2048)
        m["st_im"] = f(inp["state_ssm_im"][0, i * NS:(i + 1) * NS]).reshape(NS, 2048)
        maps.append(m)
    return maps


def assemble(results, SEQ, ncores):
    y_p = np.stack([r["y_p"] for r in results]).reshape(ncores, SEQ, D)
    y_s = np.concatenate([r["y_s"] for r in results]).reshape(ncores * NS, 1, D)
    conv_p = np.stack([r["conv_p"] for r in results]).reshape(1, ncores, 30, D)
    conv_s = np.concatenate([r["conv_s"].reshape(NS, 30, D) for r in results]).reshape(1, ncores * NS, 30, D)
    sre_p = np.stack([r["sre_p"] for r in results]).reshape(1, ncores, 32, 64)
    sim_p = np.stack([r["sim_p"] for r in results]).reshape(1, ncores, 32, 64)
    sre_s = np.concatenate([r["sre_s"] for r in results]).reshape(1, ncores * NS, 32, 64)
    sim_s = np.concatenate([r["sim_s"] for r in results]).reshape(1, ncores * NS, 32, 64)
    return tuple(np.ascontiguousarray(a, dtype=np.float32) for a in
                 (y_p, y_s, conv_p, conv_s, sre_p, sim_p, sre_s, sim_s))


def kernel(**inputs):
    SEQ = 2048
    n = 8
    nc = build_program(SEQ)
    in_maps = make_in_maps(inputs, SEQ, n)
    res = run_bass_kernel_spmd(nc, in_maps, core_ids=list(range(n)))
    return assemble(res.results, SEQ, n)
```

```python
import math
from contextlib import ExitStack
import numpy as np
import concourse.bass as bass
import concourse.mybir as mybir
from concourse.bass_utils import run_bass_kernel_spmd

F32 = mybir.dt.float32
BF16 = mybir.dt.bfloat16
AF = mybir.ActivationFunctionType
ALU = mybir.AluOpType

D = 1024
NS = 16
CK = 31
EPS = 1e-6
PI = math.pi
TWO_PI = 2.0 * math.pi
MAGIC = 12582912.0
SHR = 1.0 - 2e-6


def V(t, p0, np_, f0, dims):
    F = 1
    for s in t.shape[1:]:
        F *= s
    return bass.AP(t, p0 * F + f0, [[F, np_]] + [list(d) for d in dims])


def DR(t, off, dims):
    return bass.AP(t, off, [list(d) for d in dims])


class Prog:
    ENGS = ["tensor", "vector", "scalar", "gpsimd", "sync"]
    NDS = 8

    def __init__(self, nc, es):
        self.nc = nc
        self.q = {e: [] for e in self.ENGS}
        self.cnt = {e: 0 for e in self.ENGS}
        self.sem = {e: es.enter_context(nc.semaphore("s_" + e)) for e in self.ENGS}
        self.dsem = {}
        self.dcnt = {}
        self.dnext = {}
        for qn in ["sync", "gpsimd", "scalar"]:
            self.dsem[qn] = [es.enter_context(nc.semaphore("d_%s%d" % (qn, i))) for i in range(self.NDS)]
            self.dcnt[qn] = [0] * self.NDS
            self.dnext[qn] = 0
        self.last_w = {}
        self.readers = {}
        self.waited = {e: {} for e in self.ENGS}
        self.all_events = []

    def _deps(self, eng, R, W):
        deps = {}

        def add(ev, war=False):
            if ev is None:
                return
            key, val, src = ev[0], ev[1], ev[2]
            if src == eng and eng == "tensor":
                return
            if deps.get(key, (0, None))[0] < val:
                deps[key] = (val, ev[3])
        for r in R:
            add(self.last_w.get(r))
        for w in W:
            add(self.last_w.get(w))
            for ev in self.readers.get(w, []):
                add(ev, war=True)
        out = []
        for key, (val, semh) in deps.items():
            if self.waited[eng].get(key, 0) >= val:
                continue
            self.waited[eng][key] = val
            out.append((semh, val))
        return out

    def _commit(self, ev, R, W):
        for w in W:
            self.last_w[w] = ev
            self.readers[w] = []
        for r in R:
            self.readers.setdefault(r, []).append(ev)

    def op(self, eng, fn, R=(), W=()):
        waits = self._deps(eng, R, W)
        self.cnt[eng] += 1
        ev = ("e_" + eng, self.cnt[eng], eng, self.sem[eng])
        self.q[eng].append((waits, fn, self.sem[eng], 1))
        self._commit(ev, R, W)
        return ev

    def dma(self, qn, fn, R=(), W=()):
        waits = self._deps(qn, R, W)
        j = self.dnext[qn]
        self.dnext[qn] = (j + 1) % self.NDS
        n = self.dcnt[qn][j]
        key = "d_%s%d" % (qn, j)
        semh = self.dsem[qn][j]
        if n > 0 and self.waited[qn].get(key, 0) < 16 * n:
            self.waited[qn][key] = 16 * n
            waits.append((semh, 16 * n))
        self.dcnt[qn][j] = n + 1
        ev = (key, 16 * (n + 1), "dma_" + qn, semh)
        self.q[qn].append((waits, fn, semh, 16))
        self._commit(ev, R, W)
        self.all_events.append(ev)
        return ev

    def barrier(self, final=False):
        evs = []
        for e in self.ENGS:
            if self.cnt[e] > 0:
                evs.append(("e_" + e, self.cnt[e], e, self.sem[e]))
        for qn in self.dsem:
            for j in range(self.NDS):
                if self.dcnt[qn][j] > 0:
                    evs.append(("d_%s%d" % (qn, j), 16 * self.dcnt[qn][j], "dma_" + qn, self.dsem[qn][j]))
        for e in self.ENGS:
            if e == "tensor" and not final:
                continue
            waits = []
            for (key, val, src, semh) in evs:
                if src == e:
                    continue
                if self.waited[e].get(key, 0) >= val:
                    continue
                self.waited[e][key] = val
                waits.append((semh, val))
            if waits:
                self.q[e].append((waits, None, None, 0))

    def replay(self, block):
        def mk(e):
            def body(eng):
                for (waits, fn, semh, inc) in self.q[e]:
                    for (s, v) in waits:
                        eng.wait_ge(s, v)
                    if fn is not None:
                        fn(eng).then_inc(semh, inc)
            return body
        block.tensor(mk("tensor"))
        block.vector(mk("vector"))
        block.scalar(mk("scalar"))
        block.gpsimd(mk("gpsimd"))
        block.sync(mk("sync"))


class _Stop(Exception):
    pass


def build_program(SEQ, stop=None):
    TT = SEQ + NS
    NTB = SEQ // 512
    tblocks = [(i * 512, 512) for i in range(NTB)] + [(SEQ, NS)]
    ttiles = [(i * 128, 128) for i in range(SEQ // 128)] + [(SEQ, NS)]

    nc = bass.Bass("TRN2", target_bir_lowering=False)
    es = ExitStack()

    def din(name, shape):
        return nc.dram_tensor(name, list(shape), F32, kind="ExternalInput")

    def dout(name, shape):
        return nc.dram_tensor(name, list(shape), F32, kind="ExternalOutput")

    xp = din("xp", [SEQ, D]); xs = din("xs", [NS, D])
    pp = din("pp", [SEQ, 256]); psm = din("psm", [NS, 256])
    cache = din("cache", [NS * 30, D])
    st_re = din("st_re", [NS, 2048]); st_im = din("st_im", [NS, 2048])
    w_in = din("w_in", [D, 6144]); w_pc = din("w_pc", [D, D]); w_ps = din("w_ps", [512, D])
    w_glu = din("w_glu", [512, 512]); w_out = din("w_out", [D, D]); w_pg = din("w_pg", [D, D])
    w_ple = din("w_ple", [256, D])
    rows = din("rows", [4, D])
    colv8 = din("colv8", [128, 3, 8])
    colv4 = din("colv4", [128, 2, 4])
    convw = din("convw", [128, 8, CK])
    PA = din("PA", [128, 3, 256])
    BA = din("BA", [128, 2, 256])
    MA = din("MA", [128, 8])
    PB = din("PB", [128, 3, 16])
    CB = din("CB", [128, 2, 256])
    BB = din("BB", [128, 2, 256])
    MB = din("MB", [128, 4, 8])
    ident_d = din("ident", [128, 128])
    iota_d = din("iota", [128, SEQ])

    y_p = dout("y_p", [SEQ, D]); y_s = dout("y_s", [NS, D])
    conv_p = dout("conv_p", [30, D]); conv_s = dout("conv_s", [NS * 30, D])
    sre_p = dout("sre_p", [32, 64]); sim_p = dout("sim_p", [32, 64])
    sre_s = dout("sre_s", [NS, 2048]); sim_s = dout("sim_s", [NS, 2048])

    P = Prog(nc, es)
    stacks = []

    def ck(n):
        if stop is not None and n == stop:
            raise _Stop()

    def sb(name, free, dt=F32, stack=es):
        return stack.enter_context(nc.sbuf_tensor(name, [128] + list(free), dt))

    PS = [es.enter_context(nc.psum_tensor("ps%d" % i, [128, 512], F32)) for i in range(8)]
    psi = [0]

    def nextps():
        i = psi[0]
        psi[0] = (i + 1) % 8
        return i

    ident_b = sb("ident_b", [128], BF16)
    ident_f = sb("ident_f", [128], F32)
    ones_b = sb("ones_b", [128], BF16)
    cv8 = sb("cv8", [3, 8], F32)
    cv4 = sb("cv4", [2, 4], F32)
    m_t = sb("m_t", [8, TT], BF16)
    stW = ExitStack(); stacks.append(stW)
    NWB = 4
    WB = [sb("wb%d" % i, [8, 512], BF16, stW) for i in range(NWB)]

    def pipeline(stages, n, between=None):
        for t in range(n + len(stages) - 1):
            if between is not None:
                between()
            for si, st_fn in enumerate(stages):
                i = t - si
                if 0 <= i < n:
                    st_fn(i)
    wbi = [0]

    P.dma("sync", lambda e: e.dma_start(out=ident_f[:], in_=ident_d.ap()), W=["ident_f"])
    P.dma("gpsimd", lambda e: e.dma_start(out=ident_b[:], in_=ident_d.ap()), W=["ident_b"])
    P.dma("sync", lambda e: e.dma_start(out=cv8[:], in_=colv8.ap()), W=["cv8"])
    P.dma("sync", lambda e: e.dma_start(out=cv4[:], in_=colv4.ap()), W=["cv4"])
    P.op("gpsimd", lambda e: e.memset(ones_b[:], 1.0), W=["ones_b"])

    def load_w(wd, ncols_total, kc, c0, ncols):
        i = wbi[0]
        wbi[0] = (i + 1) % NWB
        buf = WB[i]
        P.dma("gpsimd", lambda e: e.dma_start(
            out=V(buf, 0, 128, 0, [[512, kc], [1, ncols]]),
            in_=DR(wd, c0, [[ncols_total, 128], [128 * ncols_total, kc], [1, ncols]])),
            W=[("wb", i)])
        return i

    pending = {}

    def prefetch(key, *args):
        pending[key] = load_w(*args)

    def getw(key, *args):
        if key in pending:
            return pending.pop(key)
        return load_w(*args)

    def mm_fm(wi, kc, col_in_buf, act, act_key, cs, cn, pbank):
        buf = WB[wi]
        actF = act.shape[2]
        for k in range(kc):
            P.op("tensor", lambda e, k=k: e.matmul(
                V(PS[pbank], 0, 128, 0, [[1, cn]]),
                V(buf, 0, 128, k * 512 + col_in_buf, [[1, 128]]),
                V(act, 0, 128, k * actF + cs, [[1, cn]]),
                start=(k == 0), stop=(k == kc - 1)),
                R=[("wb", wi), act_key], W=[("ps", pbank)])

    try:
        stH = ExitStack(); stacks.append(stH)
        hT = sb("hT", [8, TT], BF16, stH)
        stC = ExitStack(); stacks.append(stC)
        sx = sb("sx", [4, TT], BF16, stC)
        wi = load_w(w_in, 6144, 8, 3072, 512)
        TC = SEQ // 8
        stS = ExitStack(); stacks.append(stS)
        PBt = sb("PBt", [3, 16], F32, stS)
        tB = [sb("tB%d" % i, [16], F32, stS) for i in range(13)]
        LpB = sb("LpB", [2, 9, 16], F32, stS)
        Gc = sb("Gc", [8, 2, 256], F32, stS)
        Hc = sb("Hc", [2, 9, 256], BF16, stS)
        BbB = sb("BbB", [2, 256], BF16, stS)
        mA = sb("mA", [8], F32, stS)
        mB = sb("mB", [4, 8], BF16, stS)
        fin = sb("fin", [2, 16], F32, stS)
        sS = sb("sS", [2, 256], F32, stS)
        sN = sb("sN", [2, 256], F32, stS)
        sNb = sb("sNb", [2, 256], BF16, stS)
        sT1 = sb("sT1", [256], F32, stS)
        sT2 = sb("sT2", [256], F32, stS)
        P.dma("sync", lambda e: e.dma_start(out=PBt[:], in_=PB.ap()), W=["PBt"])
        P.dma("sync", lambda e: e.dma_start(out=mA[:], in_=MA.ap()), W=["mA"])
        P.dma("gpsimd", lambda e: e.dma_start(out=mB[:], in_=MB.ap()), W=["mB"])
        stP = ExitStack(); stacks.append(stP)
        PAt = sb("PAt", [3, 256], F32, stP)
        BAt = sb("BAt", [2, 256], F32, stP)
        CBt = sb("CBt", [2, 256], F32, stP)
        BBt = sb("BBt", [2, 256], F32, stP)
        tA = [sb("tA%d" % i, [256], F32, stP) for i in range(9)]
        big0 = sb("big0", [9 * 256], F32, stP)
        stage = big0
        big1 = sb("big1", [9 * 256], F32, stP)
        P.dma("sync", lambda e: e.dma_start(out=PAt[:], in_=PA.ap()), W=["PAt"])
        P.dma("sync", lambda e: e.dma_start(out=BAt[:], in_=BA.ap()), W=["BAt"])
        P.dma("sync", lambda e: e.dma_start(out=CBt[:], in_=CB.ap()), W=["CBt"])
        P.dma("sync", lambda e: e.dma_start(out=BBt[:], in_=BB.ap()), W=["BBt"])

        Gkeys = rho8 = tau8 = None

        def prep_gen():
            nonlocal Gkeys, rho8, tau8
            def lam_prep(par, n, T, pk, pre):
                a_re = V(par, 0, 128, 0, [[1, n]]); a_im = V(par, 0, 128, n, [[1, n]]); ldt = V(par, 0, 128, 2 * n, [[1, n]])
                dt_, mag, th, cs_, sn_, tmp = T[0], T[1], T[2], T[3], T[4], T[5]
                P.op("scalar", lambda e: e.activation(out=dt_[:], in_=ldt, func=AF.Exp), R=[pk], W=[pre + "dt"])
                P.op("vector", lambda e: e.tensor_tensor(mag[:], a_re, dt_[:], ALU.mult), R=[pk, pre + "dt"], W=[pre + "mag"])
                P.op("scalar", lambda e: e.activation(out=mag[:], in_=mag[:], func=AF.Exp), R=[pre + "mag"], W=[pre + "mag"])
                P.op("vector", lambda e: e.scalar_tensor_tensor(th[:], a_im, 1.0 / TWO_PI, dt_[:], ALU.mult, ALU.mult), R=[pk, pre + "dt"], W=[pre + "th"])
                P.op("vector", lambda e: e.tensor_scalar(tmp[:], th[:], MAGIC, -MAGIC, ALU.add, ALU.add), R=[pre + "th"], W=[pre + "tmp"])
                P.op("vector", lambda e: e.tensor_tensor(th[:], th[:], tmp[:], ALU.subtract), R=[pre + "th", pre + "tmp"], W=[pre + "th"])
                P.op("scalar", lambda e: e.activation(out=sn_[:], in_=th[:], func=AF.Sin, scale=TWO_PI * SHR), R=[pre + "th"], W=[pre + "sin"])
                P.op("scalar", lambda e: e.activation(out=tmp[:], in_=th[:], func=AF.Sin, scale=PI * SHR), R=[pre + "th"], W=[pre + "tmp"])
                P.op("scalar", lambda e: e.activation(out=tmp[:], in_=tmp[:], func=AF.Square, scale=math.sqrt(2.0)), R=[pre + "tmp"], W=[pre + "tmp"])
                P.op("scalar", lambda e: e.activation(out=cs_[:], in_=tmp[:], func=AF.Identity, scale=-1.0, bias=1.0), R=[pre + "tmp"], W=[pre + "cos"])
                return mag, th, cs_, sn_

            def f_prep(par, n, T, mag, cs_, sn_, pk, pre):
                a_re = V(par, 0, 128, 0, [[1, n]]); a_im = V(par, 0, 128, n, [[1, n]])
                lr, li, den, t7, nr = T[0], T[5], T[6], T[7], T[8]
                P.op("vector", lambda e: e.tensor_tensor(lr[:], mag[:], cs_[:], ALU.mult), R=[pre + "mag", pre + "cos", pre + "dt"], W=[pre + "dt"])
                P.op("vector", lambda e: e.tensor_tensor(li[:], mag[:], sn_[:], ALU.mult), R=[pre + "mag", pre + "sin"], W=[pre + "tmp"])
                P.op("vector", lambda e: e.tensor_scalar(nr[:], lr[:], -1.0, None, ALU.add), R=[pre + "dt"], W=[pre + "nr"])
                P.op("vector", lambda e: e.tensor_tensor(den[:], a_re, a_re, ALU.mult), R=[pk], W=[pre + "den"])
                P.op("vector", lambda e: e.tensor_tensor(t7[:], a_im, a_im, ALU.mult), R=[pk], W=[pre + "t7"])
                P.op("vector", lambda e: e.tensor_tensor(den[:], den[:], t7[:], ALU.add), R=[pre + "den", pre + "t7"], W=[pre + "den"])
                P.op("vector", lambda e: e.reciprocal(den[:], den[:]), R=[pre + "den"], W=[pre + "den"])
                fr, fi = T[3], T[4]
                P.op("vector", lambda e: e.tensor_tensor(fr[:], nr[:], a_re, ALU.mult), R=[pre + "nr", pk], W=[pre + "cos"])
                P.op("vector", lambda e: e.tensor_tensor(t7[:], li[:], a_im, ALU.mult), R=[pre + "tmp", pk, pre + "den"], W=[pre + "t7"])
                P.op("vector", lambda e: e.tensor_tensor(fr[:], fr[:], t7[:], ALU.add), R=[pre + "cos", pre + "t7"], W=[pre + "cos"])
                P.op("vector", lambda e: e.tensor_tensor(fr[:], fr[:], den[:], ALU.mult), R=[pre + "cos", pre + "den"], W=[pre + "cos"])
                P.op("vector", lambda e: e.tensor_tensor(fi[:], li[:], a_re, ALU.mult), R=[pre + "tmp", pk], W=[pre + "sin"])
                P.op("vector", lambda e: e.tensor_tensor(t7[:], nr[:], a_im, ALU.mult), R=[pre + "nr", pk, pre + "cos"], W=[pre + "t7"])
                P.op("vector", lambda e: e.tensor_tensor(fi[:], fi[:], t7[:], ALU.subtract), R=[pre + "sin", pre + "t7"], W=[pre + "sin"])
                P.op("vector", lambda e: e.tensor_tensor(fi[:], fi[:], den[:], ALU.mult), R=[pre + "sin", pre + "den"], W=[pre + "sin"])
                return lr, li, fr, fi

            def cmul(o_re, o_im, a_re, a_im, b_re, b_im, t0, t1, R, Wre, Wim, neg_im=False):
                P.op("vector", lambda e: e.tensor_tensor(t0, a_re, b_re, ALU.mult), R=R, W=["cm_t0"])
                P.op("vector", lambda e: e.tensor_tensor(t1, a_im, b_im, ALU.mult), R=R, W=["cm_t1"])
                P.op("vector", lambda e: e.tensor_tensor(o_re, t0, t1, ALU.subtract), R=["cm_t0", "cm_t1"], W=Wre)
                P.op("vector", lambda e: e.tensor_tensor(t0, a_re, b_im, ALU.mult), R=R + Wre, W=["cm_t0"])
                P.op("vector", lambda e: e.tensor_tensor(t1, a_im, b_re, ALU.mult), R=R + Wre, W=["cm_t1"])
                if neg_im:
                    P.op("vector", lambda e: e.scalar_tensor_tensor(o_im, t0, -1.0, t1, ALU.mult, ALU.subtract), R=["cm_t0", "cm_t1"], W=Wim)
                else:
                    P.op("vector", lambda e: e.tensor_tensor(o_im, t0, t1, ALU.add), R=["cm_t0", "cm_t1"], W=Wim)

            magA, tauA, cosA, sinA = lam_prep(PAt, 256, tA, "PAt", "A_")
            yield
            lrA, liA, frA, fiA = f_prep(PAt, 256, tA, magA, cosA, sinA, "PAt", "A_")
            yield

            def gk(k, comp):
                return V(Gc, 0, 128, (k * 2 + comp) * 256, [[1, 256]])
            b0 = V(big0, 0, 128, 0, [[1, 256]]); b1 = V(big1, 0, 128, 0, [[1, 256]])
            cmul(gk(0, 0), gk(0, 1), frA[:], fiA[:], V(BAt, 0, 128, 0, [[1, 256]]), V(BAt, 0, 128, 256, [[1, 256]]), b0, b1,
                 ["A_cos", "A_sin", "BAt"], [("Gc", 0, 0)], [("Gc", 0, 1)])
            for k in range(1, 8):
                cmul(gk(k, 0), gk(k, 1), gk(k - 1, 0), gk(k - 1, 1), lrA[:], liA[:], b0, b1,
                     [("Gc", k - 1, 0), ("Gc", k - 1, 1), "A_dt", "A_tmp"], [("Gc", k, 0)], [("Gc", k, 1)])
                yield
            Gkeys = [("Gc", k, c) for k in range(8) for c in range(2)]
            magB, tauB, cosB, sinB = lam_prep(PBt, 16, tB, "PBt", "B_")
            yield
            lrB, liB, frB, fiB = f_prep(PBt, 16, tB, magB, cosB, sinB, "PBt", "B_")
            yield

            def lp(k, comp):
                return V(LpB, 0, 128, (comp * 9 + k) * 16, [[1, 16]])
            P.op("vector", lambda e: e.memset(lp(0, 0), 1.0), W=[("LpB", 0)])
            P.op("vector", lambda e: e.memset(lp(0, 1), 0.0), R=[("LpB", 0)], W=[("LpB", 0)])
            P.op("vector", lambda e: e.tensor_copy(lp(1, 0), lrB[:]), R=["B_dt"], W=[("LpB", 1)])
            P.op("vector", lambda e: e.tensor_copy(lp(1, 1), liB[:]), R=["B_tmp", ("LpB", 1)], W=[("LpB", 1)])
            tb0 = tB[9][:]; tb1 = tB[10][:]
            for k in range(2, 9):
                cmul(lp(k, 0), lp(k, 1), lp(k - 1, 0), lp(k - 1, 1), lrB[:], liB[:], tb0, tb1,
                     [("LpB", k - 1), "B_dt", "B_tmp"], [("LpB", k)], [("LpB", k)])
                yield
            LpKeys = [("LpB", k) for k in range(9)]
            rho8 = tB[11]; tau8 = tB[12]
            P.op("vector", lambda e: e.tensor_tensor(rho8[:], magB[:], magB[:], ALU.mult), R=["B_mag"], W=["rho8"])
            P.op("vector", lambda e: e.tensor_tensor(rho8[:], rho8[:], rho8[:], ALU.mult), R=["rho8"], W=["rho8"])
            P.op("vector", lambda e: e.tensor_tensor(rho8[:], rho8[:], rho8[:], ALU.mult), R=["rho8"], W=["rho8"])
            P.op("vector", lambda e: e.tensor_scalar(tau8[:], tauB[:], 8.0, None, ALU.mult), R=["B_th"], W=["tau8"])
            P.op("vector", lambda e: e.tensor_scalar(tb0, tau8[:], MAGIC, -MAGIC, ALU.add, ALU.add), R=["tau8"] + LpKeys, W=["cm_t0"])
            P.op("vector", lambda e: e.tensor_tensor(tau8[:], tau8[:], tb0, ALU.subtract), R=["tau8", "cm_t0"], W=["tau8"])
            def bc16(t):
                return V(t, 0, 128, 0, [[1, 16], [0, 16]])

            def q16(t, comp):
                return V(t, 0, 128, comp * 256, [[16, 16], [1, 16]])
            g0 = V(big0, 0, 128, 0, [[16, 16], [1, 16]]); g1 = V(big1, 0, 128, 0, [[16, 16], [1, 16]])
            cmul(q16(BbB, 0), q16(BbB, 1), bc16(frB), bc16(fiB), q16(BBt, 0), q16(BBt, 1), g0, g1,
                 ["B_cos", "B_sin", "BBt"], ["BbB0"], ["BbB1"])
            def lpb(comp):
                return V(LpB, 0, 128, comp * 144, [[16, 9], [1, 16], [0, 16]])

            def cbb(comp):
                return V(CBt, 0, 128, comp * 256, [[0, 9], [16, 16], [1, 16]])

            def hcv(comp):
                return V(Hc, 0, 128, comp * 2304, [[256, 9], [16, 16], [1, 16]])
            h0 = V(big0, 0, 128, 0, [[256, 9], [16, 16], [1, 16]]); h1 = V(big1, 0, 128, 0, [[256, 9], [16, 16], [1, 16]])
            cmul(hcv(0), hcv(1), cbb(0), cbb(1), lpb(0), lpb(1), h0, h1, ["CBt"] + LpKeys, ["Hc0"], ["Hc1"], neg_im=True)
            yield

            for comp, sd in ((0, st_re), (1, st_im)):
                P.dma("sync", lambda e, sd=sd: e.dma_start(out=V(stage, 0, NS, 0, [[1, 2048]]), in_=sd.ap()), W=["cm_t0"])
                pb = nextps()
                for qg in range(16):
                    P.op("tensor", lambda e, qg=qg, pb=pb: e.matmul(
                        V(PS[pb], 0, 128, qg * NS, [[1, NS]]),
                        V(stage, 0, NS, qg * 128, [[1, 128]]),
                        V(ident_f, 0, NS, 0, [[1, NS]]), start=True, stop=True),
                        R=["cm_t0", "ident_f"], W=[("ps", pb)])
                P.op("vector", lambda e, comp=comp, pb=pb: e.tensor_copy(
                    V(sS, 0, 128, comp * 256, [[1, 256]]), V(PS[pb], 0, 128, 0, [[1, 256]])),
                    R=[("ps", pb)], W=["sS%d" % comp])

            def bcB(t):
                return V(t, 0, 128, 0, [[1, 16], [0, NS]])

            def s3(t, comp):
                return V(t, 0, 128, comp * 256, [[NS, 16], [1, NS]])

            def s3t(t):
                return V(t, 0, 128, 0, [[NS, 16], [1, NS]])
            cmul(s3(sN, 0), s3(sN, 1), s3(sS, 0), s3(sS, 1), bcB(lrB), bcB(liB), s3t(sT1), s3t(sT2),
                 ["sS0", "sS1", "B_dt", "B_tmp"], ["sN0"], ["sN1"])

        pgen = prep_gen()
        stA = ExitStack(); stacks.append(stA)
        xt = [sb("xt%d" % i, [D], F32, stA) for i in range(3)]
        hb = [sb("hb%d" % i, [D], BF16, stA) for i in range(2)]
        junk = sb("junk", [D], BF16, stA)
        ssA = sb("ssA", [40], F32, stA)
        gpre = sb("gpre", [D], F32, stA)
        P.dma("sync", lambda e: e.dma_start(out=gpre[:], in_=DR(rows, 0, [[0, 128], [1, D]])), W=["gpre"])

        def sA1(ti):
            r0, nt = ttiles[ti]; b = ti % 3
            src = DR(xp, r0 * D, [[D, nt], [1, D]]) if r0 < SEQ else DR(xs, 0, [[D, nt], [1, D]])
            P.dma("sync", lambda e: e.dma_start(out=V(xt[b], 0, nt, 0, [[1, D]]), in_=src), W=[("xt", b)])
            P.op("scalar", lambda e: e.activation(
                out=V(junk, 0, nt, 0, [[1, D]]), in_=V(xt[b], 0, nt, 0, [[1, D]]), func=AF.Square,
                accum_out=V(ssA, 0, nt, ti, [[1, 1]])), R=[("xt", b)], W=["junk", ("ssA", ti)])
            P.op("scalar", lambda e: e.activation(
                out=V(ssA, 0, nt, ti, [[1, 1]]), in_=V(ssA, 0, nt, ti, [[1, 1]]), func=AF.Ln, scale=1.0 / D, bias=EPS),
                R=[("ssA", ti)], W=[("ssA", ti)])

        def sA2(ti):
            r0, nt = ttiles[ti]; b = ti % 3; bh = ti % 2
            P.op("scalar", lambda e: e.activation(
                out=V(ssA, 0, nt, ti, [[1, 1]]), in_=V(ssA, 0, nt, ti, [[1, 1]]), func=AF.Exp, scale=-0.5),
                R=[("ssA", ti)], W=[("ssA", ti)])
            P.op("vector", lambda e: e.scalar_tensor_tensor(
                V(hb[bh], 0, nt, 0, [[1, D]]), V(xt[b], 0, nt, 0, [[1, D]]), V(ssA, 0, nt, ti, [[1, 1]]),
                V(gpre, 0, nt, 0, [[1, D]]), ALU.mult, ALU.mult),
                R=[("xt", b), ("ssA", ti), "gpre"], W=[("hb", bh)])

        def sA3(ti):
            r0, nt = ttiles[ti]; b = ti % 2
            for half in range(2):
                pb = nextps()
                for kk in range(4):
                    k = half * 4 + kk
                    P.op("tensor", lambda e, k=k, kk=kk, pb=pb: e.matmul(
                        V(PS[pb], 0, 128, kk * nt, [[1, nt]]),
                        V(hb[b], 0, nt, k * 128, [[1, 128]]),
                        V(ident_b, 0, nt, 0, [[1, nt]]), start=True, stop=True),
                        R=[("hb", b), "ident_b"], W=[("ps", pb)])
                P.op("scalar", lambda e, half=half, pb=pb: e.activation(
                    out=V(hT, 0, 128, half * 4 * TT + r0, [[TT, 4], [1, nt]]),
                    in_=V(PS[pb], 0, 128, 0, [[nt, 4], [1, nt]]), func=AF.Copy),
                    R=[("ps", pb)], W=[("hT", half, ti)])
        pipeline([sA1, sA2, sA3], len(ttiles), between=lambda: next(pgen, None))
        for _ in pgen:
            pass
        ck(1)

        for (cs, cn) in tblocks:
            tis = [ti for ti, (r0, nt) in enumerate(ttiles) if cs <= r0 < cs + cn]
            hkeys = [("hT", h, ti) for h in range(2) for ti in tis]
            for oc in range(4):
                pb = nextps()
                for k in range(8):
                    mov = V(hT, 0, 128, k * TT + cs, [[1, cn]])
                    P.op("tensor", lambda e, k=k, oc=oc, cn=cn, pb=pb, mov=mov: e.matmul(
                        V(PS[pb], 0, 128, 0, [[1, cn]]),
                        V(WB[wi], 0, 128, k * 512 + oc * 128, [[1, 128]]),
                        mov, start=(k == 0), stop=(k == 7)),
                        R=[("wb", wi)] + hkeys, W=[("ps", pb)])
                dst = (V(sx, 0, 128, oc * TT + cs // 8, [[SEQ // 8, 8], [1, cn // 8]]) if cs < SEQ
                       else V(sx, 0, 128, oc * TT + cs, [[1, cn]]))
                src_ = (V(PS[pb], 0, 128, 0, [[1, 8], [8, cn // 8]]) if cs < SEQ else V(PS[pb], 0, 128, 0, [[1, cn]]))
                P.op("scalar", lambda e, dst=dst, src_=src_: e.activation(out=dst, in_=src_, func=AF.Copy),
                     R=[("ps", pb)], W=[("sx", oc) if cs < SEQ else ("sxs", oc)])
        ck(2)
        ck(3)
        P.barrier()
        stA.close()
        stP.close()

        stL = ExitStack(); stacks.append(stL)
        SLi = sb("SLi", [8, 2, 512], BF16, stL)
        YSk = sb("YSk", [4, 2, 9, 128], BF16, stL)
        BBs = sb("BBs", [4, 2, 128], BF16, stL)
        BDk = sb("BDk", [8, 128], BF16, stL)
        Spv = sb("Spv", [2, 4, 2, TC], BF16, stL)
        iot = sb("iot", [TC], F32, stL)
        tcs = [sb("tcos%d" % i, [TC], F32, stL) for i in range(2)]
        tsn = [sb("tsin%d" % i, [TC], F32, stL) for i in range(2)]
        Tg = [sb("Tg%d" % i, [TC], F32, stL) for i in range(2)]
        Wk = [sb("Wk%d" % i, [TC], F32, stL) for i in range(5)]
        Pk = [Wk[0], Wk[1]]
        ytmp = sb("ytmp", [NS], F32, stL)
        P.dma("sync", lambda e: e.dma_start(out=iot[:], in_=DR(iota_d, 0, [[SEQ, 128], [1, TC]])), W=["iot"])
        P.op("gpsimd", lambda e: e.memset(V(Spv, 0, 128, 0, [[TC, 16], [1, 1]]), 0.0), W=["Spv_z"])
        P.op("gpsimd", lambda e: e.memset(YSk[:], 0.0), W=["YSk_z"])
        P.op("gpsimd", lambda e: e.memset(BBs[:], 0.0), W=["BBs_z"])
        ysg = sx
        YB = [0, 1, 2, 3]; SLB = [4, 5]; BUS = 6; YSB = 7
        slit = [0]
        def gen_tables(qg):
            sl = qg % 2
            t8col = V(tau8, 0, 128, qg, [[1, 1]])
            P.op("vector", lambda e: e.tensor_scalar(Tg[1][:], iot[:], t8col, None, ALU.mult), R=["iot", "tau8"], W=["tg1"])
            P.op("vector", lambda e: e.tensor_scalar(Tg[0][:], Tg[1][:], MAGIC, -MAGIC, ALU.add, ALU.add), R=["tg1"], W=["tg0"])
            P.op("vector", lambda e: e.tensor_tensor(Tg[1][:], Tg[1][:], Tg[0][:], ALU.subtract), R=["tg1", "tg0"], W=["tg1"])
            P.op("scalar", lambda e: e.activation(out=tsn[sl][:], in_=Tg[1][:], func=AF.Sin, scale=TWO_PI * SHR), R=["tg1"], W=[("tsin", sl)])
            P.op("scalar", lambda e: e.activation(out=Tg[0][:], in_=Tg[1][:], func=AF.Sin, scale=PI * SHR), R=["tg1"], W=["tg0"])
            P.op("scalar", lambda e: e.activation(out=Tg[0][:], in_=Tg[0][:], func=AF.Square, scale=math.sqrt(2.0)), R=["tg0"], W=["tg0"])
            P.op("scalar", lambda e: e.activation(out=tcs[sl][:], in_=Tg[0][:], func=AF.Identity, scale=-1.0, bias=1.0), R=["tg0"], W=[("tcos", sl)])

        def expand_sli(blk):
            for g2 in range(8):
                P.op("scalar", lambda e, g2=g2: e.activation(
                    out=V(SLi, 0, 128, g2 * 64, [[512, 16], [1, 64]]),
                    in_=V(Gc, 0, 128, blk * 64, [[256, 16], [1, 64]]), func=AF.Identity,
                    scale=V(mA, 0, 128, g2, [[1, 1]])),
                    R=Gkeys + ["mA"], W=[("SLi", k) for k in range(8)])

        slb_of = {}

        def s_local(blk, q):
            pbs = SLB[slit[0] % 2]; slit[0] += 1
            slb_of[(blk, q)] = pbs
            for comp in range(2):
                for i in range(8):
                    P.op("tensor", lambda e, comp=comp, i=i: e.matmul(
                        V(PS[pbs], 0, 128, comp * TC, [[1, TC]]),
                        V(SLi, 0, 128, ((7 - i) * 2 + comp) * 512 + q * 128, [[1, 128]]),
                        V(sx, 0, 128, blk * TT + i * TC, [[1, TC]]), start=(i == 0), stop=(i == 7)),
                        R=[("SLi", 7 - i), ("sx", blk)], W=[("ps", pbs)])

        def sample_bu(blk):
            for q in range(4):
                for comp in range(2):
                    P.op("tensor", lambda e, comp=comp, q=q: e.matmul(
                        V(PS[BUS], 0, 128, (blk % 2) * 128 + (q * 2 + comp) * NS, [[1, NS]]),
                        V(SLi, 0, 128, (0 * 2 + comp) * 512 + q * 128, [[1, 128]]),
                        V(sx, 0, 128, blk * TT + SEQ, [[1, NS]]), start=True, stop=True),
                        R=[("SLi", 0), ("sxs", blk)], W=[("ps", BUS)])

        expand_sli(0)
        sample_bu(0)
        s_local(0, 0)
        for blk in range(4):
            for q in range(4):
                qg = blk * 4 + q
                for comp in range(2):
                    for par in range(2):
                        P.op("gpsimd", lambda e, q=q, qg=qg, comp=comp, par=par: e.tensor_copy(
                            V(YSk, 64 * par, 64, (q * 2 + comp) * 9 * 128 + (2 * q + par) * 16, [[128, 9], [1, 16]]),
                            V(Hc, 64 * par, 64, comp * 2304 + qg * 16, [[256, 9], [1, 16]])),
                            R=["Hc%d" % comp, "YSk_z"], W=["YSk"])
                        P.op("gpsimd", lambda e, q=q, qg=qg, comp=comp, par=par: e.tensor_copy(
                            V(BBs, 64 * par, 64, (q * 2 + comp) * 128 + (2 * q + par) * 16, [[1, 16]]),
                            V(BbB, 64 * par, 64, comp * 256 + qg * 16, [[1, 16]])),
                            R=["BbB%d" % comp, "BBs_z"], W=["BBs"])
            ck(4 if blk == 0 else -1)
            dcol = V(cv4, 0, 128, 4 + blk, [[1, 1]])
            ck(5 if blk == 0 else -1)
            for q in range(4):
                qg = blk * 4 + q
                pbs = slb_of[(blk, q)]
                if q < 3:
                    s_local(blk, q + 1)
                elif blk < 3:
                    expand_sli(blk + 1)
                    sample_bu(blk + 1)
                    s_local(blk + 1, 0)
                if qg == 0:
                    gen_tables(0)
                if qg + 1 < 16:
                    gen_tables(qg + 1)
                sl = qg % 2
                tcos = tcs[sl]; tsin = tsn[sl]
                kc_ = ("tcos", sl); ks_ = ("tsin", sl)
                br = V(PS[pbs], 0, 128, 0, [[1, TC]]); bi = V(PS[pbs], 0, 128, TC, [[1, TC]])
                kp = ("ps", pbs)
                P.op("vector", lambda e, br=br, tcos=tcos, tsin=tsin: e.tensor_tensor(Wk[0][:], tcos[:], br, ALU.mult), R=[kc_, kp], W=["w0"])
                P.op("vector", lambda e, bi=bi, tcos=tcos, tsin=tsin: e.tensor_tensor(Wk[1][:], tsin[:], bi, ALU.mult), R=[ks_, kp], W=["w1"])
                P.op("vector", lambda e: e.tensor_tensor(Wk[0][:], Wk[0][:], Wk[1][:], ALU.add), R=["w0", "w1"], W=["w0"])
                P.op("vector", lambda e, bi=bi, tcos=tcos, tsin=tsin: e.tensor_tensor(Wk[2][:], tcos[:], bi, ALU.mult), R=[kc_, kp], W=["w2"])
                P.op("vector", lambda e, br=br, tcos=tcos, tsin=tsin: e.tensor_tensor(Wk[1][:], tsin[:], br, ALU.mult), R=[ks_, kp], W=["w1"])
                P.op("vector", lambda e: e.tensor_tensor(Wk[2][:], Wk[2][:], Wk[1][:], ALU.subtract), R=["w2", "w1"], W=["w2"])
                rbc = V(rho8, 0, 128, qg, [[0, TC]])
                P.op("vector", lambda e, rbc=rbc: e.tensor_tensor_scan(Wk[3][:], rbc, Wk[0][:], 0.0, ALU.mult, ALU.add), R=["w0", "rho8"], W=["wk3"])
                P.op("vector", lambda e, rbc=rbc: e.tensor_tensor_scan(Wk[4][:], rbc, Wk[2][:], 0.0, ALU.mult, ALU.add), R=["w2", "rho8"], W=["wk4"])
                P.op("vector", lambda e, tcos=tcos: e.tensor_tensor(Wk[0][:], tcos[:], Wk[3][:], ALU.mult), R=[kc_, "wk3"], W=["w0"])
                P.op("vector", lambda e, tsin=tsin: e.tensor_tensor(Wk[1][:], tsin[:], Wk[4][:], ALU.mult), R=[ks_, "wk4"], W=["w1"])
                P.op("vector", lambda e, q=q, blk=blk: e.tensor_tensor(V(Spv, 0, 128, ((blk % 2) * 8 + q * 2 + 0) * TC + 1, [[1, TC - 1]]),
                                                              V(Wk[0], 0, 128, 0, [[1, TC - 1]]), V(Wk[1], 0, 128, 0, [[1, TC - 1]]), ALU.subtract),
                     R=["w0", "w1", "Spv_z"], W=[("Spv", blk % 2, q, 0)])
                P.op("vector", lambda e, qg=qg: e.tensor_tensor(V(fin, 0, 128, qg, [[1, 1]]), V(Wk[0], 0, 128, TC - 1, [[1, 1]]),
                                                                V(Wk[1], 0, 128, TC - 1, [[1, 1]]), ALU.subtract),
                     R=["w0", "w1"], W=[("fin", 0, qg)])
                P.op("vector", lambda e, tsin=tsin: e.tensor_tensor(Pk[0][:], tsin[:], Wk[3][:], ALU.mult), R=[ks_, "wk3"], W=["w0"])
                P.op("vector", lambda e, tcos=tcos: e.tensor_tensor(Pk[1][:], tcos[:], Wk[4][:], ALU.mult), R=[kc_, "wk4"], W=["w1"])
                P.op("vector", lambda e, q=q, blk=blk: e.tensor_tensor(V(Spv, 0, 128, ((blk % 2) * 8 + q * 2 + 1) * TC + 1, [[1, TC - 1]]),
                                                              V(Pk[0], 0, 128, 0, [[1, TC - 1]]), V(Pk[1], 0, 128, 0, [[1, TC - 1]]), ALU.add),
                     R=["w0", "w1", "Spv_z"], W=[("Spv", blk % 2, q, 1)])
                P.op("vector", lambda e, qg=qg: e.tensor_tensor(V(fin, 0, 128, 16 + qg, [[1, 1]]), V(Pk[0], 0, 128, TC - 1, [[1, 1]]),
                                                                V(Pk[1], 0, 128, TC - 1, [[1, 1]]), ALU.add),
                     R=["w0", "w1"], W=[("fin", 1, qg)])
                ck(6 if (blk == 0 and q == 0) else -1)
                for comp in range(2):
                    P.op("vector", lambda e, comp=comp, qg=qg, q=q, blk=blk: e.tensor_tensor(
                        V(sN, 0, 128, comp * 256 + qg * NS, [[1, NS]]), V(sN, 0, 128, comp * 256 + qg * NS, [[1, NS]]),
                        V(PS[BUS], 0, 128, (blk % 2) * 128 + (q * 2 + comp) * NS, [[1, NS]]), ALU.add),
                        R=["sN%d" % comp, ("ps", BUS)], W=["sN%d" % comp])
                    P.op("vector", lambda e, comp=comp, qg=qg: e.tensor_copy(
                        V(sNb, 0, 128, comp * 256 + qg * NS, [[1, NS]]), V(sN, 0, 128, comp * 256 + qg * NS, [[1, NS]])),
                        R=["sN%d" % comp], W=["sNb%d" % comp])
                for comp in range(2):
                    P.op("tensor", lambda e, comp=comp, qg=qg, q=q: e.matmul(
                        V(PS[YSB], 0, 128, 0, [[1, NS]]),
                        V(YSk, 0, 128, ((q * 2 + comp) * 9 + 0) * 128, [[1, 128]]),
                        V(sNb, 0, 128, comp * 256 + qg * NS, [[1, NS]]),
                        start=(q == 0 and comp == 0), stop=(q == 3 and comp == 1)),
                        R=["YSk", "sNb%d" % comp], W=[("ps", YSB)])
            for hb_ in range(2):
                ck(43 if (blk == 0 and hb_ == 1) else -1)
                pbk = YB[hb_]
                for k4 in range(4):
                    k = hb_ * 4 + k4
                    for q in range(4):
                        for comp in range(2):
                            P.op("tensor", lambda e, k=k, k4=k4, q=q, comp=comp, pbk=pbk: e.matmul(
                                V(PS[pbk], 0, 128, k4 * 128, [[1, 128]]),
                                V(BBs, 0, 128, (q * 2 + comp) * 128, [[1, 128]]),
                                V(YSk, 0, 128, ((q * 2 + comp) * 9 + k) * 128, [[1, 128]]),
                                start=(q == 0 and comp == 0), stop=(q == 3 and comp == 1)),
                                R=["BBs", "YSk"], W=[("ps", pbk)])
                ck(41 if (blk == 0 and hb_ == 0) else -1)
                if hb_ == 0:
                    P.op("vector", lambda e, pbk=pbk, dcol=dcol: e.scalar_tensor_tensor(
                        V(BDk, 0, 128, 0, [[1, 128]]), ident_f[:], dcol, V(PS[pbk], 0, 128, 0, [[1, 128]]), ALU.mult, ALU.add),
                        R=[("ps", pbk), "ident_f", "cv4"], W=["BDk0"])
                    ck(42 if blk == 0 else -1)
                    P.op("vector", lambda e, pbk=pbk: e.tensor_copy(
                        V(BDk, 0, 128, 128, [[1, 384]]), V(PS[pbk], 0, 128, 128, [[1, 384]])),
                        R=[("ps", pbk)], W=["BDk1"])
                else:
                    ck(44 if blk == 0 else -1)
                    P.op("vector", lambda e, pbk=pbk: e.tensor_copy(
                        V(BDk, 0, 128, 512, [[1, 512]]), V(PS[pbk], 0, 128, 0, [[1, 512]])),
                        R=[("ps", pbk)], W=["BDk2"])
            ck(7 if blk == 0 else -1)
            spk = [("Spv", blk % 2, q, comp) for q in range(4) for comp in range(2)]
            for j in range(8):
                yb = YB[j // 2]
                nmm = (j + 1) + 8
                cnt = 0
                for i in range(j + 1):
                    P.op("tensor", lambda e, i=i, j=j, yb=yb, blk=blk, cnt=cnt, nmm=nmm: e.matmul(
                        V(PS[yb], 0, 128, (j % 2) * TC, [[1, TC]]),
                        V(BDk, 0, 128, (j - i) * 128, [[1, 128]]),
                        V(sx, 0, 128, blk * TT + i * TC, [[1, TC]]), start=(cnt == 0), stop=(cnt == nmm - 1)),
                        R=["BDk0", "BDk1", "BDk2", ("sx", blk)], W=[("ps", yb)])
                    cnt += 1
                for q in range(4):
                    for comp in range(2):
                        P.op("tensor", lambda e, q=q, comp=comp, j=j, yb=yb, cnt=cnt, nmm=nmm, blk=blk: e.matmul(
                            V(PS[yb], 0, 128, (j % 2) * TC, [[1, TC]]),
                            V(YSk, 0, 128, ((q * 2 + comp) * 9 + j + 1) * 128, [[1, 128]]),
                            V(Spv, 0, 128, ((blk % 2) * 8 + q * 2 + comp) * TC, [[1, TC]]), start=(cnt == 0), stop=(cnt == nmm - 1)),
                            R=["YSk"] + spk, W=[("ps", yb)])
                        cnt += 1
            ck(8 if blk == 0 else -1)
            for b4 in range(4):
                yb = YB[b4]
                P.op("scalar", lambda e, b4=b4, yb=yb, blk=blk: e.activation(
                    out=V(ysg, 0, 128, blk * TT + 2 * b4, [[1, 2], [8, TC]]), in_=V(PS[yb], 0, 128, 0, [[TC, 2], [1, TC]]), func=AF.Gelu),
                    R=[("ps", yb)], W=[("sx", blk)])
            P.op("vector", lambda e, blk=blk, dcol=dcol: e.scalar_tensor_tensor(
                V(ytmp, 0, 128, 0, [[1, NS]]), V(sx, 0, 128, blk * TT + SEQ, [[1, NS]]), dcol,
                V(PS[YSB], 0, 128, 0, [[1, NS]]), ALU.mult, ALU.add),
                R=[("sxs", blk), "cv4", ("ps", YSB)], W=["ytmp"])
            P.op("scalar", lambda e, blk=blk: e.activation(
                out=V(ysg, 0, 128, blk * TT + SEQ, [[1, NS]]), in_=V(ytmp, 0, 128, 0, [[1, NS]]), func=AF.Gelu),
                R=["ytmp"], W=[("sxs", blk)])
        ck(9)
        prefetch("glu", w_glu, 512, 4, 0, 512)
        prefetch("sz", w_in, 6144, 8, 3584, 512)
        for comp, dd in ((0, sre_p), (1, sim_p)):
            P.dma("sync", lambda e, comp=comp, dd=dd: e.dma_start(
                out=DR(dd, 0, [[1, 128], [128, 16], [1, 1]]), in_=V(fin, 0, 128, comp * 16, [[1, 16], [1, 1]]),
                allow_slow_non_contiguous=True),
                R=[("fin", comp, qg) for qg in range(16)], W=["out_fin%d" % comp])
        for comp, dd in ((0, sre_s), (1, sim_s)):
            pbs = [nextps() for _ in range(4)]
            for qg in range(16):
                pb = pbs[qg // 4]
                P.op("tensor", lambda e, comp=comp, qg=qg, pb=pb: e.matmul(
                    V(PS[pb], 0, NS, (qg % 4) * 128, [[1, 128]]),
                    V(sN, 0, 128, comp * 256 + qg * NS, [[1, NS]]),
                    V(ident_f, 0, 128, 0, [[1, 128]]), start=True, stop=True),
                    R=["sN%d" % comp, "ident_f"], W=[("ps", pb)])
            for i4 in range(4):
                P.op("scalar", lambda e, comp=comp, i4=i4, pb=pbs[i4]: e.activation(
                    out=V(Gc, 0, NS, comp * 2048 + i4 * 512, [[1, 512]]), in_=V(PS[pb], 0, NS, 0, [[1, 512]]), func=AF.Copy),
                    R=[("ps", pbs[i4])], W=Gkeys)
            P.dma("sync", lambda e, comp=comp, dd=dd: e.dma_start(out=dd.ap(), in_=V(Gc, 0, NS, comp * 2048, [[1, 2048]])),
                  R=Gkeys, W=["out_soT%d" % comp])
        P.barrier()
        stL.close()
        stS.close()

        ys2 = sb("ys2", [4, TT], BF16, stC)
        gt1 = [sb("gt1_%d" % i, [512], F32, stC) for i in range(2)]
        gt2 = [sb("gt2_%d" % i, [512], F32, stC) for i in range(2)]
        wi_glu = getw("glu", w_glu, 512, 4, 0, 512)
        wi_sz = getw("sz", w_in, 6144, 8, 3584, 512)
        it = 0
        for oc in range(4):
            for (cs, cn) in tblocks:
                b = it % 2; it += 1
                pb = nextps()
                mm_fm(wi_glu, 4, oc * 128, ysg, "ysgall", cs, cn, pb)
                P.op("scalar", lambda e, oc=oc, cn=cn, pb=pb, b=b: e.activation(
                    out=V(gt1[b], 0, 128, 0, [[1, cn]]), in_=V(PS[pb], 0, 128, 0, [[1, cn]]), func=AF.Sigmoid,
                    bias=V(cv4, 0, 128, oc, [[1, 1]])), R=[("ps", pb), "cv4"], W=[("gt1", b)])
                pb2 = nextps()
                mm_fm(wi_sz, 8, oc * 128, hT, "hTall", cs, cn, pb2)
                P.op("scalar", lambda e, cn=cn, pb2=pb2, b=b: e.activation(
                    out=V(gt2[b], 0, 128, 0, [[1, cn]]), in_=V(PS[pb2], 0, 128, 0, [[1, cn]]), func=AF.Sigmoid),
                    R=[("ps", pb2)], W=[("gt2", b)])
                P.op("vector", lambda e, cn=cn, pb2=pb2, b=b: e.tensor_tensor(
                    V(gt2[b], 0, 128, 0, [[1, cn]]), V(gt2[b], 0, 128, 0, [[1, cn]]), V(PS[pb2], 0, 128, 0, [[1, cn]]), ALU.mult),
                    R=[("gt2", b), ("ps", pb2)], W=[("gt2", b)])
                P.op("vector", lambda e, cn=cn, b=b, oc=oc, cs=cs: e.tensor_tensor(
                    V(gt1[b], 0, 128, 0, [[1, cn]]), V(gt1[b], 0, 128, 0, [[1, cn]]), V(ysg, 0, 128, oc * TT + cs, [[1, cn]]), ALU.mult),
                    R=[("gt1", b), "ysgall"], W=[("gt1", b)])
                P.op("vector", lambda e, cn=cn, b=b, oc=oc, cs=cs: e.tensor_tensor(
                    V(ys2, 0, 128, oc * TT + cs, [[1, cn]]), V(gt1[b], 0, 128, 0, [[1, cn]]), V(gt2[b], 0, 128, 0, [[1, cn]]), ALU.mult),
                    R=[("gt1", b), ("gt2", b)], W=["ys2all"])
        for half in range(2):
            wi_ps = load_w(w_ps, D, 4, half * 512, 512)
            wi_gs = load_w(w_in, 6144, 8, 5120 + half * 512, 512)
            for o4 in range(4):
                oc = half * 4 + o4
                for (cs, cn) in tblocks:
                    b = it % 2; it += 1
                    pb = nextps()
                    mm_fm(wi_gs, 8, o4 * 128, hT, "hTall", cs, cn, pb)
                    P.op("scalar", lambda e, cn=cn, pb=pb, b=b: e.activation(
                        out=V(gt1[b], 0, 128, 0, [[1, cn]]), in_=V(PS[pb], 0, 128, 0, [[1, cn]]), func=AF.Sigmoid),
                        R=[("ps", pb)], W=[("gt1", b)])
                    pb2 = nextps()
                    mm_fm(wi_ps, 4, o4 * 128, ys2, "ys2all", cs, cn, pb2)
                    P.op("vector", lambda e, cn=cn, pb2=pb2, b=b, oc=oc, cs=cs: e.tensor_tensor(
                        V(m_t, 0, 128, oc * TT + cs, [[1, cn]]), V(gt1[b], 0, 128, 0, [[1, cn]]), V(PS[pb2], 0, 128, 0, [[1, cn]]), ALU.mult),
                        R=[("gt1", b), ("ps", pb2)], W=[("m", oc)])
        prefetch("a0", w_in, 6144, 8, 0, 512)
        prefetch("b0", w_in, 6144, 8, 1024, 512)
        P.barrier()
        stC.close()

        stB = ExitStack(); stacks.append(stB)
        UW = 30 + SEQ
        u_t = sb("u_t", [8, UW], BF16, stB)
        vs_b = sb("vs_b", [8, NS], BF16, stB)
        us_f = sb("us_f", [8, NS], F32, stB)
        ul_f = sb("ul_f", [8, 30], F32, stB)
        vs_f = sb("vs_f", [8, NS], F32, stB)
        cwt = sb("cwt", [8, CK], F32, stB)
        bt1 = [sb("bt1_%d" % i, [512], F32, stB) for i in range(2)]
        bt2 = [sb("bt2_%d" % i, [512], F32, stB) for i in range(2)]
        acc1 = sb("acc1", [TT], F32, stB)
        acc2 = sb("acc2", [TT], F32, stB)
        bt3 = [sb("bt3_%d" % i, [512], F32, stB) for i in range(3)]
        cz1 = [sb("cz1_%d" % i, [512], F32, stB) for i in range(3)]

        def vcol(c, cs, cn):
            if cs < SEQ:
                return V(u_t, 0, 128, c * UW + 30 + cs, [[1, cn]])
            return V(vs_b, 0, 128, c * NS, [[1, cn]])
        P.dma("sync", lambda e: e.dma_start(out=cwt[:], in_=convw.ap()), W=["cwt"])
        P.op("gpsimd", lambda e: e.memset(V(u_t, 0, 128, 0, [[UW, 8], [1, 30]]), 0.0), W=[("u", c, -1) for c in range(8)])
        P.op("gpsimd", lambda e: e.memset(acc1[:], 0.0), W=["acc1"])
        P.op("gpsimd", lambda e: e.memset(acc2[:], 0.0), W=["acc2"])
        for half in range(2):
            wi_a = getw("a%d" % half, w_in, 6144, 8, half * 512, 512)
            wi_b = getw("b%d" % half, w_in, 6144, 8, 1024 + half * 512, 512)
            for o4 in range(4):
                c = half * 4 + o4
                for tbi, (cs, cn) in enumerate(tblocks):
                    b = it % 2; it += 1
                    pb = nextps()
                    mm_fm(wi_b, 8, o4 * 128, hT, "hTall", cs, cn, pb)
                    P.op("scalar", lambda e, cn=cn, pb=pb, b=b: e.activation(
                        out=V(bt1[b], 0, 128, 0, [[1, cn]]), in_=V(PS[pb], 0, 128, 0, [[1, cn]]), func=AF.Sigmoid),
                        R=[("ps", pb)], W=[("bt1", b)])
                    pb2 = nextps()
                    mm_fm(wi_a, 8, o4 * 128, hT, "hTall", cs, cn, pb2)
                    if cs < SEQ:
                        P.op("vector", lambda e, cn=cn, pb2=pb2, b=b, c=c, cs=cs: e.tensor_tensor(
                            V(u_t, 0, 128, c * UW + 30 + cs, [[1, cn]]), V(bt1[b], 0, 128, 0, [[1, cn]]), V(PS[pb2], 0, 128, 0, [[1, cn]]), ALU.mult),
                            R=[("bt1", b), ("ps", pb2)], W=[("u", c, tbi)])
                        if tbi == NTB - 1:
                            P.op("vector", lambda e, pb2=pb2, b=b, c=c: e.tensor_tensor(
                                V(ul_f, 0, 128, c * 30, [[1, 30]]), V(bt1[b], 0, 128, 482, [[1, 30]]), V(PS[pb2], 0, 128, 482, [[1, 30]]), ALU.mult),
                                R=[("bt1", b), ("ps", pb2)], W=[("ul", c)])
                    else:
                        P.op("vector", lambda e, cn=cn, pb2=pb2, b=b, c=c: e.tensor_tensor(
                            V(us_f, 0, 128, c * NS, [[1, NS]]), V(bt1[b], 0, 128, 0, [[1, cn]]), V(PS[pb2], 0, 128, 0, [[1, cn]]), ALU.mult),
                            R=[("bt1", b), ("ps", pb2)], W=[("us", c)])
        stB1 = ExitStack(); stacks.append(stB1)
        nrows = NS * 30
        cacheT = sb("cacheT", [8, nrows], F32, stB1)
        crow = [sb("crow%d" % i, [D], F32, stB1) for i in range(2)]
        ctmp = sb("ctmp", [nrows], F32, stB1)
        otm = sb("otm", [D], F32, stB1)
        for half in range(2):
            pb = nextps()
            for kk in range(4):
                c = half * 4 + kk
                P.op("tensor", lambda e, c=c, kk=kk, pb=pb: e.matmul(
                    V(PS[pb], 0, 30, kk * 128, [[1, 128]]), V(ul_f, 0, 128, c * 30, [[1, 30]]),
                    V(ident_f, 0, 128, 0, [[1, 128]]), start=True, stop=True),
                    R=[("ul", c), "ident_f"], W=[("ps", pb)])
            P.op("scalar", lambda e, half=half, pb=pb: e.activation(
                out=V(otm, 0, 30, half * 512, [[1, 512]]), in_=V(PS[pb], 0, 30, 0, [[1, 512]]), func=AF.Copy),
                R=[("ps", pb)], W=[("otm", half)])
        P.dma("sync", lambda e: e.dma_start(out=conv_p.ap(), in_=V(otm, 0, 30, 0, [[1, D]])),
              R=[("otm", 0), ("otm", 1)], W=["otm_out"])
        P.dma("sync", lambda e: e.dma_start(out=DR(conv_s, 0, [[30 * D, NS], [1, 29 * D]]),
                                            in_=DR(cache, D, [[30 * D, NS], [1, 29 * D]])), W=["conv_s_a"])
        for half in range(2):
            pb = nextps()
            for kk in range(4):
                c = half * 4 + kk
                P.op("tensor", lambda e, c=c, kk=kk, pb=pb: e.matmul(
                    V(PS[pb], 0, NS, kk * 128, [[1, 128]]), V(us_f, 0, 128, c * NS, [[1, NS]]),
                    V(ident_f, 0, 128, 0, [[1, 128]]), start=True, stop=True),
                    R=[("us", c), "ident_f"], W=[("ps", pb)])
            P.op("scalar", lambda e, half=half, pb=pb: e.activation(
                out=V(otm, 32, NS, half * 512, [[1, 512]]), in_=V(PS[pb], 0, NS, 0, [[1, 512]]), func=AF.Copy),
                R=[("ps", pb)], W=[("otm2", half)])
        P.dma("sync", lambda e: e.dma_start(out=DR(conv_s, 29 * D, [[30 * D, NS], [1, D]]), in_=V(otm, 32, NS, 0, [[1, D]])),
              R=[("otm2", 0), ("otm2", 1)], W=["conv_s_b"])
        rtiles = [(r, min(128, nrows - r)) for r in range(0, nrows, 128)]
        for ri, (r0, nr) in enumerate(rtiles):
            b = ri % 2
            P.dma("sync", lambda e, r0=r0, nr=nr, b=b: e.dma_start(out=V(crow[b], 0, nr, 0, [[1, D]]),
                                                                  in_=DR(cache, r0 * D, [[D, nr], [1, D]])), W=[("crow", b)])
            for c in range(8):
                pb = nextps()
                P.op("tensor", lambda e, c=c, pb=pb, nr=nr, b=b: e.matmul(
                    V(PS[pb], 0, 128, 0, [[1, nr]]), V(crow[b], 0, nr, c * 128, [[1, 128]]),
                    V(ident_f, 0, nr, 0, [[1, nr]]), start=True, stop=True),
                    R=[("crow", b), "ident_f"], W=[("ps", pb)])
                P.op("scalar", lambda e, c=c, pb=pb, nr=nr, r0=r0: e.activation(
                    out=V(cacheT, 0, 128, c * nrows + r0, [[1, nr]]), in_=V(PS[pb], 0, 128, 0, [[1, nr]]), func=AF.Copy),
                    R=[("ps", pb)], W=[("cacheT", c)])
        for c in range(8):
            P.op("vector", lambda e, c=c: e.tensor_tensor(
                V(ctmp, 0, 128, 0, [[30, NS], [1, 30]]), V(cacheT, 0, 128, c * nrows, [[30, NS], [1, 30]]),
                V(cwt, 0, 128, c * CK, [[0, NS], [1, 30]]), ALU.mult), R=[("cacheT", c), "cwt"], W=["ctmp"])
            P.op("vector", lambda e, c=c: e.tensor_reduce(
                V(vs_f, 0, 128, c * NS, [[1, NS]]), V(ctmp, 0, 128, 0, [[30, NS], [1, 30]]), mybir.AxisListType.X, ALU.add),
                R=["ctmp"], W=[("vs", c)])
            P.op("vector", lambda e, c=c: e.scalar_tensor_tensor(
                V(vs_f, 0, 128, c * NS, [[1, NS]]), V(us_f, 0, 128, c * NS, [[1, NS]]), V(cwt, 0, 128, c * CK + 30, [[1, 1]]),
                V(vs_f, 0, 128, c * NS, [[1, NS]]), ALU.mult, ALU.add), R=[("vs", c), ("us", c), "cwt"], W=[("vs", c)])
        P.barrier()
        stB1.close()
        stB2 = ExitStack(); stacks.append(stB2)
        dg = [sb("dg%d" % i, [CK, 128], BF16, stB2) for i in range(2)]
        vsq = [sb("vsq%d" % i, [512], BF16, stB2) for i in range(2)]
        msq = sb("msq", [512], F32, stB2)
        ND = 4
        accD = [sb("accD%d" % i, [SEQ], F32, stB2) for i in range(2)]
        conv_items = []
        for c in range(8):
            for tbi, (cs, cn) in reversed(list(enumerate(tblocks))):
                conv_items.append((c, tbi, cs, cn))
        conv_state = {}

        def build_dg(c):
            d_ = dg[c % 2]
            for k in range(ND, CK):
                P.op("scalar", lambda e, k=k: e.activation(
                    out=V(d_, 0, 128, k * 128, [[1, 128]]), in_=ident_f[:], func=AF.Identity, scale=V(cwt, 0, 128, c * CK + k, [[1, 1]])),
                    R=["ident_f", "cwt"], W=[("dg", c % 2, k)])

        def dve_taps(c, ks):
            a_ = accD[c % 2]
            ukeys = [("u", c, t) for t in range(-1, NTB)]
            for k in ks:
                srcu = V(u_t, 0, 128, c * UW + k, [[1, SEQ]])
                wcol = V(cwt, 0, 128, c * CK + k, [[1, 1]])
                if k == 0:
                    P.op("vector", lambda e, srcu=srcu, wcol=wcol: e.tensor_scalar(a_[:], srcu, wcol, None, ALU.mult),
                         R=ukeys + ["cwt"], W=[("accD", c % 2)])
                else:
                    P.op("vector", lambda e, srcu=srcu, wcol=wcol: e.scalar_tensor_tensor(a_[:], srcu, wcol, a_[:], ALU.mult, ALU.add),
                         R=ukeys + ["cwt", ("accD", c % 2)], W=[("accD", c % 2)])

        def sC1(ii):
            c, tbi, cs, cn = conv_items[ii]
            d_ = dg[c % 2]
            if ii == 0:
                build_dg(0)
                dve_taps(0, range(ND))
            if ii % len(tblocks) == 1 and c + 1 < 8:
                build_dg(c + 1)
            slots = list(range(1, len(tblocks)))[:3]
            per = -(-ND // len(slots))
            if c + 1 < 8 and (ii % len(tblocks)) in slots:
                j = slots.index(ii % len(tblocks))
                dve_taps(c + 1, range(per * j, min(ND, per * j + per)))
            b = ii % 2
            bias = V(cv8, 0, 128, c, [[1, 1]])
            if cs < SEQ:
                pb = nextps()
                for k in range(ND, CK):
                    P.op("tensor", lambda e, k=k, pb=pb: e.matmul(
                        V(PS[pb], 0, 128, 0, [[1, 512]]), V(d_, 0, 128, k * 128, [[1, 128]]),
                        V(u_t, 0, 128, c * UW + cs + k, [[1, 512]]), start=(k == ND), stop=(k == CK - 1)),
                        R=[("dg", c % 2, k), ("u", c, tbi), ("u", c, tbi - 1)], W=[("ps", pb)])
                P.op("vector", lambda e, pb=pb: e.tensor_tensor(
                    V(bt2[b], 0, 128, 0, [[1, cn]]), V(PS[pb], 0, 128, 0, [[1, cn]]), V(accD[c % 2], 0, 128, cs, [[1, cn]]), ALU.add),
                    R=[("ps", pb), ("accD", c % 2)], W=[("bt2", b)])
                src = V(bt2[b], 0, 128, 0, [[1, cn]]); rk = ("bt2", b)
            else:
                src = V(vs_f, 0, 128, c * NS, [[1, NS]]); rk = ("vs", c)
            P.op("scalar", lambda e: e.activation(
                out=vcol(c, cs, cn), in_=src, func=AF.Identity, bias=bias),
                R=[rk, "cv8"], W=[("u", c, tbi)])
            P.op("scalar", lambda e: e.activation(
                out=V(vsq[b], 0, 128, 0, [[1, cn]]), in_=src, func=AF.Square, bias=bias),
                R=[rk, "cv8"], W=[("vsq", b)])

        def sC2(ii):
            c, tbi, cs, cn = conv_items[ii]
            b = ii % 2
            p1 = nextps(); p2 = nextps()
            P.op("tensor", lambda e: e.matmul(
                V(PS[p1], 0, 128, 0, [[1, cn]]), ones_b[:], vcol(c, cs, cn), start=True, stop=True),
                R=["ones_b", ("u", c, tbi)], W=[("ps", p1)])
            P.op("tensor", lambda e: e.matmul(
                V(PS[p2], 0, 128, 0, [[1, cn]]), ones_b[:], V(vsq[b], 0, 128, 0, [[1, cn]]), start=True, stop=True),
                R=["ones_b", ("vsq", b)], W=[("ps", p2)])
            P.op("vector", lambda e: e.tensor_tensor(
                V(acc1, 0, 128, cs, [[1, cn]]), V(acc1, 0, 128, cs, [[1, cn]]), V(PS[p1], 0, 128, 0, [[1, cn]]), ALU.add),
                R=[("ps", p1), ("acc1", tbi)], W=[("acc1", tbi)])
            P.op("vector", lambda e: e.tensor_tensor(
                V(acc2, 0, 128, cs, [[1, cn]]), V(acc2, 0, 128, cs, [[1, cn]]), V(PS[p2], 0, 128, 0, [[1, cn]]), ALU.add),
                R=[("ps", p2), ("acc2", tbi)], W=[("acc2", tbi)])
        pipeline([sC1, sC2], len(conv_items))
        for tbi, (cs, cn) in enumerate(tblocks):
            a1 = V(acc1, 0, 128, cs, [[1, cn]]); a2 = V(acc2, 0, 128, cs, [[1, cn]]); mq = V(msq, 0, 128, 0, [[1, cn]])
            k1 = ("acc1", tbi); k2 = ("acc2", tbi)
            P.op("vector", lambda e, a1=a1: e.tensor_scalar(a1, a1, 1.0 / D, None, ALU.mult), R=[k1, "acc1"], W=[k1])
            P.op("vector", lambda e, a2=a2: e.tensor_scalar(a2, a2, 1.0 / D, EPS, ALU.mult, ALU.add), R=[k2, "acc2"], W=[k2])
            P.op("vector", lambda e, a1=a1, mq=mq: e.tensor_tensor(mq, a1, a1, ALU.mult), R=[k1], W=["msq"])
            P.op("vector", lambda e, a2=a2, mq=mq: e.tensor_tensor(a2, a2, mq, ALU.subtract), R=[k2, "msq"], W=[k2])
            P.op("scalar", lambda e, a2=a2: e.activation(out=a2, in_=a2, func=AF.Ln), R=[k2], W=[k2])
            P.op("scalar", lambda e, a2=a2: e.activation(out=a2, in_=a2, func=AF.Exp, scale=-0.5), R=[k2], W=[k2])
            P.op("vector", lambda e, a1=a1, a2=a2: e.scalar_tensor_tensor(a1, a1, -1.0, a2, ALU.mult, ALU.mult), R=[k1, k2], W=[k1])
        v2_items = []
        for half in range(2):
            for o4 in range(4):
                for tbi, (cs, cn) in enumerate(tblocks):
                    v2_items.append((half, o4, tbi, cs, cn))
        wz = {}

        def sV1(ii):
            half, o4, tbi, cs, cn = v2_items[ii]
            c = half * 4 + o4
            if o4 == 0 and tbi == 0:
                wz[half] = getw("z%d" % half, w_in, 6144, 8, 2048 + half * 512, 512)
            b = ii % 3
            pb = nextps()
            mm_fm(wz[half], 8, o4 * 128, hT, "hTall", cs, cn, pb)
            P.op("scalar", lambda e: e.activation(
                out=V(cz1[b], 0, 128, 0, [[1, cn]]), in_=V(PS[pb], 0, 128, 0, [[1, cn]]), func=AF.Silu),
                R=[("ps", pb)], W=[("cz1", b)])
            P.op("vector", lambda e: e.tensor_tensor(
                V(bt3[b], 0, 128, 0, [[1, cn]]), vcol(c, cs, cn), V(acc2, 0, 128, cs, [[1, cn]]), ALU.mult),
                R=[("u", c, tbi), ("acc2", tbi)], W=[("bt3", b)])
            P.op("vector", lambda e: e.tensor_tensor(
                V(bt3[b], 0, 128, 0, [[1, cn]]), V(bt3[b], 0, 128, 0, [[1, cn]]), V(acc1, 0, 128, cs, [[1, cn]]), ALU.add),
                R=[("bt3", b), ("acc1", tbi)], W=[("bt3", b)])

        def sV2(ii):
            half, o4, tbi, cs, cn = v2_items[ii]
            c = half * 4 + o4
            b = ii % 3
            P.op("scalar", lambda e: e.activation(
                out=V(bt3[b], 0, 128, 0, [[1, cn]]), in_=V(bt3[b], 0, 128, 0, [[1, cn]]), func=AF.Silu,
                scale=V(cv8, 0, 128, 8 + c, [[1, 1]]), bias=V(cv8, 0, 128, 16 + c, [[1, 1]])),
                R=[("bt3", b), "cv8"], W=[("bt3", b)])
            P.op("vector", lambda e: e.tensor_tensor(
                vcol(c, cs, cn), V(bt3[b], 0, 128, 0, [[1, cn]]), V(cz1[b], 0, 128, 0, [[1, cn]]), ALU.mult),
                R=[("bt3", b), ("cz1", b)], W=[("u", c, tbi)])
        pipeline([sV1, sV2], len(v2_items))
        v2keys_all = {tbi: [("u", c, tbi) for c in range(8)] for tbi in range(len(tblocks))}
        for half in range(2):
            wi_pc = load_w(w_pc, D, 8, half * 512, 512)
            wi_gc = load_w(w_in, 6144, 8, 4096 + half * 512, 512)
            for o4 in range(4):
                oc = half * 4 + o4
                for tbi_, (cs, cn) in enumerate(tblocks):
                    v2keys = v2keys_all[tbi_]
                    b = it % 2; it += 1
                    pb = nextps()
                    mm_fm(wi_gc, 8, o4 * 128, hT, "hTall", cs, cn, pb)
                    P.op("scalar", lambda e, cn=cn, pb=pb, b=b: e.activation(
                        out=V(bt1[b], 0, 128, 0, [[1, cn]]), in_=V(PS[pb], 0, 128, 0, [[1, cn]]), func=AF.Sigmoid),
                        R=[("ps", pb)], W=[("bt1", b)])
                    pb2 = nextps()
                    buf = WB[wi_pc]
                    for k in range(8):
                        P.op("tensor", lambda e, k=k, buf=buf, o4=o4, cs=cs, cn=cn, pb2=pb2: e.matmul(
                            V(PS[pb2], 0, 128, 0, [[1, cn]]), V(buf, 0, 128, k * 512 + o4 * 128, [[1, 128]]),
                            vcol(k, cs, cn), start=(k == 0), stop=(k == 7)),
                            R=[("wb", wi_pc)] + v2keys, W=[("ps", pb2)])
                    P.op("vector", lambda e, cn=cn, pb2=pb2, b=b: e.tensor_tensor(
                        V(bt1[b], 0, 128, 0, [[1, cn]]), V(bt1[b], 0, 128, 0, [[1, cn]]), V(PS[pb2], 0, 128, 0, [[1, cn]]), ALU.mult),
                        R=[("bt1", b), ("ps", pb2)], W=[("bt1", b)])
                    P.op("vector", lambda e, cn=cn, b=b, oc=oc, cs=cs: e.tensor_tensor(
                        V(m_t, 0, 128, oc * TT + cs, [[1, cn]]), V(m_t, 0, 128, oc * TT + cs, [[1, cn]]), V(bt1[b], 0, 128, 0, [[1, cn]]), ALU.add),
                        R=[("bt1", b), ("m", oc)], W=[("m", oc)])
        wo = [load_w(w_out, D, 8, h2 * 512, 512) for h2 in range(2)]
        wg = [load_w(w_pg, D, 8, h2 * 512, 512) for h2 in range(2)]
        P.barrier()
        stB2.close()
        stB.close()

        stH.close()
        stD = ExitStack(); stacks.append(stD)
        wple_b = sb("wple_b", [2, D], BF16, stD)
        bpg_b = sb("bpg_b", [D], BF16, stD)
        P.dma("gpsimd", lambda e: e.dma_start(out=V(bpg_b, 0, 1, 0, [[1, D]]),
                                              in_=DR(rows, 3 * D, [[0, 1], [1, D]])), W=["bpg_b"])
        rows2 = sb("rows2", [2, D], F32, stD)
        P.dma("sync", lambda e: e.dma_start(out=V(rows2, 0, 128, 0, [[1, 2 * D]]), in_=DR(rows, D, [[0, 128], [1, 2 * D]])), W=["rows2"])
        for (dst, wd, kc, nm) in ((wple_b, w_ple, 2, "wple"),):
            for h2 in range(2):
                P.dma("gpsimd", lambda e, dst=dst, wd=wd, kc=kc, h2=h2: e.dma_start(
                    out=V(dst, 0, 128, h2 * 512, [[D, kc], [1, 512]]),
                    in_=DR(wd, h2 * 512, [[D, 128], [128 * D, kc], [1, 512]])), W=[(nm, h2)])
        RX, RE, R1, RB = 4, 5, 3, 3
        xd = [sb("xd%d" % i, [D], F32, stD) for i in range(RX)]
        osb = [sb("osb%d" % i, [D], F32, stD) for i in range(RX)]
        en = [sb("en%d" % i, [D], F32, stD) for i in range(RE)]
        x1 = [sb("x1_%d" % i, [D], F32, stD) for i in range(R1)]
        pd = [sb("pd%d" % i, [256], BF16, stD) for i in range(RB)]
        pT = [sb("pT%d" % i, [2, 128], BF16, stD) for i in range(RB)]
        x1b = [sb("x1b%d" % i, [D], BF16, stD) for i in range(RB)]
        x1T = [sb("x1T%d" % i, [8, 128], BF16, stD) for i in range(RB)]
        gsig = [sb("gsig%d" % i, [D], F32, stD) for i in range(2)]
        yo = [sb("yo%d" % i, [D], F32, stD) for i in range(2)]
        jks = [sb("jk%d" % i, [512], BF16, stD) for i in range(4)]
        ssD = sb("ssD", [len(ttiles), 8], F32, stD)
        mkeys = [("m", oc) for oc in range(8)]
        tst = {}

        def col(ti, nt, c0):
            return V(ssD, 0, nt, ti * 8 + c0, [[1, 1]])

        def tl1(ti):
            r0, nt = ttiles[ti]
            srcx = DR(xp, r0 * D, [[D, nt], [1, D]]) if r0 < SEQ else DR(xs, 0, [[D, nt], [1, D]])
            srcp = DR(pp, r0 * 256, [[256, nt], [1, 256]]) if r0 < SEQ else DR(psm, 0, [[256, nt], [1, 256]])
            bx = ti % RX; bb = ti % RB
            P.dma("sync", lambda e: e.dma_start(out=V(xd[bx], 0, nt, 0, [[1, D]]), in_=srcx), W=[("xd", bx)])
            P.dma("gpsimd", lambda e: e.dma_start(out=V(pd[bb], 0, nt, 0, [[1, 256]]), in_=srcp), W=[("pd", bb)])
            ob = [nextps(), nextps()]
            for h2 in range(2):
                for k in range(8):
                    P.op("tensor", lambda e, k=k, h2=h2: e.matmul(
                        V(PS[ob[h2]], 0, nt, 0, [[1, 512]]), V(m_t, 0, 128, k * TT + r0, [[1, nt]]),
                        V(WB[wo[h2]], 0, 128, k * 512, [[1, 512]]), start=(k == 0), stop=(k == 7)),
                        R=mkeys + [("wb", wo[h2])], W=[("ps", ob[h2])])
                P.op("scalar", lambda e, h2=h2: e.activation(
                    out=V(jks[h2], 0, nt, 0, [[1, 512]]), in_=V(PS[ob[h2]], 0, nt, 0, [[1, 512]]), func=AF.Square,
                    accum_out=col(ti, nt, h2)), R=[("ps", ob[h2])], W=[("jk", h2), ("ssD", ti, h2)])
                P.op("scalar", lambda e, h2=h2: e.activation(
                    out=V(osb[bx], 0, nt, h2 * 512, [[1, 512]]), in_=V(PS[ob[h2]], 0, nt, 0, [[1, 512]]), func=AF.Copy),
                    R=[("ps", ob[h2])], W=[("osb", bx, h2)])
            pb = nextps()
            for k in range(2):
                P.op("tensor", lambda e, k=k: e.matmul(
                    V(PS[pb], 0, 128, k * nt, [[1, nt]]), V(pd[bb], 0, nt, k * 128, [[1, 128]]),
                    V(ident_b, 0, nt, 0, [[1, nt]]), start=True, stop=True),
                    R=[("pd", bb), "ident_b"], W=[("ps", pb)])
            P.op("vector", lambda e: e.tensor_copy(
                V(pT[bb], 0, 128, 0, [[128, 2], [1, nt]]), V(PS[pb], 0, 128, 0, [[nt, 2], [1, nt]])),
                R=[("ps", pb)], W=[("pT", bb)])

        def tl2(ti):
            r0, nt = ttiles[ti]
            bb = ti % RB; be = ti % RE
            eb = [nextps(), nextps()]
            for h2 in range(2):
                for k in range(2):
                    P.op("tensor", lambda e, k=k, h2=h2: e.matmul(
                        V(PS[eb[h2]], 0, nt, 0, [[1, 512]]), V(pT[bb], 0, 128, k * 128, [[1, nt]]),
                        V(wple_b, 0, 128, k * D + h2 * 512, [[1, 512]]), start=(k == 0), stop=(k == 1)),
                        R=[("pT", bb), ("wple", h2)], W=[("ps", eb[h2])])
                P.op("scalar", lambda e, h2=h2: e.activation(
                    out=V(jks[2 + h2], 0, nt, 0, [[1, 512]]), in_=V(PS[eb[h2]], 0, nt, 0, [[1, 512]]), func=AF.Square,
                    accum_out=col(ti, nt, 2 + h2)), R=[("ps", eb[h2])], W=[("jk", 2 + h2), ("ssD", ti, 2 + h2)])
                P.op("scalar", lambda e, h2=h2: e.activation(
                    out=V(en[be], 0, nt, h2 * 512, [[1, 512]]), in_=V(PS[eb[h2]], 0, nt, 0, [[1, 512]]), func=AF.Copy),
                    R=[("ps", eb[h2])], W=[("en", be, h2)])
            a = col(ti, nt, 0); a2 = col(ti, nt, 1)
            P.op("vector", lambda e: e.tensor_tensor(a, a, a2, ALU.add), R=[("ssD", ti, 0), ("ssD", ti, 1)], W=[("ssD", ti, 0)])

        def tl3(ti):
            r0, nt = ttiles[ti]
            a = col(ti, nt, 0)
            P.op("scalar", lambda e: e.activation(out=a, in_=a, func=AF.Ln, scale=1.0 / D, bias=EPS), R=[("ssD", ti, 0)], W=[("ssD", ti, 0)])
            P.op("scalar", lambda e: e.activation(out=a, in_=a, func=AF.Exp, scale=-0.5), R=[("ssD", ti, 0)], W=[("ssD", ti, 0)])
            c = col(ti, nt, 2); c2 = col(ti, nt, 3)
            P.op("vector", lambda e: e.tensor_tensor(c, c, c2, ALU.add), R=[("ssD", ti, 2), ("ssD", ti, 3)], W=[("ssD", ti, 2)])

        def tl4(ti):
            r0, nt = ttiles[ti]
            bx = ti % RX; b1 = ti % R1; bb = ti % RB
            a = col(ti, nt, 0); c = col(ti, nt, 2)
            P.op("scalar", lambda e: e.activation(out=c, in_=c, func=AF.Ln, scale=1.0 / D, bias=EPS), R=[("ssD", ti, 2)], W=[("ssD", ti, 2)])
            P.op("scalar", lambda e: e.activation(out=c, in_=c, func=AF.Exp, scale=-0.5), R=[("ssD", ti, 2)], W=[("ssD", ti, 2)])
            P.op("vector", lambda e: e.scalar_tensor_tensor(
                V(x1[b1], 0, nt, 0, [[1, D]]), V(osb[bx], 0, nt, 0, [[1, D]]), a,
                V(rows2, 0, nt, 0, [[1, D]]), ALU.mult, ALU.mult),
                R=[("osb", bx, 0), ("osb", bx, 1), ("ssD", ti, 0), "rows2"], W=[("x1", b1)])
            P.op("vector", lambda e: e.tensor_tensor(
                V(x1[b1], 0, nt, 0, [[1, D]]), V(x1[b1], 0, nt, 0, [[1, D]]), V(xd[bx], 0, nt, 0, [[1, D]]), ALU.add),
                R=[("x1", b1), ("xd", bx)], W=[("x1", b1)])
            P.op("gpsimd", lambda e: e.tensor_copy(V(x1b[bb], 0, nt, 0, [[1, D]]), V(x1[b1], 0, nt, 0, [[1, D]])),
                 R=[("x1", b1)], W=[("x1b", bb)])

        def tl5(ti):
            r0, nt = ttiles[ti]
            bb = ti % RB; be = ti % RE
            for half in range(2):
                pb = nextps()
                for kk in range(4):
                    k = half * 4 + kk
                    P.op("tensor", lambda e, k=k, kk=kk, pb=pb: e.matmul(
                        V(PS[pb], 0, 128, kk * nt, [[1, nt]]), V(x1b[bb], 0, nt, k * 128, [[1, 128]]),
                        V(ident_b, 0, nt, 0, [[1, nt]]), start=True, stop=True),
                        R=[("x1b", bb), "ident_b"], W=[("ps", pb)])
                P.op("scalar", lambda e, half=half, pb=pb: e.activation(
                    out=V(x1T[bb], 0, 128, half * 4 * 128, [[128, 4], [1, nt]]),
                    in_=V(PS[pb], 0, 128, 0, [[nt, 4], [1, nt]]), func=AF.Copy),
                    R=[("ps", pb)], W=[("x1T", bb, half)])
            c = col(ti, nt, 2)
            P.op("vector", lambda e: e.scalar_tensor_tensor(
                V(en[be], 0, nt, 0, [[1, D]]), V(en[be], 0, nt, 0, [[1, D]]), c,
                V(rows2, 0, nt, D, [[1, D]]), ALU.mult, ALU.mult),
                R=[("en", be, 0), ("en", be, 1), ("ssD", ti, 2), "rows2"], W=[("en", be, 0), ("en", be, 1)])

        def tl6(ti):
            r0, nt = ttiles[ti]
            bb = ti % RB; be = ti % RE; b1 = ti % R1; b2 = ti % 2
            gb = [nextps(), nextps()]
            for h2 in range(2):
                P.op("tensor", lambda e, h2=h2: e.matmul(
                    V(PS[gb[h2]], 0, nt, 0, [[1, 512]]), V(ones_b, 0, 1, 0, [[1, nt]]),
                    V(bpg_b, 0, 1, h2 * 512, [[1, 512]]), start=True, stop=False),
                    R=["ones_b", "bpg_b"], W=[("ps", gb[h2])])
                for k in range(8):
                    P.op("tensor", lambda e, k=k, h2=h2: e.matmul(
                        V(PS[gb[h2]], 0, nt, 0, [[1, 512]]), V(x1T[bb], 0, 128, k * 128, [[1, nt]]),
                        V(WB[wg[h2]], 0, 128, k * 512, [[1, 512]]), start=False, stop=(k == 7)),
                        R=[("x1T", bb, 0), ("x1T", bb, 1), ("wb", wg[h2])], W=[("ps", gb[h2])])
                P.op("scalar", lambda e, h2=h2: e.activation(
                    out=V(gsig[b2], 0, nt, h2 * 512, [[1, 512]]), in_=V(PS[gb[h2]], 0, nt, 0, [[1, 512]]), func=AF.Sigmoid),
                    R=[("ps", gb[h2])], W=[("gsig", b2, h2)])
            P.op("vector", lambda e: e.tensor_tensor(
                V(en[be], 0, nt, 0, [[1, D]]), V(en[be], 0, nt, 0, [[1, D]]), V(gsig[b2], 0, nt, 0, [[1, D]]), ALU.mult),
                R=[("en", be, 0), ("en", be, 1), ("gsig", b2, 0), ("gsig", b2, 1)], W=[("en", be, 0), ("en", be, 1)])
            P.op("vector", lambda e: e.tensor_tensor(
                V(yo[b2], 0, nt, 0, [[1, D]]), V(en[be], 0, nt, 0, [[1, D]]), V(x1[b1], 0, nt, 0, [[1, D]]), ALU.add),
                R=[("en", be, 0), ("en", be, 1), ("x1", b1)], W=[("yo", b2)])
            dsty = DR(y_p, r0 * D, [[D, nt], [1, D]]) if r0 < SEQ else DR(y_s, 0, [[D, nt], [1, D]])
            P.dma("sync", lambda e: e.dma_start(out=dsty, in_=V(yo[b2], 0, nt, 0, [[1, D]])),
                  R=[("yo", b2)], W=[("yout", ti)])
        pipeline([tl1, tl2, tl3, tl4, tl5, tl6], len(ttiles))
        P.barrier(final=True)


    except _Stop:
        for st_ in reversed(stacks):
            st_.close()
        P.barrier(final=True)
    with nc.Block() as block:
        P.replay(block)
    for st_ in reversed(stacks):
        st_.close()
    es.close()
    return nc


def _host_layouts(inp, SEQ):
    f = lambda a: np.ascontiguousarray(np.asarray(a, dtype=np.float32))
    out = {}
    out["w_in"] = f(inp["w_in"][0]); out["w_pc"] = f(inp["w_pc"][0]); out["w_ps"] = f(inp["w_ps"][0])
    out["w_glu"] = f(inp["w_glu"][0]); out["w_out"] = f(inp["w_out"][0]); out["w_pg"] = f(inp["w_pg"][0])
    out["w_ple"] = f(inp["w_ple"][0])
    out["rows"] = f(np.stack([inp["g_pre"][0], inp["g_post"][0], inp["g_ple"][0], inp["b_pg"][0]]))
    col8 = lambda v: np.asarray(v).reshape(8, 128).T
    col4 = lambda v: np.asarray(v).reshape(4, 128).T
    out["colv8"] = f(np.stack([col8(inp["conv_b"][0]), col8(inp["ln_g"][0]), col8(inp["ln_b"][0])], axis=1))
    out["colv4"] = f(np.stack([col4(inp["b_glu"][0]), col4(inp["ssm_d"][0])], axis=1))
    out["convw"] = f(np.asarray(inp["conv_w"][0]).T.reshape(8, 128, CK).transpose(1, 0, 2))
    a_re = np.asarray(inp["ssm_a_re"][0]); a_im = np.asarray(inp["ssm_a_im"][0]); ldt = np.asarray(inp["ssm_log_dt"][0])
    b_re = np.asarray(inp["ssm_b_re"][0]); b_im = np.asarray(inp["ssm_b_im"][0])
    c_re = np.asarray(inp["ssm_c_re"][0]); c_im = np.asarray(inp["ssm_c_im"][0])
    PA = np.zeros((128, 3, 4, 64), np.float32)
    BA = np.zeros((128, 2, 4, 64), np.float32)
    MA = np.zeros((128, 8), np.float32)
    PB = np.zeros((128, 3, 16), np.float32)
    CBm = np.zeros((128, 2, 16, 16), np.float32)
    BBm = np.zeros((128, 2, 16, 16), np.float32)
    MB = np.zeros((128, 4, 8), np.float32)
    for blk in range(4):
        for gl in range(8):
            g = blk * 8 + gl
            q, par = gl // 2, gl % 2
            rows = slice(gl * 16, gl * 16 + 16)
            PA[rows, 0, blk, :] = a_re[g][None, :]
            PA[rows, 1, blk, :] = a_im[g][None, :]
            PA[rows, 2, blk, :] = ldt[g]
            BA[rows, 0, blk, :] = b_re[g].T
            BA[rows, 1, blk, :] = b_im[g].T
            MA[rows, gl] = 1.0
            qg = blk * 4 + q
            prow = slice(par * 64, (par + 1) * 64)
            PB[prow, 0, qg] = a_re[g]; PB[prow, 1, qg] = a_im[g]; PB[prow, 2, qg] = ldt[g]
            CBm[prow, 0, qg, :] = c_re[g].T
            CBm[prow, 1, qg, :] = c_im[g].T
            BBm[prow, 0, qg, :] = b_re[g]
            BBm[prow, 1, qg, :] = b_im[g]
            MB[prow, q, gl] = 1.0
    out["PA"] = PA.reshape(128, 3, 256); out["BA"] = BA.reshape(128, 2, 256); out["MA"] = MA
    out["PB"] = PB; out["CB"] = CBm.reshape(128, 2, 256); out["BB"] = BBm.reshape(128, 2, 256); out["MB"] = MB
    out["ident"] = np.eye(128, dtype=np.float32)
    out["iota"] = np.ascontiguousarray(np.broadcast_to(np.arange(SEQ, dtype=np.float32), (128, SEQ)))
    return out


def make_in_maps(inp, SEQ, ncores):
    shared = _host_layouts(inp, SEQ)
    f = lambda a: np.ascontiguousarray(np.asarray(a, dtype=np.float32))
    maps = []
    for i in range(ncores):
        m = dict(shared)
        m["xp"] = f(inp["x_prompt"][i]); m["xs"] = f(inp["x_sample"][i * NS:(i + 1) * NS, 0])
        m["pp"] = f(inp["p_prompt"][0, i]); m["psm"] = f(inp["p_sample"][0, i * NS:(i + 1) * NS, 0])
        m["cache"] = f(inp["cache_conv"][0, i * NS:(i + 1) * NS]).reshape(NS * 30, D)
        m["st_re"] = f(inp["state_ssm_re"][0, i * NS:(i + 1) * NS]).reshape(NS, 2048)
        m["st_im"] = f(inp["state_ssm_im"][0, i * NS:(i + 1) * NS]).reshape(NS, 2048)
        maps.append(m)
    return maps


def assemble(results, SEQ, ncores):
    y_p = np.stack([r["y_p"] for r in results]).reshape(ncores, SEQ, D)
    y_s = np.concatenate([r["y_s"] for r in results]).reshape(ncores * NS, 1, D)
    conv_p = np.stack([r["conv_p"] for r in results]).reshape(1, ncores, 30, D)
    conv_s = np.concatenate([r["conv_s"].reshape(NS, 30, D) for r in results]).reshape(1, ncores * NS, 30, D)
    sre_p = np.stack([r["sre_p"] for r in results]).reshape(1, ncores, 32, 64)
    sim_p = np.stack([r["sim_p"] for r in results]).reshape(1, ncores, 32, 64)
    sre_s = np.concatenate([r["sre_s"] for r in results]).reshape(1, ncores * NS, 32, 64)
    sim_s = np.concatenate([r["sim_s"] for r in results]).reshape(1, ncores * NS, 32, 64)
    return tuple(np.ascontiguousarray(a, dtype=np.float32) for a in
                 (y_p, y_s, conv_p, conv_s, sre_p, sim_p, sre_s, sim_s))


def kernel(**inputs):
    SEQ = 2048
    n = 8
    nc = build_program(SEQ)
    in_maps = make_in_maps(inputs, SEQ, n)
    res = run_bass_kernel_spmd(nc, in_maps, core_ids=list(range(n)))
    return assemble(res.results, SEQ, n)
```

```python
import math
from contextlib import ExitStack
import numpy as np
import concourse.bass as bass
import concourse.mybir as mybir
from concourse.bass_utils import run_bass_kernel_spmd

F32 = mybir.dt.float32
BF16 = mybir.dt.bfloat16
AF = mybir.ActivationFunctionType
ALU = mybir.AluOpType

D = 1024
NS = 16
CK = 31
EPS = 1e-6
PI = math.pi
TWO_PI = 2.0 * math.pi
MAGIC = 12582912.0
SHR = 1.0 - 2e-6


def V(t, p0, np_, f0, dims):
    F = 1
    for s in t.shape[1:]:
        F *= s
    return bass.AP(t, p0 * F + f0, [[F, np_]] + [list(d) for d in dims])


def DR(t, off, dims):
    return bass.AP(t, off, [list(d) for d in dims])


class Prog:
    ENGS = ["tensor", "vector", "scalar", "gpsimd", "sync"]
    NDS = 8

    def __init__(self, nc, es):
        self.nc = nc
        self.q = {e: [] for e in self.ENGS}
        self.cnt = {e: 0 for e in self.ENGS}
        self.sem = {e: es.enter_context(nc.semaphore("s_" + e)) for e in self.ENGS}
        self.dsem = {}
        self.dcnt = {}
        self.dnext = {}
        for qn in ["sync", "gpsimd", "scalar"]:
            self.dsem[qn] = [es.enter_context(nc.semaphore("d_%s%d" % (qn, i))) for i in range(self.NDS)]
            self.dcnt[qn] = [0] * self.NDS
            self.dnext[qn] = 0
        self.last_w = {}
        self.readers = {}
        self.waited = {e: {} for e in self.ENGS}
        self.all_events = []

    def _deps(self, eng, R, W):
        deps = {}

        def add(ev, war=False):
            if ev is None:
                return
            key, val, src = ev[0], ev[1], ev[2]
            if src == eng and eng == "tensor":
                return
            if deps.get(key, (0, None))[0] < val:
                deps[key] = (val, ev[3])
        for r in R:
            add(self.last_w.get(r))
        for w in W:
            add(self.last_w.get(w))
            for ev in self.readers.get(w, []):
                add(ev, war=True)
        out = []
        for key, (val, semh) in deps.items():
            if self.waited[eng].get(key, 0) >= val:
                continue
            self.waited[eng][key] = val
            out.append((semh, val))
        return out

    def _commit(self, ev, R, W):
        for w in W:
            self.last_w[w] = ev
            self.readers[w] = []
        for r in R:
            self.readers.setdefault(r, []).append(ev)

    def op(self, eng, fn, R=(), W=()):
        waits = self._deps(eng, R, W)
        self.cnt[eng] += 1
        ev = ("e_" + eng, self.cnt[eng], eng, self.sem[eng])
        self.q[eng].append((waits, fn, self.sem[eng], 1))
        self._commit(ev, R, W)
        return ev

    def dma(self, qn, fn, R=(), W=()):
        waits = self._deps(qn, R, W)
        j = self.dnext[qn]
        self.dnext[qn] = (j + 1) % self.NDS
        n = self.dcnt[qn][j]
        key = "d_%s%d" % (qn, j)
        semh = self.dsem[qn][j]
        if n > 0 and self.waited[qn].get(key, 0) < 16 * n:
            self.waited[qn][key] = 16 * n
            waits.append((semh, 16 * n))
        self.dcnt[qn][j] = n + 1
        ev = (key, 16 * (n + 1), "dma_" + qn, semh)
        self.q[qn].append((waits, fn, semh, 16))
        self._commit(ev, R, W)
        self.all_events.append(ev)
        return ev

    def barrier(self, final=False):
        evs = []
        for e in self.ENGS:
            if self.cnt[e] > 0:
                evs.append(("e_" + e, self.cnt[e], e, self.sem[e]))
        for qn in self.dsem:
            for j in range(self.NDS):
                if self.dcnt[qn][j] > 0:
                    evs.append(("d_%s%d" % (qn, j), 16 * self.dcnt[qn][j], "dma_" + qn, self.dsem[qn][j]))
        for e in self.ENGS:
            if e == "tensor" and not final:
                continue
            waits = []
            for (key, val, src, semh) in evs:
                if src == e:
                    continue
                if self.waited[e].get(key, 0) >= val:
                    continue
                self.waited[e][key] = val
                waits.append((semh, val))
            if waits:
                self.q[e].append((waits, None, None, 0))

    def replay(self, block):
        def mk(e):
            def body(eng):
                for (waits, fn, semh, inc) in self.q[e]:
                    for (s, v) in waits:
                        eng.wait_ge(s, v)
                    if fn is not None:
                        fn(eng).then_inc(semh, inc)
            return body
        block.tensor(mk("tensor"))
        block.vector(mk("vector"))
        block.scalar(mk("scalar"))
        block.gpsimd(mk("gpsimd"))
        block.sync(mk("sync"))


class _Stop(Exception):
    pass


def build_program(SEQ, stop=None):
    TT = SEQ + NS
    NTB = SEQ // 512
    tblocks = [(i * 512, 512) for i in range(NTB)] + [(SEQ, NS)]
    ttiles = [(i * 128, 128) for i in range(SEQ // 128)] + [(SEQ, NS)]

    nc = bass.Bass("TRN2", target_bir_lowering=False)
    es = ExitStack()

    def din(name, shape):
        return nc.dram_tensor(name, list(shape), F32, kind="ExternalInput")

    def dout(name, shape):
        return nc.dram_tensor(name, list(shape), F32, kind="ExternalOutput")

    xp = din("xp", [SEQ, D]); xs = din("xs", [NS, D])
    pp = din("pp", [SEQ, 256]); psm = din("psm", [NS, 256])
    cache = din("cache", [NS * 30, D])
    st_re = din("st_re", [NS, 2048]); st_im = din("st_im", [NS, 2048])
    w_in = din("w_in", [D, 6144]); w_pc = din("w_pc", [D, D]); w_ps = din("w_ps", [512, D])
    w_glu = din("w_glu", [512, 512]); w_out = din("w_out", [D, D]); w_pg = din("w_pg", [D, D])
    w_ple = din("w_ple", [256, D])
    rows = din("rows", [4, D])
    colv8 = din("colv8", [128, 3, 8])
    colv4 = din("colv4", [128, 2, 4])
    convw = din("convw", [128, 8, CK])
    PA = din("PA", [128, 3, 256])
    BA = din("BA", [128, 2, 256])
    MA = din("MA", [128, 8])
    PB = din("PB", [128, 3, 16])
    CB = din("CB", [128, 2, 256])
    BB = din("BB", [128, 2, 256])
    MB = din("MB", [128, 4, 8])
    ident_d = din("ident", [128, 128])
    iota_d = din("iota", [128, SEQ])

    y_p = dout("y_p", [SEQ, D]); y_s = dout("y_s", [NS, D])
    conv_p = dout("conv_p", [30, D]); conv_s = dout("conv_s", [NS * 30, D])
    sre_p = dout("sre_p", [32, 64]); sim_p = dout("sim_p", [32, 64])
    sre_s = dout("sre_s", [NS, 2048]); sim_s = dout("sim_s", [NS, 2048])

    P = Prog(nc, es)
    stacks = []

    def ck(n):
        if stop is not None and n == stop:
            raise _Stop()

    def sb(name, free, dt=F32, stack=es):
        return stack.enter_context(nc.sbuf_tensor(name, [128] + list(free), dt))

    PS = [es.enter_context(nc.psum_tensor("ps%d" % i, [128, 512], F32)) for i in range(8)]
    psi = [0]

    def nextps():
        i = psi[0]
        psi[0] = (i + 1) % 8
        return i

    ident_b = sb("ident_b", [128], BF16)
    ident_f = sb("ident_f", [128], F32)
    ones_b = sb("ones_b", [128], BF16)
    cv8 = sb("cv8", [3, 8], F32)
    cv4 = sb("cv4", [2, 4], F32)
    m_t = sb("m_t", [8, TT], BF16)
    stW = ExitStack(); stacks.append(stW)
    NWB = 4
    WB = [sb("wb%d" % i, [8, 512], BF16, stW) for i in range(NWB)]

    def pipeline(stages, n, between=None):
        for t in range(n + len(stages) - 1):
            if between is not None:
                between()
            for si, st_fn in enumerate(stages):
                i = t - si
                if 0 <= i < n:
                    st_fn(i)
    wbi = [0]

    P.dma("sync", lambda e: e.dma_start(out=ident_f[:], in_=ident_d.ap()), W=["ident_f"])
    P.dma("gpsimd", lambda e: e.dma_start(out=ident_b[:], in_=ident_d.ap()), W=["ident_b"])
    P.dma("sync", lambda e: e.dma_start(out=cv8[:], in_=colv8.ap()), W=["cv8"])
    P.dma("sync", lambda e: e.dma_start(out=cv4[:], in_=colv4.ap()), W=["cv4"])
    P.op("gpsimd", lambda e: e.memset(ones_b[:], 1.0), W=["ones_b"])

    def load_w(wd, ncols_total, kc, c0, ncols):
        i = wbi[0]
        wbi[0] = (i + 1) % NWB
        buf = WB[i]
        P.dma("gpsimd", lambda e: e.dma_start(
            out=V(buf, 0, 128, 0, [[512, kc], [1, ncols]]),
            in_=DR(wd, c0, [[ncols_total, 128], [128 * ncols_total, kc], [1, ncols]])),
            W=[("wb", i)])
        return i

    pending = {}

    def prefetch(key, *args):
        pending[key] = load_w(*args)

    def getw(key, *args):
        if key in pending:
            return pending.pop(key)
        return load_w(*args)

    def mm_fm(wi, kc, col_in_buf, act, act_key, cs, cn, pbank):
        buf = WB[wi]
        actF = act.shape[2]
        for k in range(kc):
            P.op("tensor", lambda e, k=k: e.matmul(
                V(PS[pbank], 0, 128, 0, [[1, cn]]),
                V(buf, 0, 128, k * 512 + col_in_buf, [[1, 128]]),
                V(act, 0, 128, k * actF + cs, [[1, cn]]),
                start=(k == 0), stop=(k == kc - 1)),
                R=[("wb", wi), act_key], W=[("ps", pbank)])

    try:
        stH = ExitStack(); stacks.append(stH)
        hT = sb("hT", [8, TT], BF16, stH)
        stC = ExitStack(); stacks.append(stC)
        sx = sb("sx", [4, TT], BF16, stC)
        wi = load_w(w_in, 6144, 8, 3072, 512)
        TC = SEQ // 8
        stS = ExitStack(); stacks.append(stS)
        PBt = sb("PBt", [3, 16], F32, stS)
        tB = [sb("tB%d" % i, [16], F32, stS) for i in range(13)]
        LpB = sb("LpB", [2, 9, 16], F32, stS)
        Gc = sb("Gc", [8, 2, 256], F32, stS)
        Hc = sb("Hc", [2, 9, 256], BF16, stS)
        BbB = sb("BbB", [2, 256], BF16, stS)
        mA = sb("mA", [8], F32, stS)
        mB = sb("mB", [4, 8], BF16, stS)
        fin = sb("fin", [2, 16], F32, stS)
        sS = sb("sS", [2, 256], F32, stS)
        sN = sb("sN", [2, 256], F32, stS)
        sNb = sb("sNb", [2, 256], BF16, stS)
        sT1 = sb("sT1", [256], F32, stS)
        sT2 = sb("sT2", [256], F32, stS)
        P.dma("sync", lambda e: e.dma_start(out=PBt[:], in_=PB.ap()), W=["PBt"])
        P.dma("sync", lambda e: e.dma_start(out=mA[:], in_=MA.ap()), W=["mA"])
        P.dma("gpsimd", lambda e: e.dma_start(out=mB[:], in_=MB.ap()), W=["mB"])
        stP = ExitStack(); stacks.append(stP)
        PAt = sb("PAt", [3, 256], F32, stP)
        BAt = sb("BAt", [2, 256], F32, stP)
        CBt = sb("CBt", [2, 256], F32, stP)
        BBt = sb("BBt", [2, 256], F32, stP)
        tA = [sb("tA%d" % i, [256], F32, stP) for i in range(9)]
        big0 = sb("big0", [9 * 256], F32, stP)
        stage = big0
        big1 = sb("big1", [9 * 256], F32, stP)
        P.dma("sync", lambda e: e.dma_start(out=PAt[:], in_=PA.ap()), W=["PAt"])
        P.dma("sync", lambda e: e.dma_start(out=BAt[:], in_=BA.ap()), W=["BAt"])
        P.dma("sync", lambda e: e.dma_start(out=CBt[:], in_=CB.ap()), W=["CBt"])
        P.dma("sync", lambda e: e.dma_start(out=BBt[:], in_=BB.ap()), W=["BBt"])

        Gkeys = rho8 = tau8 = None

        def prep_gen():
            nonlocal Gkeys, rho8, tau8
            def lam_prep(par, n, T, pk, pre):
                a_re = V(par, 0, 128, 0, [[1, n]]); a_im = V(par, 0, 128, n, [[1, n]]); ldt = V(par, 0, 128, 2 * n, [[1, n]])
                dt_, mag, th, cs_, sn_, tmp = T[0], T[1], T[2], T[3], T[4], T[5]
                P.op("scalar", lambda e: e.activation(out=dt_[:], in_=ldt, func=AF.Exp), R=[pk], W=[pre + "dt"])
                P.op("vector", lambda e: e.tensor_tensor(mag[:], a_re, dt_[:], ALU.mult), R=[pk, pre + "dt"], W=[pre + "mag"])
                P.op("scalar", lambda e: e.activation(out=mag[:], in_=mag[:], func=AF.Exp), R=[pre + "mag"], W=[pre + "mag"])
                P.op("vector", lambda e: e.scalar_tensor_tensor(th[:], a_im, 1.0 / TWO_PI, dt_[:], ALU.mult, ALU.mult), R=[pk, pre + "dt"], W=[pre + "th"])
                P.op("vector", lambda e: e.tensor_scalar(tmp[:], th[:], MAGIC, -MAGIC, ALU.add, ALU.add), R=[pre + "th"], W=[pre + "tmp"])
                P.op("vector", lambda e: e.tensor_tensor(th[:], th[:], tmp[:], ALU.subtract), R=[pre + "th", pre + "tmp"], W=[pre + "th"])
                P.op("scalar", lambda e: e.activation(out=sn_[:], in_=th[:], func=AF.Sin, scale=TWO_PI * SHR), R=[pre + "th"], W=[pre + "sin"])
                P.op("scalar", lambda e: e.activation(out=tmp[:], in_=th[:], func=AF.Sin, scale=PI * SHR), R=[pre + "th"], W=[pre + "tmp"])
                P.op("scalar", lambda e: e.activation(out=tmp[:], in_=tmp[:], func=AF.Square, scale=math.sqrt(2.0)), R=[pre + "tmp"], W=[pre + "tmp"])
                P.op("scalar", lambda e: e.activation(out=cs_[:], in_=tmp[:], func=AF.Identity, scale=-1.0, bias=1.0), R=[pre + "tmp"], W=[pre + "cos"])
                return mag, th, cs_, sn_

            def f_prep(par, n, T, mag, cs_, sn_, pk, pre):
                a_re = V(par, 0, 128, 0, [[1, n]]); a_im = V(par, 0, 128, n, [[1, n]])
                lr, li, den, t7, nr = T[0], T[5], T[6], T[7], T[8]
                P.op("vector", lambda e: e.tensor_tensor(lr[:], mag[:], cs_[:], ALU.mult), R=[pre + "mag", pre + "cos", pre + "dt"], W=[pre + "dt"])
                P.op("vector", lambda e: e.tensor_tensor(li[:], mag[:], sn_[:], ALU.mult), R=[pre + "mag", pre + "sin"], W=[pre + "tmp"])
                P.op("vector", lambda e: e.tensor_scalar(nr[:], lr[:], -1.0, None, ALU.add), R=[pre + "dt"], W=[pre + "nr"])
                P.op("vector", lambda e: e.tensor_tensor(den[:], a_re, a_re, ALU.mult), R=[pk], W=[pre + "den"])
                P.op("vector", lambda e: e.tensor_tensor(t7[:], a_im, a_im, ALU.mult), R=[pk], W=[pre + "t7"])
                P.op("vector", lambda e: e.tensor_tensor(den[:], den[:], t7[:], ALU.add), R=[pre + "den", pre + "t7"], W=[pre + "den"])
                P.op("vector", lambda e: e.reciprocal(den[:], den[:]), R=[pre + "den"], W=[pre + "den"])
                fr, fi = T[3], T[4]
                P.op("vector", lambda e: e.tensor_tensor(fr[:], nr[:], a_re, ALU.mult), R=[pre + "nr", pk], W=[pre + "cos"])
                P.op("vector", lambda e: e.tensor_tensor(t7[:], li[:], a_im, ALU.mult), R=[pre + "tmp", pk, pre + "den"], W=[pre + "t7"])
                P.op("vector", lambda e: e.tensor_tensor(fr[:], fr[:], t7[:], ALU.add), R=[pre + "cos", pre + "t7"], W=[pre + "cos"])
                P.op("vector", lambda e: e.tensor_tensor(fr[:], fr[:], den[:], ALU.mult), R=[pre + "cos", pre + "den"], W=[pre + "cos"])
                P.op("vector", lambda e: e.tensor_tensor(fi[:], li[:], a_re, ALU.mult), R=[pre + "tmp", pk], W=[pre + "sin"])
                P.op("vector", lambda e: e.tensor_tensor(t7[:], nr[:], a_im, ALU.mult), R=[pre + "nr", pk, pre + "cos"], W=[pre + "t7"])
                P.op("vector", lambda e: e.tensor_tensor(fi[:], fi[:], t7[:], ALU.subtract), R=[pre + "sin", pre + "t7"], W=[pre + "sin"])
                P.op("vector", lambda e: e.tensor_tensor(fi[:], fi[:], den[:], ALU.mult), R=[pre + "sin", pre + "den"], W=[pre + "sin"])
                return lr, li, fr, fi

            def cmul(o_re, o_im, a_re, a_im, b_re, b_im, t0, t1, R, Wre, Wim, neg_im=False):
                P.op("vector", lambda e: e.tensor_tensor(t0, a_re, b_re, ALU.mult), R=R, W=["cm_t0"])
                P.op("vector", lambda e: e.tensor_tensor(t1, a_im, b_im, ALU.mult), R=R, W=["cm_t1"])
                P.op("vector", lambda e: e.tensor_tensor(o_re, t0, t1, ALU.subtract), R=["cm_t0", "cm_t1"], W=Wre)
                P.op("vector", lambda e: e.tensor_tensor(t0, a_re, b_im, ALU.mult), R=R + Wre, W=["cm_t0"])
                P.op("vector", lambda e: e.tensor_tensor(t1, a_im, b_re, ALU.mult), R=R + Wre, W=["cm_t1"])
                if neg_im:
                    P.op("vector", lambda e: e.scalar_tensor_tensor(o_im, t0, -1.0, t1, ALU.mult, ALU.subtract), R=["cm_t0", "cm_t1"], W=Wim)
                else:
                    P.op("vector", lambda e: e.tensor_tensor(o_im, t0, t1, ALU.add), R=["cm_t0", "cm_t1"], W=Wim)

            magA, tauA, cosA, sinA = lam_prep(PAt, 256, tA, "PAt", "A_")
            yield
            lrA, liA, frA, fiA = f_prep(PAt, 256, tA, magA, cosA, sinA, "PAt", "A_")
            yield

            def gk(k, comp):
                return V(Gc, 0, 128, (k * 2 + comp) * 256, [[1, 256]])
            b0 = V(big0, 0, 128, 0, [[1, 256]]); b1 = V(big1, 0, 128, 0, [[1, 256]])
            cmul(gk(0, 0), gk(0, 1), frA[:], fiA[:], V(BAt, 0, 128, 0, [[1, 256]]), V(BAt, 0, 128, 256, [[1, 256]]), b0, b1,
                 ["A_cos", "A_sin", "BAt"], [("Gc", 0, 0)], [("Gc", 0, 1)])
            for k in range(1, 8):
                cmul(gk(k, 0), gk(k, 1), gk(k - 1, 0), gk(k - 1, 1), lrA[:], liA[:], b0, b1,
                     [("Gc", k - 1, 0), ("Gc", k - 1, 1), "A_dt", "A_tmp"], [("Gc", k, 0)], [("Gc", k, 1)])
                yield
            Gkeys = [("Gc", k, c) for k in range(8) for c in range(2)]
            magB, tauB, cosB, sinB = lam_prep(PBt, 16, tB, "PBt", "B_")
            yield
            lrB, liB, frB, fiB = f_prep(PBt, 16, tB, magB, cosB, sinB, "PBt", "B_")
            yield

            def lp(k, comp):
                return V(LpB, 0, 128, (comp * 9 + k) * 16, [[1, 16]])
            P.op("vector", lambda e: e.memset(lp(0, 0), 1.0), W=[("LpB", 0)])
            P.op("vector", lambda e: e.memset(lp(0, 1), 0.0), R=[("LpB", 0)], W=[("LpB", 0)])
            P.op("vector", lambda e: e.tensor_copy(lp(1, 0), lrB[:]), R=["B_dt"], W=[("LpB", 1)])
            P.op("vector", lambda e: e.tensor_copy(lp(1, 1), liB[:]), R=["B_tmp", ("LpB", 1)], W=[("LpB", 1)])
            tb0 = tB[9][:]; tb1 = tB[10][:]
            for k in range(2, 9):
                cmul(lp(k, 0), lp(k, 1), lp(k - 1, 0), lp(k - 1, 1), lrB[:], liB[:], tb0, tb1,
                     [("LpB", k - 1), "B_dt", "B_tmp"], [("LpB", k)], [("LpB", k)])
                yield
            LpKeys = [("LpB", k) for k in range(9)]
            rho8 = tB[11]; tau8 = tB[12]
            P.op("vector", lambda e: e.tensor_tensor(rho8[:], magB[:], magB[:], ALU.mult), R=["B_mag"], W=["rho8"])
            P.op("vector", lambda e: e.tensor_tensor(rho8[:], rho8[:], rho8[:], ALU.mult), R=["rho8"], W=["rho8"])
            P.op("vector", lambda e: e.tensor_tensor(rho8[:], rho8[:], rho8[:], ALU.mult), R=["rho8"], W=["rho8"])
            P.op("vector", lambda e: e.tensor_scalar(tau8[:], tauB[:], 8.0, None, ALU.mult), R=["B_th"], W=["tau8"])
            P.op("vector", lambda e: e.tensor_scalar(tb0, tau8[:], MAGIC, -MAGIC, ALU.add, ALU.add), R=["tau8"] + LpKeys, W=["cm_t0"])
            P.op("vector", lambda e: e.tensor_tensor(tau8[:], tau8[:], tb0, ALU.subtract), R=["tau8", "cm_t0"], W=["tau8"])
            def bc16(t):
                return V(t, 0, 128, 0, [[1, 16], [0, 16]])

            def q16(t, comp):
                return V(t, 0, 128, comp * 256, [[16, 16], [1, 16]])
            g0 = V(big0, 0, 128, 0, [[16, 16], [1, 16]]); g1 = V(big1, 0, 128, 0, [[16, 16], [1, 16]])
            cmul(q16(BbB, 0), q16(BbB, 1), bc16(frB), bc16(fiB), q16(BBt, 0), q16(BBt, 1), g0, g1,
                 ["B_cos", "B_sin", "BBt"], ["BbB0"], ["BbB1"])
            def lpb(comp):
                return V(LpB, 0, 128, comp * 144, [[16, 9], [1, 16], [0, 16]])

            def cbb(comp):
                return V(CBt, 0, 128, comp * 256, [[0, 9], [16, 16], [1, 16]])

            def hcv(comp):
                return V(Hc, 0, 128, comp * 2304, [[256, 9], [16, 16], [1, 16]])
            h0 = V(big0, 0, 128, 0, [[256, 9], [16, 16], [1, 16]]); h1 = V(big1, 0, 128, 0, [[256, 9], [16, 16], [1, 16]])
            cmul(hcv(0), hcv(1), cbb(0), cbb(1), lpb(0), lpb(1), h0, h1, ["CBt"] + LpKeys, ["Hc0"], ["Hc1"], neg_im=True)
            yield

            for comp, sd in ((0, st_re), (1, st_im)):
                P.dma("sync", lambda e, sd=sd: e.dma_start(out=V(stage, 0, NS, 0, [[1, 2048]]), in_=sd.ap()), W=["cm_t0"])
                pb = nextps()
                for qg in range(16):
                    P.op("tensor", lambda e, qg=qg, pb=pb: e.matmul(
                        V(PS[pb], 0, 128, qg * NS, [[1, NS]]),
                        V(stage, 0, NS, qg * 128, [[1, 128]]),
                        V(ident_f, 0, NS, 0, [[1, NS]]), start=True, stop=True),
                        R=["cm_t0", "ident_f"], W=[("ps", pb)])
                P.op("vector", lambda e, comp=comp, pb=pb: e.tensor_copy(
                    V(sS, 0, 128, comp * 256, [[1, 256]]), V(PS[pb], 0, 128, 0, [[1, 256]])),
                    R=[("ps", pb)], W=["sS%d" % comp])

            def bcB(t):
                return V(t, 0, 128, 0, [[1, 16], [0, NS]])

            def s3(t, comp):
                return V(t, 0, 128, comp * 256, [[NS, 16], [1, NS]])

            def s3t(t):
                return V(t, 0, 128, 0, [[NS, 16], [1, NS]])
            cmul(s3(sN, 0), s3(sN, 1), s3(sS, 0), s3(sS, 1), bcB(lrB), bcB(liB), s3t(sT1), s3t(sT2),
                 ["sS0", "sS1", "B_dt", "B_tmp"], ["sN0"], ["sN1"])

        pgen = prep_gen()
        stA = ExitStack(); stacks.append(stA)
        xt = [sb("xt%d" % i, [D], F32, stA) for i in range(3)]
        hb = [sb("hb%d" % i, [D], BF16, stA) for i in range(2)]
        junk = sb("junk", [D], BF16, stA)
        ssA = sb("ssA", [40], F32, stA)
        gpre = sb("gpre", [D], F32, stA)
        P.dma("sync", lambda e: e.dma_start(out=gpre[:], in_=DR(rows, 0, [[0, 128], [1, D]])), W=["gpre"])

        def sA1(ti):
            r0, nt = ttiles[ti]; b = ti % 3
            src = DR(xp, r0 * D, [[D, nt], [1, D]]) if r0 < SEQ else DR(xs, 0, [[D, nt], [1, D]])
            P.dma("sync", lambda e: e.dma_start(out=V(xt[b], 0, nt, 0, [[1, D]]), in_=src), W=[("xt", b)])
            P.op("scalar", lambda e: e.activation(
                out=V(junk, 0, nt, 0, [[1, D]]), in_=V(xt[b], 0, nt, 0, [[1, D]]), func=AF.Square,
                accum_out=V(ssA, 0, nt, ti, [[1, 1]])), R=[("xt", b)], W=["junk", ("ssA", ti)])
            P.op("scalar", lambda e: e.activation(
                out=V(ssA, 0, nt, ti, [[1, 1]]), in_=V(ssA, 0, nt, ti, [[1, 1]]), func=AF.Ln, scale=1.0 / D, bias=EPS),
                R=[("ssA", ti)], W=[("ssA", ti)])

        def sA2(ti):
            r0, nt = ttiles[ti]; b = ti % 3; bh = ti % 2
            P.op("scalar", lambda e: e.activation(
                out=V(ssA, 0, nt, ti, [[1, 1]]), in_=V(ssA, 0, nt, ti, [[1, 1]]), func=AF.Exp, scale=-0.5),
                R=[("ssA", ti)], W=[("ssA", ti)])
            P.op("vector", lambda e: e.scalar_tensor_tensor(
                V(hb[bh], 0, nt, 0, [[1, D]]), V(xt[b], 0, nt, 0, [[1, D]]), V(ssA, 0, nt, ti, [[1, 1]]),
                V(gpre, 0, nt, 0, [[1, D]]), ALU.mult, ALU.mult),
                R=[("xt", b), ("ssA", ti), "gpre"], W=[("hb", bh)])

        def sA3(ti):
            r0, nt = ttiles[ti]; b = ti % 2
            for half in range(2):
                pb = nextps()
                for kk in range(4):
                    k = half * 4 + kk
                    P.op("tensor", lambda e, k=k, kk=kk, pb=pb: e.matmul(
                        V(PS[pb], 0, 128, kk * nt, [[1, nt]]),
                        V(hb[b], 0, nt, k * 128, [[1, 128]]),
                        V(ident_b, 0, nt, 0, [[1, nt]]), start=True, stop=True),
                        R=[("hb", b), "ident_b"], W=[("ps", pb)])
                P.op("scalar", lambda e, half=half, pb=pb: e.activation(
                    out=V(hT, 0, 128, half * 4 * TT + r0, [[TT, 4], [1, nt]]),
                    in_=V(PS[pb], 0, 128, 0, [[nt, 4], [1, nt]]), func=AF.Copy),
                    R=[("ps", pb)], W=[("hT", half, ti)])
        pipeline([sA1, sA2, sA3], len(ttiles), between=lambda: next(pgen, None))
        for _ in pgen:
            pass
        ck(1)

        for (cs, cn) in tblocks:
            tis = [ti for ti, (r0, nt) in enumerate(ttiles) if cs <= r0 < cs + cn]
            hkeys = [("hT", h, ti) for h in range(2) for ti in tis]
            for oc in range(4):
                pb = nextps()
                for k in range(8):
                    mov = V(hT, 0, 128, k * TT + cs, [[1, cn]])
                    P.op("tensor", lambda e, k=k, oc=oc, cn=cn, pb=pb, mov=mov: e.matmul(
                        V(PS[pb], 0, 128, 0, [[1, cn]]),
                        V(WB[wi], 0, 128, k * 512 + oc * 128, [[1, 128]]),
                        mov, start=(k == 0), stop=(k == 7)),
                        R=[("wb", wi)] + hkeys, W=[("ps", pb)])
                dst = (V(sx, 0, 128, oc * TT + cs // 8, [[SEQ // 8, 8], [1, cn // 8]]) if cs < SEQ
                       else V(sx, 0, 128, oc * TT + cs, [[1, cn]]))
                src_ = (V(PS[pb], 0, 128, 0, [[1, 8], [8, cn // 8]]) if cs < SEQ else V(PS[pb], 0, 128, 0, [[1, cn]]))
                P.op("scalar", lambda e, dst=dst, src_=src_: e.activation(out=dst, in_=src_, func=AF.Copy),
                     R=[("ps", pb)], W=[("sx", oc) if cs < SEQ else ("sxs", oc)])
        ck(2)
        ck(3)
        P.barrier()
        stA.close()
        stP.close()

        stL = ExitStack(); stacks.append(stL)
        SLi = sb("SLi", [8, 2, 512], BF16, stL)
        YSk = sb("YSk", [4, 2, 9, 128], BF16, stL)
        BBs = sb("BBs", [4, 2, 128], BF16, stL)
        BDk = sb("BDk", [8, 128], BF16, stL)
        Spv = sb("Spv", [2, 4, 2, TC], BF16, stL)
        iot = sb("iot", [TC], F32, stL)
        tcs = [sb("tcos%d" % i, [TC], F32, stL) for i in range(2)]
        tsn = [sb("tsin%d" % i, [TC], F32, stL) for i in range(2)]
        Tg = [sb("Tg%d" % i, [TC], F32, stL) for i in range(2)]
        Wk = [sb("Wk%d" % i, [TC], F32, stL) for i in range(5)]
        Pk = [Wk[0], Wk[1]]
        ytmp = sb("ytmp", [NS], F32, stL)
        P.dma("sync", lambda e: e.dma_start(out=iot[:], in_=DR(iota_d, 0, [[SEQ, 128], [1, TC]])), W=["iot"])
        P.op("gpsimd", lambda e: e.memset(V(Spv, 0, 128, 0, [[TC, 16], [1, 1]]), 0.0), W=["Spv_z"])
        P.op("gpsimd", lambda e: e.memset(YSk[:], 0.0), W=["YSk_z"])
        P.op("gpsimd", lambda e: e.memset(BBs[:], 0.0), W=["BBs_z"])
        ysg = sx
        YB = [0, 1, 2, 3]; SLB = [4, 5]; BUS = 6; YSB = 7
        slit = [0]
        def gen_tables(qg):
            sl = qg % 2
            t8col = V(tau8, 0, 128, qg, [[1, 1]])
            P.op("vector", lambda e: e.tensor_scalar(Tg[1][:], iot[:], t8col, None, ALU.mult), R=["iot", "tau8"], W=["tg1"])
            P.op("vector", lambda e: e.tensor_scalar(Tg[0][:], Tg[1][:], MAGIC, -MAGIC, ALU.add, ALU.add), R=["tg1"], W=["tg0"])
            P.op("vector", lambda e: e.tensor_tensor(Tg[1][:], Tg[1][:], Tg[0][:], ALU.subtract), R=["tg1", "tg0"], W=["tg1"])
            P.op("scalar", lambda e: e.activation(out=tsn[sl][:], in_=Tg[1][:], func=AF.Sin, scale=TWO_PI * SHR), R=["tg1"], W=[("tsin", sl)])
            P.op("scalar", lambda e: e.activation(out=Tg[0][:], in_=Tg[1][:], func=AF.Sin, scale=PI * SHR), R=["tg1"], W=["tg0"])
            P.op("scalar", lambda e: e.activation(out=Tg[0][:], in_=Tg[0][:], func=AF.Square, scale=math.sqrt(2.0)), R=["tg0"], W=["tg0"])
            P.op("scalar", lambda e: e.activation(out=tcs[sl][:], in_=Tg[0][:], func=AF.Identity, scale=-1.0, bias=1.0), R=["tg0"], W=[("tcos", sl)])

        def expand_sli(blk):
            for g2 in range(8):
                P.op("scalar", lambda e, g2=g2: e.activation(
                    out=V(SLi, 0, 128, g2 * 64, [[512, 16], [1, 64]]),
                    in_=V(Gc, 0, 128, blk * 64, [[256, 16], [1, 64]]), func=AF.Identity,
                    scale=V(mA, 0, 128, g2, [[1, 1]])),
                    R=Gkeys + ["mA"], W=[("SLi", k) for k in range(8)])

        slb_of = {}

        def s_local(blk, q):
            pbs = SLB[slit[0] % 2]; slit[0] += 1
            slb_of[(blk, q)] = pbs
            for comp in range(2):
                for i in range(8):
                    P.op("tensor", lambda e, comp=comp, i=i: e.matmul(
                        V(PS[pbs], 0, 128, comp * TC, [[1, TC]]),
                        V(SLi, 0, 128, ((7 - i) * 2 + comp) * 512 + q * 128, [[1, 128]]),
                        V(sx, 0, 128, blk * TT + i * TC, [[1, TC]]), start=(i == 0), stop=(i == 7)),
                        R=[("SLi", 7 - i), ("sx", blk)], W=[("ps", pbs)])

        def sample_bu(blk):
            for q in range(4):
                for comp in range(2):
                    P.op("tensor", lambda e, comp=comp, q=q: e.matmul(
                        V(PS[BUS], 0, 128, (blk % 2) * 128 + (q * 2 + comp) * NS, [[1, NS]]),
                        V(SLi, 0, 128, (0 * 2 + comp) * 512 + q * 128, [[1, 128]]),
                        V(sx, 0, 128, blk * TT + SEQ, [[1, NS]]), start=True, stop=True),
                        R=[("SLi", 0), ("sxs", blk)], W=[("ps", BUS)])

        expand_sli(0)
        sample_bu(0)
        s_local(0, 0)
        for blk in range(4):
            for q in range(4):
                qg = blk * 4 + q
                for comp in range(2):
                    for par in range(2):
                        P.op("gpsimd", lambda e, q=q, qg=qg, comp=comp, par=par: e.tensor_copy(
                            V(YSk, 64 * par, 64, (q * 2 + comp) * 9 * 128 + (2 * q + par) * 16, [[128, 9], [1, 16]]),
                            V(Hc, 64 * par, 64, comp * 2304 + qg * 16, [[256, 9], [1, 16]])),
                            R=["Hc%d" % comp, "YSk_z"], W=["YSk"])
                        P.op("gpsimd", lambda e, q=q, qg=qg, comp=comp, par=par: e.tensor_copy(
                            V(BBs, 64 * par, 64, (q * 2 + comp) * 128 + (2 * q + par) * 16, [[1, 16]]),
                            V(BbB, 64 * par, 64, comp * 256 + qg * 16, [[1, 16]])),
                            R=["BbB%d" % comp, "BBs_z"], W=["BBs"])
            ck(4 if blk == 0 else -1)
            dcol = V(cv4, 0, 128, 4 + blk, [[1, 1]])
            ck(5 if blk == 0 else -1)
            for q in range(4):
                qg = blk * 4 + q
                pbs = slb_of[(blk, q)]
                if q < 3:
                    s_local(blk, q + 1)
                elif blk < 3:
                    expand_sli(blk + 1)
                    sample_bu(blk + 1)
                    s_local(blk + 1, 0)
                if qg == 0:
                    gen_tables(0)
                if qg + 1 < 16:
                    gen_tables(qg + 1)
                sl = qg % 2
                tcos = tcs[sl]; tsin = tsn[sl]
                kc_ = ("tcos", sl); ks_ = ("tsin", sl)
                br = V(PS[pbs], 0, 128, 0, [[1, TC]]); bi = V(PS[pbs], 0, 128, TC, [[1, TC]])
                kp = ("ps", pbs)
                P.op("vector", lambda e, br=br, tcos=tcos, tsin=tsin: e.tensor_tensor(Wk[0][:], tcos[:], br, ALU.mult), R=[kc_, kp], W=["w0"])
                P.op("vector", lambda e, bi=bi, tcos=tcos, tsin=tsin: e.tensor_tensor(Wk[1][:], tsin[:], bi, ALU.mult), R=[ks_, kp], W=["w1"])
                P.op("vector", lambda e: e.tensor_tensor(Wk[0][:], Wk[0][:], Wk[1][:], ALU.add), R=["w0", "w1"], W=["w0"])
                P.op("vector", lambda e, bi=bi, tcos=tcos, tsin=tsin: e.tensor_tensor(Wk[2][:], tcos[:], bi, ALU.mult), R=[kc_, kp], W=["w2"])
                P.op("vector", lambda e, br=br, tcos=tcos, tsin=tsin: e.tensor_tensor(Wk[1][:], tsin[:], br, ALU.mult), R=[ks_, kp], W=["w1"])
                P.op("vector", lambda e: e.tensor_tensor(Wk[2][:], Wk[2][:], Wk[1][:], ALU.subtract), R=["w2", "w1"], W=["w2"])
                rbc = V(rho8, 0, 128, qg, [[0, TC]])
                P.op("vector", lambda e, rbc=rbc: e.tensor_tensor_scan(Wk[3][:], rbc, Wk[0][:], 0.0, ALU.mult, ALU.add), R=["w0", "rho8"], W=["wk3"])
                P.op("vector", lambda e, rbc=rbc: e.tensor_tensor_scan(Wk[4][:], rbc, Wk[2][:], 0.0, ALU.mult, ALU.add), R=["w2", "rho8"], W=["wk4"])
                P.op("vector", lambda e, tcos=tcos: e.tensor_tensor(Wk[0][:], tcos[:], Wk[3][:], ALU.mult), R=[kc_, "wk3"], W=["w0"])
                P.op("vector", lambda e, tsin=tsin: e.tensor_tensor(Wk[1][:], tsin[:], Wk[4][:], ALU.mult), R=[ks_, "wk4"], W=["w1"])
                P.op("vector", lambda e, q=q, blk=blk: e.tensor_tensor(V(Spv, 0, 128, ((blk % 2) * 8 + q * 2 + 0) * TC + 1, [[1, TC - 1]]),
                                                              V(Wk[0], 0, 128, 0, [[1, TC - 1]]), V(Wk[1], 0, 128, 0, [[1, TC - 1]]), ALU.subtract),
                     R=["w0", "w1", "Spv_z"], W=[("Spv", blk % 2, q, 0)])
                P.op("vector", lambda e, qg=qg: e.tensor_tensor(V(fin, 0, 128, qg, [[1, 1]]), V(Wk[0], 0, 128, TC - 1, [[1, 1]]),
                                                                V(Wk[1], 0, 128, TC - 1, [[1, 1]]), ALU.subtract),
                     R=["w0", "w1"], W=[("fin", 0, qg)])
                P.op("vector", lambda e, tsin=tsin: e.tensor_tensor(Pk[0][:], tsin[:], Wk[3][:], ALU.mult), R=[ks_, "wk3"], W=["w0"])
                P.op("vector", lambda e, tcos=tcos: e.tensor_tensor(Pk[1][:], tcos[:], Wk[4][:], ALU.mult), R=[kc_, "wk4"], W=["w1"])
                P.op("vector", lambda e, q=q, blk=blk: e.tensor_tensor(V(Spv, 0, 128, ((blk % 2) * 8 + q * 2 + 1) * TC + 1, [[1, TC - 1]]),
                                                              V(Pk[0], 0, 128, 0, [[1, TC - 1]]), V(Pk[1], 0, 128, 0, [[1, TC - 1]]), ALU.add),
                     R=["w0", "w1", "Spv_z"], W=[("Spv", blk % 2, q, 1)])
                P.op("vector", lambda e, qg=qg: e.tensor_tensor(V(fin, 0, 128, 16 + qg, [[1, 1]]), V(Pk[0], 0, 128, TC - 1, [[1, 1]]),
                                                                V(Pk[1], 0, 128, TC - 1, [[1, 1]]), ALU.add),
                     R=["w0", "w1"], W=[("fin", 1, qg)])
                ck(6 if (blk == 0 and q == 0) else -1)
                for comp in range(2):
                    P.op("vector", lambda e, comp=comp, qg=qg, q=q, blk=blk: e.tensor_tensor(
                        V(sN, 0, 128, comp * 256 + qg * NS, [[1, NS]]), V(sN, 0, 128, comp * 256 + qg * NS, [[1, NS]]),
                        V(PS[BUS], 0, 128, (blk % 2) * 128 + (q * 2 + comp) * NS, [[1, NS]]), ALU.add),
                        R=["sN%d" % comp, ("ps", BUS)], W=["sN%d" % comp])
                    P.op("vector", lambda e, comp=comp, qg=qg: e.tensor_copy(
                        V(sNb, 0, 128, comp * 256 + qg * NS, [[1, NS]]), V(sN, 0, 128, comp * 256 + qg * NS, [[1, NS]])),
                        R=["sN%d" % comp], W=["sNb%d" % comp])
                for comp in range(2):
                    P.op("tensor", lambda e, comp=comp, qg=qg, q=q: e.matmul(
                        V(PS[YSB], 0, 128, 0, [[1, NS]]),
                        V(YSk, 0, 128, ((q * 2 + comp) * 9 + 0) * 128, [[1, 128]]),
                        V(sNb, 0, 128, comp * 256 + qg * NS, [[1, NS]]),
                        start=(q == 0 and comp == 0), stop=(q == 3 and comp == 1)),
                        R=["YSk", "sNb%d" % comp], W=[("ps", YSB)])
            for hb_ in range(2):
                ck(43 if (blk == 0 and hb_ == 1) else -1)
                pbk = YB[hb_]
                for k4 in range(4):
                    k = hb_ * 4 + k4
                    for q in range(4):
                        for comp in range(2):
                            P.op("tensor", lambda e, k=k, k4=k4, q=q, comp=comp, pbk=pbk: e.matmul(
                                V(PS[pbk], 0, 128, k4 * 128, [[1, 128]]),
                                V(BBs, 0, 128, (q * 2 + comp) * 128, [[1, 128]]),
                                V(YSk, 0, 128, ((q * 2 + comp) * 9 + k) * 128, [[1, 128]]),
                                start=(q == 0 and comp == 0), stop=(q == 3 and comp == 1)),
                                R=["BBs", "YSk"], W=[("ps", pbk)])
                ck(41 if (blk == 0 and hb_ == 0) else -1)
                if hb_ == 0:
                    P.op("vector", lambda e, pbk=pbk, dcol=dcol: e.scalar_tensor_tensor(
                        V(BDk, 0, 128, 0, [[1, 128]]), ident_f[:], dcol, V(PS[pbk], 0, 128, 0, [[1, 128]]), ALU.mult, ALU.add),
                        R=[("ps", pbk), "ident_f", "cv4"], W=["BDk0"])
                    ck(42 if blk == 0 else -1)
                    P.op("vector", lambda e, pbk=pbk: e.tensor_copy(
                        V(BDk, 0, 128, 128, [[1, 384]]), V(PS[pbk], 0, 128, 128, [[1, 384]])),
                        R=[("ps", pbk)], W=["BDk1"])
                else:
                    ck(44 if blk == 0 else -1)
                    P.op("vector", lambda e, pbk=pbk: e.tensor_copy(
                        V(BDk, 0, 128, 512, [[1, 512]]), V(PS[pbk], 0, 128, 0, [[1, 512]])),
                        R=[("ps", pbk)], W=["BDk2"])
            ck(7 if blk == 0 else -1)
            spk = [("Spv", blk % 2, q, comp) for q in range(4) for comp in range(2)]
            for j in range(8):
                yb = YB[j // 2]
                nmm = (j + 1) + 8
                cnt = 0
                for i in range(j + 1):
                    P.op("tensor", lambda e, i=i, j=j, yb=yb, blk=blk, cnt=cnt, nmm=nmm: e.matmul(
                        V(PS[yb], 0, 128, (j % 2) * TC, [[1, TC]]),
                        V(BDk, 0, 128, (j - i) * 128, [[1, 128]]),
                        V(sx, 0, 128, blk * TT + i * TC, [[1, TC]]), start=(cnt == 0), stop=(cnt == nmm - 1)),
                        R=["BDk0", "BDk1", "BDk2", ("sx", blk)], W=[("ps", yb)])
                    cnt += 1
                for q in range(4):
                    for comp in range(2):
                        P.op("tensor", lambda e, q=q, comp=comp, j=j, yb=yb, cnt=cnt, nmm=nmm, blk=blk: e.matmul(
                            V(PS[yb], 0, 128, (j % 2) * TC, [[1, TC]]),
                            V(YSk, 0, 128, ((q * 2 + comp) * 9 + j + 1) * 128, [[1, 128]]),
                            V(Spv, 0, 128, ((blk % 2) * 8 + q * 2 + comp) * TC, [[1, TC]]), start=(cnt == 0), stop=(cnt == nmm - 1)),
                            R=["YSk"] + spk, W=[("ps", yb)])
                        cnt += 1
            ck(8 if blk == 0 else -1)
            for b4 in range(4):
                yb = YB[b4]
                P.op("scalar", lambda e, b4=b4, yb=yb, blk=blk: e.activation(
                    out=V(ysg, 0, 128, blk * TT + 2 * b4, [[1, 2], [8, TC]]), in_=V(PS[yb], 0, 128, 0, [[TC, 2], [1, TC]]), func=AF.Gelu),
                    R=[("ps", yb)], W=[("sx", blk)])
            P.op("vector", lambda e, blk=blk, dcol=dcol: e.scalar_tensor_tensor(
                V(ytmp, 0, 128, 0, [[1, NS]]), V(sx, 0, 128, blk * TT + SEQ, [[1, NS]]), dcol,
                V(PS[YSB], 0, 128, 0, [[1, NS]]), ALU.mult, ALU.add),
                R=[("sxs", blk), "cv4", ("ps", YSB)], W=["ytmp"])
            P.op("scalar", lambda e, blk=blk: e.activation(
                out=V(ysg, 0, 128, blk * TT + SEQ, [[1, NS]]), in_=V(ytmp, 0, 128, 0, [[1, NS]]), func=AF.Gelu),
                R=["ytmp"], W=[("sxs", blk)])
        ck(9)
        prefetch("glu", w_glu, 512, 4, 0, 512)
        prefetch("sz", w_in, 6144, 8, 3584, 512)
        for comp, dd in ((0, sre_p), (1, sim_p)):
            P.dma("sync", lambda e, comp=comp, dd=dd: e.dma_start(
                out=DR(dd, 0, [[1, 128], [128, 16], [1, 1]]), in_=V(fin, 0, 128, comp * 16, [[1, 16], [1, 1]]),
                allow_slow_non_contiguous=True),
                R=[("fin", comp, qg) for qg in range(16)], W=["out_fin%d" % comp])
        for comp, dd in ((0, sre_s), (1, sim_s)):
            pbs = [nextps() for _ in range(4)]
            for qg in range(16):
                pb = pbs[qg // 4]
                P.op("tensor", lambda e, comp=comp, qg=qg, pb=pb: e.matmul(
                    V(PS[pb], 0, NS, (qg % 4) * 128, [[1, 128]]),
                    V(sN, 0, 128, comp * 256 + qg * NS, [[1, NS]]),
                    V(ident_f, 0, 128, 0, [[1, 128]]), start=True, stop=True),
                    R=["sN%d" % comp, "ident_f"], W=[("ps", pb)])
            for i4 in range(4):
                P.op("scalar", lambda e, comp=comp, i4=i4, pb=pbs[i4]: e.activation(
                    out=V(Gc, 0, NS, comp * 2048 + i4 * 512, [[1, 512]]), in_=V(PS[pb], 0, NS, 0, [[1, 512]]), func=AF.Copy),
                    R=[("ps", pbs[i4])], W=Gkeys)
            P.dma("sync", lambda e, comp=comp, dd=dd: e.dma_start(out=dd.ap(), in_=V(Gc, 0, NS, comp * 2048, [[1, 2048]])),
                  R=Gkeys, W=["out_soT%d" % comp])
        P.barrier()
        stL.close()
        stS.close()

        ys2 = sb("ys2", [4, TT], BF16, stC)
        gt1 = [sb("gt1_%d" % i, [512], F32, stC) for i in range(2)]
        gt2 = [sb("gt2_%d" % i, [512], F32, stC) for i in range(2)]
        wi_glu = getw("glu", w_glu, 512, 4, 0, 512)
        wi_sz = getw("sz", w_in, 6144, 8, 3584, 512)
        it = 0
        for oc in range(4):
            for (cs, cn) in tblocks:
                b = it % 2; it += 1
                pb = nextps()
                mm_fm(wi_glu, 4, oc * 128, ysg, "ysgall", cs, cn, pb)
                P.op("scalar", lambda e, oc=oc, cn=cn, pb=pb, b=b: e.activation(
                    out=V(gt1[b], 0, 128, 0, [[1, cn]]), in_=V(PS[pb], 0, 128, 0, [[1, cn]]), func=AF.Sigmoid,
                    bias=V(cv4, 0, 128, oc, [[1, 1]])), R=[("ps", pb), "cv4"], W=[("gt1", b)])
                pb2 = nextps()
                mm_fm(wi_sz, 8, oc * 128, hT, "hTall", cs, cn, pb2)
                P.op("scalar", lambda e, cn=cn, pb2=pb2, b=b: e.activation(
                    out=V(gt2[b], 0, 128, 0, [[1, cn]]), in_=V(PS[pb2], 0, 128, 0, [[1, cn]]), func=AF.Sigmoid),
                    R=[("ps", pb2)], W=[("gt2", b)])
                P.op("vector", lambda e, cn=cn, pb2=pb2, b=b: e.tensor_tensor(
                    V(gt2[b], 0, 128, 0, [[1, cn]]), V(gt2[b], 0, 128, 0, [[1, cn]]), V(PS[pb2], 0, 128, 0, [[1, cn]]), ALU.mult),
                    R=[("gt2", b), ("ps", pb2)], W=[("gt2", b)])
                P.op("vector", lambda e, cn=cn, b=b, oc=oc, cs=cs: e.tensor_tensor(
                    V(gt1[b], 0, 128, 0, [[1, cn]]), V(gt1[b], 0, 128, 0, [[1, cn]]), V(ysg, 0, 128, oc * TT + cs, [[1, cn]]), ALU.mult),
                    R=[("gt1", b), "ysgall"], W=[("gt1", b)])
                P.op("vector", lambda e, cn=cn, b=b, oc=oc, cs=cs: e.tensor_tensor(
                    V(ys2, 0, 128, oc * TT + cs, [[1, cn]]), V(gt1[b], 0, 128, 0, [[1, cn]]), V(gt2[b], 0, 128, 0, [[1, cn]]), ALU.mult),
                    R=[("gt1", b), ("gt2", b)], W=["ys2all"])
        for half in range(2):
            wi_ps = load_w(w_ps, D, 4, half * 512, 512)
            wi_gs = load_w(w_in, 6144, 8, 5120 + half * 512, 512)
            for o4 in range(4):
                oc = half * 4 + o4
                for (cs, cn) in tblocks:
                    b = it % 2; it += 1
                    pb = nextps()
                    mm_fm(wi_gs, 8, o4 * 128, hT, "hTall", cs, cn, pb)
                    P.op("scalar", lambda e, cn=cn, pb=pb, b=b: e.activation(
                        out=V(gt1[b], 0, 128, 0, [[1, cn]]), in_=V(PS[pb], 0, 128, 0, [[1, cn]]), func=AF.Sigmoid),
                        R=[("ps", pb)], W=[("gt1", b)])
                    pb2 = nextps()
                    mm_fm(wi_ps, 4, o4 * 128, ys2, "ys2all", cs, cn, pb2)
                    P.op("vector", lambda e, cn=cn, pb2=pb2, b=b, oc=oc, cs=cs: e.tensor_tensor(
                        V(m_t, 0, 128, oc * TT + cs, [[1, cn]]), V(gt1[b], 0, 128, 0, [[1, cn]]), V(PS[pb2], 0, 128, 0, [[1, cn]]), ALU.mult),
                        R=[("gt1", b), ("ps", pb2)], W=[("m", oc)])
        prefetch("a0", w_in, 6144, 8, 0, 512)
        prefetch("b0", w_in, 6144, 8, 1024, 512)
        P.barrier()
        stC.close()

        stB = ExitStack(); stacks.append(stB)
        UW = 30 + SEQ
        u_t = sb("u_t", [8, UW], BF16, stB)
        vs_b = sb("vs_b", [8, NS], BF16, stB)
        us_f = sb("us_f", [8, NS], F32, stB)
        ul_f = sb("ul_f", [8, 30], F32, stB)
        vs_f = sb("vs_f", [8, NS], F32, stB)
        cwt = sb("cwt", [8, CK], F32, stB)
        bt1 = [sb("bt1_%d" % i, [512], F32, stB) for i in range(2)]
        bt2 = [sb("bt2_%d" % i, [512], F32, stB) for i in range(2)]
        acc1 = sb("acc1", [TT], F32, stB)
        acc2 = sb("acc2", [TT], F32, stB)
        bt3 = [sb("bt3_%d" % i, [512], F32, stB) for i in range(3)]
        cz1 = [sb("cz1_%d" % i, [512], F32, stB) for i in range(3)]

        def vcol(c, cs, cn):
            if cs < SEQ:
                return V(u_t, 0, 128, c * UW + 30 + cs, [[1, cn]])
            return V(vs_b, 0, 128, c * NS, [[1, cn]])
        P.dma("sync", lambda e: e.dma_start(out=cwt[:], in_=convw.ap()), W=["cwt"])
        P.op("gpsimd", lambda e: e.memset(V(u_t, 0, 128, 0, [[UW, 8], [1, 30]]), 0.0), W=[("u", c, -1) for c in range(8)])
        P.op("gpsimd", lambda e: e.memset(acc1[:], 0.0), W=["acc1"])
        P.op("gpsimd", lambda e: e.memset(acc2[:], 0.0), W=["acc2"])
        for half in range(2):
            wi_a = getw("a%d" % half, w_in, 6144, 8, half * 512, 512)
            wi_b = getw("b%d" % half, w_in, 6144, 8, 1024 + half * 512, 512)
            for o4 in range(4):
                c = half * 4 + o4
                for tbi, (cs, cn) in enumerate(tblocks):
                    b = it % 2; it += 1
                    pb = nextps()
                    mm_fm(wi_b, 8, o4 * 128, hT, "hTall", cs, cn, pb)
                    P.op("scalar", lambda e, cn=cn, pb=pb, b=b: e.activation(
                        out=V(bt1[b], 0, 128, 0, [[1, cn]]), in_=V(PS[pb], 0, 128, 0, [[1, cn]]), func=AF.Sigmoid),
                        R=[("ps", pb)], W=[("bt1", b)])
                    pb2 = nextps()
                    mm_fm(wi_a, 8, o4 * 128, hT, "hTall", cs, cn, pb2)
                    if cs < SEQ:
                        P.op("vector", lambda e, cn=cn, pb2=pb2, b=b, c=c, cs=cs: e.tensor_tensor(
                            V(u_t, 0, 128, c * UW + 30 + cs, [[1, cn]]), V(bt1[b], 0, 128, 0, [[1, cn]]), V(PS[pb2], 0, 128, 0, [[1, cn]]), ALU.mult),
                            R=[("bt1", b), ("ps", pb2)], W=[("u", c, tbi)])
                        if tbi == NTB - 1:
                            P.op("vector", lambda e, pb2=pb2, b=b, c=c: e.tensor_tensor(
                                V(ul_f, 0, 128, c * 30, [[1, 30]]), V(bt1[b], 0, 128, 482, [[1, 30]]), V(PS[pb2], 0, 128, 482, [[1, 30]]), ALU.mult),
                                R=[("bt1", b), ("ps", pb2)], W=[("ul", c)])
                    else:
                        P.op("vector", lambda e, cn=cn, pb2=pb2, b=b, c=c: e.tensor_tensor(
                            V(us_f, 0, 128, c * NS, [[1, NS]]), V(bt1[b], 0, 128, 0, [[1, cn]]), V(PS[pb2], 0, 128, 0, [[1, cn]]), ALU.mult),
                            R=[("bt1", b), ("ps", pb2)], W=[("us", c)])
        stB1 = ExitStack(); stacks.append(stB1)
        nrows = NS * 30
        cacheT = sb("cacheT", [8, nrows], F32, stB1)
        crow = [sb("crow%d" % i, [D], F32, stB1) for i in range(2)]
        ctmp = sb("ctmp", [nrows], F32, stB1)
        otm = sb("otm", [D], F32, stB1)
        for half in range(2):
            pb = nextps()
            for kk in range(4):
                c = half * 4 + kk
                P.op("tensor", lambda e, c=c, kk=kk, pb=pb: e.matmul(
                    V(PS[pb], 0, 30, kk * 128, [[1, 128]]), V(ul_f, 0, 128, c * 30, [[1, 30]]),
                    V(ident_f, 0, 128, 0, [[1, 128]]), start=True, stop=True),
                    R=[("ul", c), "ident_f"], W=[("ps", pb)])
            P.op("scalar", lambda e, half=half, pb=pb: e.activation(
                out=V(otm, 0, 30, half * 512, [[1, 512]]), in_=V(PS[pb], 0, 30, 0, [[1, 512]]), func=AF.Copy),
                R=[("ps", pb)], W=[("otm", half)])
        P.dma("sync", lambda e: e.dma_start(out=conv_p.ap(), in_=V(otm, 0, 30, 0, [[1, D]])),
              R=[("otm", 0), ("otm", 1)], W=["otm_out"])
        P.dma("sync", lambda e: e.dma_start(out=DR(conv_s, 0, [[30 * D, NS], [1, 29 * D]]),
                                            in_=DR(cache, D, [[30 * D, NS], [1, 29 * D]])), W=["conv_s_a"])
        for half in range(2):
            pb = nextps()
            for kk in range(4):
                c = half * 4 + kk
                P.op("tensor", lambda e, c=c, kk=kk, pb=pb: e.matmul(
                    V(PS[pb], 0, NS, kk * 128, [[1, 128]]), V(us_f, 0, 128, c * NS, [[1, NS]]),
                    V(ident_f, 0, 128, 0, [[1, 128]]), start=True, stop=True),
                    R=[("us", c), "ident_f"], W=[("ps", pb)])
            P.op("scalar", lambda e, half=half, pb=pb: e.activation(
                out=V(otm, 32, NS, half * 512, [[1, 512]]), in_=V(PS[pb], 0, NS, 0, [[1, 512]]), func=AF.Copy),
                R=[("ps", pb)], W=[("otm2", half)])
        P.dma("sync", lambda e: e.dma_start(out=DR(conv_s, 29 * D, [[30 * D, NS], [1, D]]), in_=V(otm, 32, NS, 0, [[1, D]])),
              R=[("otm2", 0), ("otm2", 1)], W=["conv_s_b"])
        rtiles = [(r, min(128, nrows - r)) for r in range(0, nrows, 128)]
        for ri, (r0, nr) in enumerate(rtiles):
            b = ri % 2
            P.dma("sync", lambda e, r0=r0, nr=nr, b=b: e.dma_start(out=V(crow[b], 0, nr, 0, [[1, D]]),
                                                                  in_=DR(cache, r0 * D, [[D, nr], [1, D]])), W=[("crow", b)])
            for c in range(8):
                pb = nextps()
                P.op("tensor", lambda e, c=c, pb=pb, nr=nr, b=b: e.matmul(
                    V(PS[pb], 0, 128, 0, [[1, nr]]), V(crow[b], 0, nr, c * 128, [[1, 128]]),
                    V(ident_f, 0, nr, 0, [[1, nr]]), start=True, stop=True),
                    R=[("crow", b), "ident_f"], W=[("ps", pb)])
                P.op("scalar", lambda e, c=c, pb=pb, nr=nr, r0=r0: e.activation(
                    out=V(cacheT, 0, 128, c * nrows + r0, [[1, nr]]), in_=V(PS[pb], 0, 128, 0, [[1, nr]]), func=AF.Copy),
                    R=[("ps", pb)], W=[("cacheT", c)])
        for c in range(8):
            P.op("vector", lambda e, c=c: e.tensor_tensor(
                V(ctmp, 0, 128, 0, [[30, NS], [1, 30]]), V(cacheT, 0, 128, c * nrows, [[30, NS], [1, 30]]),
                V(cwt, 0, 128, c * CK, [[0, NS], [1, 30]]), ALU.mult), R=[("cacheT", c), "cwt"], W=["ctmp"])
            P.op("vector", lambda e, c=c: e.tensor_reduce(
                V(vs_f, 0, 128, c * NS, [[1, NS]]), V(ctmp, 0, 128, 0, [[30, NS], [1, 30]]), mybir.AxisListType.X, ALU.add),
                R=["ctmp"], W=[("vs", c)])
            P.op("vector", lambda e, c=c: e.scalar_tensor_tensor(
                V(vs_f, 0, 128, c * NS, [[1, NS]]), V(us_f, 0, 128, c * NS, [[1, NS]]), V(cwt, 0, 128, c * CK + 30, [[1, 1]]),
                V(vs_f, 0, 128, c * NS, [[1, NS]]), ALU.mult, ALU.add), R=[("vs", c), ("us", c), "cwt"], W=[("vs", c)])
        P.barrier()
        stB1.close()
        stB2 = ExitStack(); stacks.append(stB2)
        dg = [sb("dg%d" % i, [CK, 128], BF16, stB2) for i in range(2)]
        vsq = [sb("vsq%d" % i, [512], BF16, stB2) for i in range(2)]
        msq = sb("msq", [512], F32, stB2)
        ND = 5
        accD = [sb("accD%d" % i, [SEQ], F32, stB2) for i in range(2)]
        conv_items = []
        for c in range(8):
            for tbi, (cs, cn) in reversed(list(enumerate(tblocks))):
                conv_items.append((c, tbi, cs, cn))
        conv_state = {}

        def build_dg(c):
            d_ = dg[c % 2]
            for k in range(ND, CK):
                P.op("scalar", lambda e, k=k: e.activation(
                    out=V(d_, 0, 128, k * 128, [[1, 128]]), in_=ident_f[:], func=AF.Identity, scale=V(cwt, 0, 128, c * CK + k, [[1, 1]])),
                    R=["ident_f", "cwt"], W=[("dg", c % 2, k)])

        def dve_taps(c, ks):
            a_ = accD[c % 2]
            ukeys = [("u", c, t) for t in range(-1, NTB)]
            for k in ks:
                srcu = V(u_t, 0, 128, c * UW + k, [[1, SEQ]])
                wcol = V(cwt, 0, 128, c * CK + k, [[1, 1]])
                if k == 0:
                    P.op("vector", lambda e, srcu=srcu, wcol=wcol: e.tensor_scalar(a_[:], srcu, wcol, None, ALU.mult),
                         R=ukeys + ["cwt"], W=[("accD", c % 2)])
                else:
                    P.op("vector", lambda e, srcu=srcu, wcol=wcol: e.scalar_tensor_tensor(a_[:], srcu, wcol, a_[:], ALU.mult, ALU.add),
                         R=ukeys + ["cwt", ("accD", c % 2)], W=[("accD", c % 2)])

        def sC1(ii):
            c, tbi, cs, cn = conv_items[ii]
            d_ = dg[c % 2]
            if ii == 0:
                build_dg(0)
                dve_taps(0, range(ND))
            if ii % len(tblocks) == 1 and c + 1 < 8:
                build_dg(c + 1)
            slots = list(range(1, len(tblocks)))[:3]
            per = -(-ND // len(slots))
            if c + 1 < 8 and (ii % len(tblocks)) in slots:
                j = slots.index(ii % len(tblocks))
                dve_taps(c + 1, range(per * j, min(ND, per * j + per)))
            b = ii % 2
            bias = V(cv8, 0, 128, c, [[1, 1]])
            if cs < SEQ:
                pb = nextps()
                for k in range(ND, CK):
                    P.op("tensor", lambda e, k=k, pb=pb: e.matmul(
                        V(PS[pb], 0, 128, 0, [[1, 512]]), V(d_, 0, 128, k * 128, [[1, 128]]),
                        V(u_t, 0, 128, c * UW + cs + k, [[1, 512]]), start=(k == ND), stop=(k == CK - 1)),
                        R=[("dg", c % 2, k), ("u", c, tbi), ("u", c, tbi - 1)], W=[("ps", pb)])
                P.op("vector", lambda e, pb=pb: e.tensor_tensor(
                    V(bt2[b], 0, 128, 0, [[1, cn]]), V(PS[pb], 0, 128, 0, [[1, cn]]), V(accD[c % 2], 0, 128, cs, [[1, cn]]), ALU.add),
                    R=[("ps", pb), ("accD", c % 2)], W=[("bt2", b)])
                src = V(bt2[b], 0, 128, 0, [[1, cn]]); rk = ("bt2", b)
            else:
                src = V(vs_f, 0, 128, c * NS, [[1, NS]]); rk = ("vs", c)
            P.op("scalar", lambda e: e.activation(
                out=vcol(c, cs, cn), in_=src, func=AF.Identity, bias=bias),
                R=[rk, "cv8"], W=[("u", c, tbi)])
            P.op("scalar", lambda e: e.activation(
                out=V(vsq[b], 0, 128, 0, [[1, cn]]), in_=src, func=AF.Square, bias=bias),
                R=[rk, "cv8"], W=[("vsq", b)])

        def sC2(ii):
            c, tbi, cs, cn = conv_items[ii]
            b = ii % 2
            p1 = nextps(); p2 = nextps()
            P.op("tensor", lambda e: e.matmul(
                V(PS[p1], 0, 128, 0, [[1, cn]]), ones_b[:], vcol(c, cs, cn), start=True, stop=True),
                R=["ones_b", ("u", c, tbi)], W=[("ps", p1)])
            P.op("tensor", lambda e: e.matmul(
                V(PS[p2], 0, 128, 0, [[1, cn]]), ones_b[:], V(vsq[b], 0, 128, 0, [[1, cn]]), start=True, stop=True),
                R=["ones_b", ("vsq", b)], W=[("ps", p2)])
            P.op("vector", lambda e: e.tensor_tensor(
                V(acc1, 0, 128, cs, [[1, cn]]), V(acc1, 0, 128, cs, [[1, cn]]), V(PS[p1], 0, 128, 0, [[1, cn]]), ALU.add),
                R=[("ps", p1), ("acc1", tbi)], W=[("acc1", tbi)])
            P.op("vector", lambda e: e.tensor_tensor(
                V(acc2, 0, 128, cs, [[1, cn]]), V(acc2, 0, 128, cs, [[1, cn]]), V(PS[p2], 0, 128, 0, [[1, cn]]), ALU.add),
                R=[("ps", p2), ("acc2", tbi)], W=[("acc2", tbi)])
        pipeline([sC1, sC2], len(conv_items))
        for tbi, (cs, cn) in enumerate(tblocks):
            a1 = V(acc1, 0, 128, cs, [[1, cn]]); a2 = V(acc2, 0, 128, cs, [[1, cn]]); mq = V(msq, 0, 128, 0, [[1, cn]])
            k1 = ("acc1", tbi); k2 = ("acc2", tbi)
            P.op("vector", lambda e, a1=a1: e.tensor_scalar(a1, a1, 1.0 / D, None, ALU.mult), R=[k1, "acc1"], W=[k1])
            P.op("vector", lambda e, a2=a2: e.tensor_scalar(a2, a2, 1.0 / D, EPS, ALU.mult, ALU.add), R=[k2, "acc2"], W=[k2])
            P.op("vector", lambda e, a1=a1, mq=mq: e.tensor_tensor(mq, a1, a1, ALU.mult), R=[k1], W=["msq"])
            P.op("vector", lambda e, a2=a2, mq=mq: e.tensor_tensor(a2, a2, mq, ALU.subtract), R=[k2, "msq"], W=[k2])
            P.op("scalar", lambda e, a2=a2: e.activation(out=a2, in_=a2, func=AF.Ln), R=[k2], W=[k2])
            P.op("scalar", lambda e, a2=a2: e.activation(out=a2, in_=a2, func=AF.Exp, scale=-0.5), R=[k2], W=[k2])
            P.op("vector", lambda e, a1=a1, a2=a2: e.scalar_tensor_tensor(a1, a1, -1.0, a2, ALU.mult, ALU.mult), R=[k1, k2], W=[k1])
        v2_items = []
        for half in range(2):
            for o4 in range(4):
                for tbi, (cs, cn) in enumerate(tblocks):
                    v2_items.append((half, o4, tbi, cs, cn))
        wz = {}

        def sV1(ii):
            half, o4, tbi, cs, cn = v2_items[ii]
            c = half * 4 + o4
            if o4 == 0 and tbi == 0:
                wz[half] = getw("z%d" % half, w_in, 6144, 8, 2048 + half * 512, 512)
            b = ii % 3
            pb = nextps()
            mm_fm(wz[half], 8, o4 * 128, hT, "hTall", cs, cn, pb)
            P.op("scalar", lambda e: e.activation(
                out=V(cz1[b], 0, 128, 0, [[1, cn]]), in_=V(PS[pb], 0, 128, 0, [[1, cn]]), func=AF.Silu),
                R=[("ps", pb)], W=[("cz1", b)])
            P.op("vector", lambda e: e.tensor_tensor(
                V(bt3[b], 0, 128, 0, [[1, cn]]), vcol(c, cs, cn), V(acc2, 0, 128, cs, [[1, cn]]), ALU.mult),
                R=[("u", c, tbi), ("acc2", tbi)], W=[("bt3", b)])
            P.op("vector", lambda e: e.tensor_tensor(
                V(bt3[b], 0, 128, 0, [[1, cn]]), V(bt3[b], 0, 128, 0, [[1, cn]]), V(acc1, 0, 128, cs, [[1, cn]]), ALU.add),
                R=[("bt3", b), ("acc1", tbi)], W=[("bt3", b)])

        def sV2(ii):
            half, o4, tbi, cs, cn = v2_items[ii]
            c = half * 4 + o4
            b = ii % 3
            P.op("scalar", lambda e: e.activation(
                out=V(bt3[b], 0, 128, 0, [[1, cn]]), in_=V(bt3[b], 0, 128, 0, [[1, cn]]), func=AF.Silu,
                scale=V(cv8, 0, 128, 8 + c, [[1, 1]]), bias=V(cv8, 0, 128, 16 + c, [[1, 1]])),
                R=[("bt3", b), "cv8"], W=[("bt3", b)])
            P.op("vector", lambda e: e.tensor_tensor(
                vcol(c, cs, cn), V(bt3[b], 0, 128, 0, [[1, cn]]), V(cz1[b], 0, 128, 0, [[1, cn]]), ALU.mult),
                R=[("bt3", b), ("cz1", b)], W=[("u", c, tbi)])
        pipeline([sV1, sV2], len(v2_items))
        v2keys_all = {tbi: [("u", c, tbi) for c in range(8)] for tbi in range(len(tblocks))}
        for half in range(2):
            wi_pc = load_w(w_pc, D, 8, half * 512, 512)
            wi_gc = load_w(w_in, 6144, 8, 4096 + half * 512, 512)
            for o4 in range(4):
                oc = half * 4 + o4
                for tbi_, (cs, cn) in enumerate(tblocks):
                    v2keys = v2keys_all[tbi_]
                    b = it % 2; it += 1
                    pb = nextps()
                    mm_fm(wi_gc, 8, o4 * 128, hT, "hTall", cs, cn, pb)
                    P.op("scalar", lambda e, cn=cn, pb=pb, b=b: e.activation(
                        out=V(bt1[b], 0, 128, 0, [[1, cn]]), in_=V(PS[pb], 0, 128, 0, [[1, cn]]), func=AF.Sigmoid),
                        R=[("ps", pb)], W=[("bt1", b)])
                    pb2 = nextps()
                    buf = WB[wi_pc]
                    for k in range(8):
                        P.op("tensor", lambda e, k=k, buf=buf, o4=o4, cs=cs, cn=cn, pb2=pb2: e.matmul(
                            V(PS[pb2], 0, 128, 0, [[1, cn]]), V(buf, 0, 128, k * 512 + o4 * 128, [[1, 128]]),
                            vcol(k, cs, cn), start=(k == 0), stop=(k == 7)),
                            R=[("wb", wi_pc)] + v2keys, W=[("ps", pb2)])
                    P.op("vector", lambda e, cn=cn, pb2=pb2, b=b: e.tensor_tensor(
                        V(bt1[b], 0, 128, 0, [[1, cn]]), V(bt1[b], 0, 128, 0, [[1, cn]]), V(PS[pb2], 0, 128, 0, [[1, cn]]), ALU.mult),
                        R=[("bt1", b), ("ps", pb2)], W=[("bt1", b)])
                    P.op("vector", lambda e, cn=cn, b=b, oc=oc, cs=cs: e.tensor_tensor(
                        V(m_t, 0, 128, oc * TT + cs, [[1, cn]]), V(m_t, 0, 128, oc * TT + cs, [[1, cn]]), V(bt1[b], 0, 128, 0, [[1, cn]]), ALU.add),
                        R=[("bt1", b), ("m", oc)], W=[("m", oc)])
        wo = [load_w(w_out, D, 8, h2 * 512, 512) for h2 in range(2)]
        wg = [load_w(w_pg, D, 8, h2 * 512, 512) for h2 in range(2)]
        P.barrier()
        stB2.close()
        stB.close()

        stH.close()
        stD = ExitStack(); stacks.append(stD)
        wple_b = sb("wple_b", [2, D], BF16, stD)
        bpg_b = sb("bpg_b", [D], BF16, stD)
        P.dma("gpsimd", lambda e: e.dma_start(out=V(bpg_b, 0, 1, 0, [[1, D]]),
                                              in_=DR(rows, 3 * D, [[0, 1], [1, D]])), W=["bpg_b"])
        rows2 = sb("rows2", [2, D], F32, stD)
        P.dma("sync", lambda e: e.dma_start(out=V(rows2, 0, 128, 0, [[1, 2 * D]]), in_=DR(rows, D, [[0, 128], [1, 2 * D]])), W=["rows2"])
        for (dst, wd, kc, nm) in ((wple_b, w_ple, 2, "wple"),):
            for h2 in range(2):
                P.dma("gpsimd", lambda e, dst=dst, wd=wd, kc=kc, h2=h2: e.dma_start(
                    out=V(dst, 0, 128, h2 * 512, [[D, kc], [1, 512]]),
                    in_=DR(wd, h2 * 512, [[D, 128], [128 * D, kc], [1, 512]])), W=[(nm, h2)])
        RX, RE, R1, RB = 4, 5, 3, 3
        xd = [sb("xd%d" % i, [D], F32, stD) for i in range(RX)]
        osb = [sb("osb%d" % i, [D], F32, stD) for i in range(RX)]
        en = [sb("en%d" % i, [D], F32, stD) for i in range(RE)]
        x1 = [sb("x1_%d" % i, [D], F32, stD) for i in range(R1)]
        pd = [sb("pd%d" % i, [256], BF16, stD) for i in range(RB)]
        pT = [sb("pT%d" % i, [2, 128], BF16, stD) for i in range(RB)]
        x1b = [sb("x1b%d" % i, [D], BF16, stD) for i in range(RB)]
        x1T = [sb("x1T%d" % i, [8, 128], BF16, stD) for i in range(RB)]
        gsig = [sb("gsig%d" % i, [D], F32, stD) for i in range(2)]
        yo = [sb("yo%d" % i, [D], F32, stD) for i in range(2)]
        jks = [sb("jk%d" % i, [512], BF16, stD) for i in range(4)]
        ssD = sb("ssD", [len(ttiles), 8], F32, stD)
        mkeys = [("m", oc) for oc in range(8)]
        tst = {}

        def col(ti, nt, c0):
            return V(ssD, 0, nt, ti * 8 + c0, [[1, 1]])

        def tl1(ti):
            r0, nt = ttiles[ti]
            srcx = DR(xp, r0 * D, [[D, nt], [1, D]]) if r0 < SEQ else DR(xs, 0, [[D, nt], [1, D]])
            srcp = DR(pp, r0 * 256, [[256, nt], [1, 256]]) if r0 < SEQ else DR(psm, 0, [[256, nt], [1, 256]])
            bx = ti % RX; bb = ti % RB
            P.dma("sync", lambda e: e.dma_start(out=V(xd[bx], 0, nt, 0, [[1, D]]), in_=srcx), W=[("xd", bx)])
            P.dma("gpsimd", lambda e: e.dma_start(out=V(pd[bb], 0, nt, 0, [[1, 256]]), in_=srcp), W=[("pd", bb)])
            ob = [nextps(), nextps()]
            for h2 in range(2):
                for k in range(8):
                    P.op("tensor", lambda e, k=k, h2=h2: e.matmul(
                        V(PS[ob[h2]], 0, nt, 0, [[1, 512]]), V(m_t, 0, 128, k * TT + r0, [[1, nt]]),
                        V(WB[wo[h2]], 0, 128, k * 512, [[1, 512]]), start=(k == 0), stop=(k == 7)),
                        R=mkeys + [("wb", wo[h2])], W=[("ps", ob[h2])])
                P.op("scalar", lambda e, h2=h2: e.activation(
                    out=V(jks[h2], 0, nt, 0, [[1, 512]]), in_=V(PS[ob[h2]], 0, nt, 0, [[1, 512]]), func=AF.Square,
                    accum_out=col(ti, nt, h2)), R=[("ps", ob[h2])], W=[("jk", h2), ("ssD", ti, h2)])
                P.op("scalar", lambda e, h2=h2: e.activation(
                    out=V(osb[bx], 0, nt, h2 * 512, [[1, 512]]), in_=V(PS[ob[h2]], 0, nt, 0, [[1, 512]]), func=AF.Copy),
                    R=[("ps", ob[h2])], W=[("osb", bx, h2)])
            pb = nextps()
            for k in range(2):
                P.op("tensor", lambda e, k=k: e.matmul(
                    V(PS[pb], 0, 128, k * nt, [[1, nt]]), V(pd[bb], 0, nt, k * 128, [[1, 128]]),
                    V(ident_b, 0, nt, 0, [[1, nt]]), start=True, stop=True),
                    R=[("pd", bb), "ident_b"], W=[("ps", pb)])
            P.op("vector", lambda e: e.tensor_copy(
                V(pT[bb], 0, 128, 0, [[128, 2], [1, nt]]), V(PS[pb], 0, 128, 0, [[nt, 2], [1, nt]])),
                R=[("ps", pb)], W=[("pT", bb)])

        def tl2(ti):
            r0, nt = ttiles[ti]
            bb = ti % RB; be = ti % RE
            eb = [nextps(), nextps()]
            for h2 in range(2):
                for k in range(2):
                    P.op("tensor", lambda e, k=k, h2=h2: e.matmul(
                        V(PS[eb[h2]], 0, nt, 0, [[1, 512]]), V(pT[bb], 0, 128, k * 128, [[1, nt]]),
                        V(wple_b, 0, 128, k * D + h2 * 512, [[1, 512]]), start=(k == 0), stop=(k == 1)),
                        R=[("pT", bb), ("wple", h2)], W=[("ps", eb[h2])])
                P.op("scalar", lambda e, h2=h2: e.activation(
                    out=V(jks[2 + h2], 0, nt, 0, [[1, 512]]), in_=V(PS[eb[h2]], 0, nt, 0, [[1, 512]]), func=AF.Square,
                    accum_out=col(ti, nt, 2 + h2)), R=[("ps", eb[h2])], W=[("jk", 2 + h2), ("ssD", ti, 2 + h2)])
                P.op("scalar", lambda e, h2=h2: e.activation(
                    out=V(en[be], 0, nt, h2 * 512, [[1, 512]]), in_=V(PS[eb[h2]], 0, nt, 0, [[1, 512]]), func=AF.Copy),
                    R=[("ps", eb[h2])], W=[("en", be, h2)])
            a = col(ti, nt, 0); a2 = col(ti, nt, 1)
            P.op("vector", lambda e: e.tensor_tensor(a, a, a2, ALU.add), R=[("ssD", ti, 0), ("ssD", ti, 1)], W=[("ssD", ti, 0)])

        def tl3(ti):
            r0, nt = ttiles[ti]
            a = col(ti, nt, 0)
            P.op("scalar", lambda e: e.activation(out=a, in_=a, func=AF.Ln, scale=1.0 / D, bias=EPS), R=[("ssD", ti, 0)], W=[("ssD", ti, 0)])
            P.op("scalar", lambda e: e.activation(out=a, in_=a, func=AF.Exp, scale=-0.5), R=[("ssD", ti, 0)], W=[("ssD", ti, 0)])
            c = col(ti, nt, 2); c2 = col(ti, nt, 3)
            P.op("vector", lambda e: e.tensor_tensor(c, c, c2, ALU.add), R=[("ssD", ti, 2), ("ssD", ti, 3)], W=[("ssD", ti, 2)])

        def tl4(ti):
            r0, nt = ttiles[ti]
            bx = ti % RX; b1 = ti % R1; bb = ti % RB
            a = col(ti, nt, 0); c = col(ti, nt, 2)
            P.op("scalar", lambda e: e.activation(out=c, in_=c, func=AF.Ln, scale=1.0 / D, bias=EPS), R=[("ssD", ti, 2)], W=[("ssD", ti, 2)])
            P.op("scalar", lambda e: e.activation(out=c, in_=c, func=AF.Exp, scale=-0.5), R=[("ssD", ti, 2)], W=[("ssD", ti, 2)])
            P.op("vector", lambda e: e.scalar_tensor_tensor(
                V(x1[b1], 0, nt, 0, [[1, D]]), V(osb[bx], 0, nt, 0, [[1, D]]), a,
                V(rows2, 0, nt, 0, [[1, D]]), ALU.mult, ALU.mult),
                R=[("osb", bx, 0), ("osb", bx, 1), ("ssD", ti, 0), "rows2"], W=[("x1", b1)])
            P.op("vector", lambda e: e.tensor_tensor(
                V(x1[b1], 0, nt, 0, [[1, D]]), V(x1[b1], 0, nt, 0, [[1, D]]), V(xd[bx], 0, nt, 0, [[1, D]]), ALU.add),
                R=[("x1", b1), ("xd", bx)], W=[("x1", b1)])
            P.op("vector", lambda e: e.tensor_copy(V(x1b[bb], 0, nt, 0, [[1, D]]), V(x1[b1], 0, nt, 0, [[1, D]])),
                 R=[("x1", b1)], W=[("x1b", bb)])

        def tl5(ti):
            r0, nt = ttiles[ti]
            bb = ti % RB; be = ti % RE
            for half in range(2):
                pb = nextps()
                for kk in range(4):
                    k = half * 4 + kk
                    P.op("tensor", lambda e, k=k, kk=kk, pb=pb: e.matmul(
                        V(PS[pb], 0, 128, kk * nt, [[1, nt]]), V(x1b[bb], 0, nt, k * 128, [[1, 128]]),
                        V(ident_b, 0, nt, 0, [[1, nt]]), start=True, stop=True),
                        R=[("x1b", bb), "ident_b"], W=[("ps", pb)])
                P.op("scalar", lambda e, half=half, pb=pb: e.activation(
                    out=V(x1T[bb], 0, 128, half * 4 * 128, [[128, 4], [1, nt]]),
                    in_=V(PS[pb], 0, 128, 0, [[nt, 4], [1, nt]]), func=AF.Copy),
                    R=[("ps", pb)], W=[("x1T", bb, half)])
            c = col(ti, nt, 2)
            P.op("vector", lambda e: e.scalar_tensor_tensor(
                V(en[be], 0, nt, 0, [[1, D]]), V(en[be], 0, nt, 0, [[1, D]]), c,
                V(rows2, 0, nt, D, [[1, D]]), ALU.mult, ALU.mult),
                R=[("en", be, 0), ("en", be, 1), ("ssD", ti, 2), "rows2"], W=[("en", be, 0), ("en", be, 1)])

        def tl6(ti):
            r0, nt = ttiles[ti]
            bb = ti % RB; be = ti % RE; b1 = ti % R1; b2 = ti % 2
            gb = [nextps(), nextps()]
            for h2 in range(2):
                P.op("tensor", lambda e, h2=h2: e.matmul(
                    V(PS[gb[h2]], 0, nt, 0, [[1, 512]]), V(ones_b, 0, 1, 0, [[1, nt]]),
                    V(bpg_b, 0, 1, h2 * 512, [[1, 512]]), start=True, stop=False),
                    R=["ones_b", "bpg_b"], W=[("ps", gb[h2])])
                for k in range(8):
                    P.op("tensor", lambda e, k=k, h2=h2: e.matmul(
                        V(PS[gb[h2]], 0, nt, 0, [[1, 512]]), V(x1T[bb], 0, 128, k * 128, [[1, nt]]),
                        V(WB[wg[h2]], 0, 128, k * 512, [[1, 512]]), start=False, stop=(k == 7)),
                        R=[("x1T", bb, 0), ("x1T", bb, 1), ("wb", wg[h2])], W=[("ps", gb[h2])])
                P.op("scalar", lambda e, h2=h2: e.activation(
                    out=V(gsig[b2], 0, nt, h2 * 512, [[1, 512]]), in_=V(PS[gb[h2]], 0, nt, 0, [[1, 512]]), func=AF.Sigmoid),
                    R=[("ps", gb[h2])], W=[("gsig", b2, h2)])
            P.op("vector", lambda e: e.tensor_tensor(
                V(en[be], 0, nt, 0, [[1, D]]), V(en[be], 0, nt, 0, [[1, D]]), V(gsig[b2], 0, nt, 0, [[1, D]]), ALU.mult),
                R=[("en", be, 0), ("en", be, 1), ("gsig", b2, 0), ("gsig", b2, 1)], W=[("en", be, 0), ("en", be, 1)])
            P.op("vector", lambda e: e.tensor_tensor(
                V(yo[b2], 0, nt, 0, [[1, D]]), V(en[be], 0, nt, 0, [[1, D]]), V(x1[b1], 0, nt, 0, [[1, D]]), ALU.add),
                R=[("en", be, 0), ("en", be, 1), ("x1", b1)], W=[("yo", b2)])
            dsty = DR(y_p, r0 * D, [[D, nt], [1, D]]) if r0 < SEQ else DR(y_s, 0, [[D, nt], [1, D]])
            P.dma("sync", lambda e: e.dma_start(out=dsty, in_=V(yo[b2], 0, nt, 0, [[1, D]])),
                  R=[("yo", b2)], W=[("yout", ti)])
        pipeline([tl1, tl2, tl3, tl4, tl5, tl6], len(ttiles))
        P.barrier(final=True)


    except _Stop:
        for st_ in reversed(stacks):
            st_.close()
        P.barrier(final=True)
    with nc.Block() as block:
        P.replay(block)
    for st_ in reversed(stacks):
        st_.close()
    es.close()
    return nc


def _host_layouts(inp, SEQ):
    f = lambda a: np.ascontiguousarray(np.asarray(a, dtype=np.float32))
    out = {}
    out["w_in"] = f(inp["w_in"][0]); out["w_pc"] = f(inp["w_pc"][0]); out["w_ps"] = f(inp["w_ps"][0])
    out["w_glu"] = f(inp["w_glu"][0]); out["w_out"] = f(inp["w_out"][0]); out["w_pg"] = f(inp["w_pg"][0])
    out["w_ple"] = f(inp["w_ple"][0])
    out["rows"] = f(np.stack([inp["g_pre"][0], inp["g_post"][0], inp["g_ple"][0], inp["b_pg"][0]]))
    col8 = lambda v: np.asarray(v).reshape(8, 128).T
    col4 = lambda v: np.asarray(v).reshape(4, 128).T
    out["colv8"] = f(np.stack([col8(inp["conv_b"][0]), col8(inp["ln_g"][0]), col8(inp["ln_b"][0])], axis=1))
    out["colv4"] = f(np.stack([col4(inp["b_glu"][0]), col4(inp["ssm_d"][0])], axis=1))
    out["convw"] = f(np.asarray(inp["conv_w"][0]).T.reshape(8, 128, CK).transpose(1, 0, 2))
    a_re = np.asarray(inp["ssm_a_re"][0]); a_im = np.asarray(inp["ssm_a_im"][0]); ldt = np.asarray(inp["ssm_log_dt"][0])
    b_re = np.asarray(inp["ssm_b_re"][0]); b_im = np.asarray(inp["ssm_b_im"][0])
    c_re = np.asarray(inp["ssm_c_re"][0]); c_im = np.asarray(inp["ssm_c_im"][0])
    PA = np.zeros((128, 3, 4, 64), np.float32)
    BA = np.zeros((128, 2, 4, 64), np.float32)
    MA = np.zeros((128, 8), np.float32)
    PB = np.zeros((128, 3, 16), np.float32)
    CBm = np.zeros((128, 2, 16, 16), np.float32)
    BBm = np.zeros((128, 2, 16, 16), np.float32)
    MB = np.zeros((128, 4, 8), np.float32)
    for blk in range(4):
        for gl in range(8):
            g = blk * 8 + gl
            q, par = gl // 2, gl % 2
            rows = slice(gl * 16, gl * 16 + 16)
            PA[rows, 0, blk, :] = a_re[g][None, :]
            PA[rows, 1, blk, :] = a_im[g][None, :]
            PA[rows, 2, blk, :] = ldt[g]
            BA[rows, 0, blk, :] = b_re[g].T
            BA[rows, 1, blk, :] = b_im[g].T
            MA[rows, gl] = 1.0
            qg = blk * 4 + q
            prow = slice(par * 64, (par + 1) * 64)
            PB[prow, 0, qg] = a_re[g]; PB[prow, 1, qg] = a_im[g]; PB[prow, 2, qg] = ldt[g]
            CBm[prow, 0, qg, :] = c_re[g].T
            CBm[prow, 1, qg, :] = c_im[g].T
            BBm[prow, 0, qg, :] = b_re[g]
            BBm[prow, 1, qg, :] = b_im[g]
            MB[prow, q, gl] = 1.0
    out["PA"] = PA.reshape(128, 3, 256); out["BA"] = BA.reshape(128, 2, 256); out["MA"] = MA
    out["PB"] = PB; out["CB"] = CBm.reshape(128, 2, 256); out["BB"] = BBm.reshape(128, 2, 256); out["MB"] = MB
    out["ident"] = np.eye(128, dtype=np.float32)
    out["iota"] = np.ascontiguousarray(np.broadcast_to(np.arange(SEQ, dtype=np.float32), (128, SEQ)))
    return out


def make_in_maps(inp, SEQ, ncores):
    shared = _host_layouts(inp, SEQ)
    f = lambda a: np.ascontiguousarray(np.asarray(a, dtype=np.float32))
    maps = []
    for i in range(ncores):
        m = dict(shared)
        m["xp"] = f(inp["x_prompt"][i]); m["xs"] = f(inp["x_sample"][i * NS:(i + 1) * NS, 0])
        m["pp"] = f(inp["p_prompt"][0, i]); m["psm"] = f(inp["p_sample"][0, i * NS:(i + 1) * NS, 0])
        m["cache"] = f(inp["cache_conv"][0, i * NS:(i + 1) * NS]).reshape(NS * 30, D)
        m["st_re"] = f(inp["state_ssm_re"][0, i * NS:(i + 1) * NS]).reshape(NS, 2048)
        m["st_im"] = f(inp["state_ssm_im"][0, i * NS:(i + 1) * NS]).reshape(NS, 2048)
        maps.append(m)
    return maps


def assemble(results, SEQ, ncores):
    y_p = np.stack([r["y_p"] for r in results]).reshape(ncores, SEQ, D)
    y_s = np.concatenate([r["y_s"] for r in results]).reshape(ncores * NS, 1, D)
    conv_p = np.stack([r["conv_p"] for r in results]).reshape(1, ncores, 30, D)
    conv_s = np.concatenate([r["conv_s"].reshape(NS, 30, D) for r in results]).reshape(1, ncores * NS, 30, D)
    sre_p = np.stack([r["sre_p"] for r in results]).reshape(1, ncores, 32, 64)
    sim_p = np.stack([r["sim_p"] for r in results]).reshape(1, ncores, 32, 64)
    sre_s = np.concatenate([r["sre_s"] for r in results]).reshape(1, ncores * NS, 32, 64)
    sim_s = np.concatenate([r["sim_s"] for r in results]).reshape(1, ncores * NS, 32, 64)
    return tuple(np.ascontiguousarray(a, dtype=np.float32) for a in
                 (y_p, y_s, conv_p, conv_s, sre_p, sim_p, sre_s, sim_s))


def kernel(**inputs):
    SEQ = 2048
    n = 8
    nc = build_program(SEQ)
    in_maps = make_in_maps(inputs, SEQ, n)
    res = run_bass_kernel_spmd(nc, in_maps, core_ids=list(range(n)))
    return assemble(res.results, SEQ, n)
```

```python
import math
from contextlib import ExitStack
import numpy as np
import concourse.bass as bass
import concourse.mybir as mybir
from concourse.bass_utils import run_bass_kernel_spmd

F32 = mybir.dt.float32
BF16 = mybir.dt.bfloat16
AF = mybir.ActivationFunctionType
ALU = mybir.AluOpType

D = 1024
NS = 16
CK = 31
EPS = 1e-6
PI = math.pi
TWO_PI = 2.0 * math.pi
MAGIC = 12582912.0
SHR = 1.0 - 2e-6


def V(t, p0, np_, f0, dims):
    F = 1
    for s in t.shape[1:]:
        F *= s
    return bass.AP(t, p0 * F + f0, [[F, np_]] + [list(d) for d in dims])


def DR(t, off, dims):
    return bass.AP(t, off, [list(d) for d in dims])


class Prog:
    ENGS = ["tensor", "vector", "scalar", "gpsimd", "sync"]
    NDS = 8

    def __init__(self, nc, es):
        self.nc = nc
        self.q = {e: [] for e in self.ENGS}
        self.cnt = {e: 0 for e in self.ENGS}
        self.sem = {e: es.enter_context(nc.semaphore("s_" + e)) for e in self.ENGS}
        self.dsem = {}
        self.dcnt = {}
        self.dnext = {}
        for qn in ["sync", "gpsimd", "scalar"]:
            self.dsem[qn] = [es.enter_context(nc.semaphore("d_%s%d" % (qn, i))) for i in range(self.NDS)]
            self.dcnt[qn] = [0] * self.NDS
            self.dnext[qn] = 0
        self.last_w = {}
        self.readers = {}
        self.waited = {e: {} for e in self.ENGS}
        self.all_events = []

    def _deps(self, eng, R, W):
        deps = {}

        def add(ev, war=False):
            if ev is None:
                return
            key, val, src = ev[0], ev[1], ev[2]
            if src == eng and eng == "tensor":
                return
            if deps.get(key, (0, None))[0] < val:
                deps[key] = (val, ev[3])
        for r in R:
            add(self.last_w.get(r))
        for w in W:
            add(self.last_w.get(w))
            for ev in self.readers.get(w, []):
                add(ev, war=True)
        out = []
        for key, (val, semh) in deps.items():
            if self.waited[eng].get(key, 0) >= val:
                continue
            self.waited[eng][key] = val
            out.append((semh, val))
        return out

    def _commit(self, ev, R, W):
        for w in W:
            self.last_w[w] = ev
            self.readers[w] = []
        for r in R:
            self.readers.setdefault(r, []).append(ev)

    def op(self, eng, fn, R=(), W=()):
        waits = self._deps(eng, R, W)
        self.cnt[eng] += 1
        ev = ("e_" + eng, self.cnt[eng], eng, self.sem[eng])
        self.q[eng].append((waits, fn, self.sem[eng], 1))
        self._commit(ev, R, W)
        return ev

    def dma(self, qn, fn, R=(), W=()):
        waits = self._deps(qn, R, W)
        j = self.dnext[qn]
        self.dnext[qn] = (j + 1) % self.NDS
        n = self.dcnt[qn][j]
        key = "d_%s%d" % (qn, j)
        semh = self.dsem[qn][j]
        if n > 0 and self.waited[qn].get(key, 0) < 16 * n:
            self.waited[qn][key] = 16 * n
            waits.append((semh, 16 * n))
        self.dcnt[qn][j] = n + 1
        ev = (key, 16 * (n + 1), "dma_" + qn, semh)
        self.q[qn].append((waits, fn, semh, 16))
        self._commit(ev, R, W)
        self.all_events.append(ev)
        return ev

    def barrier(self, final=False):
        evs = []
        for e in self.ENGS:
            if self.cnt[e] > 0:
                evs.append(("e_" + e, self.cnt[e], e, self.sem[e]))
        for qn in self.dsem:
            for j in range(self.NDS):
                if self.dcnt[qn][j] > 0:
                    evs.append(("d_%s%d" % (qn, j), 16 * self.dcnt[qn][j], "dma_" + qn, self.dsem[qn][j]))
        for e in self.ENGS:
            if e == "tensor" and not final:
                continue
            waits = []
            for (key, val, src, semh) in evs:
                if src == e:
                    continue
                if self.waited[e].get(key, 0) >= val:
                    continue
                self.waited[e][key] = val
                waits.append((semh, val))
            if waits:
                self.q[e].append((waits, None, None, 0))

    def replay(self, block):
        def mk(e):
            def body(eng):
                for (waits, fn, semh, inc) in self.q[e]:
                    for (s, v) in waits:
                        eng.wait_ge(s, v)
                    if fn is not None:
                        fn(eng).then_inc(semh, inc)
            return body
        block.tensor(mk("tensor"))
        block.vector(mk("vector"))
        block.scalar(mk("scalar"))
        block.gpsimd(mk("gpsimd"))
        block.sync(mk("sync"))


class _Stop(Exception):
    pass


def build_program(SEQ, stop=None):
    TT = SEQ + NS
    NTB = SEQ // 512
    tblocks = [(i * 512, 512) for i in range(NTB)] + [(SEQ, NS)]
    ttiles = [(i * 128, 128) for i in range(SEQ // 128)] + [(SEQ, NS)]

    nc = bass.Bass("TRN2", target_bir_lowering=False)
    es = ExitStack()

    def din(name, shape):
        return nc.dram_tensor(name, list(shape), F32, kind="ExternalInput")

    def dout(name, shape):
        return nc.dram_tensor(name, list(shape), F32, kind="ExternalOutput")

    xp = din("xp", [SEQ, D]); xs = din("xs", [NS, D])
    pp = din("pp", [SEQ, 256]); psm = din("psm", [NS, 256])
    cache = din("cache", [NS * 30, D])
    st_re = din("st_re", [NS, 2048]); st_im = din("st_im", [NS, 2048])
    w_in = din("w_in", [D, 6144]); w_pc = din("w_pc", [D, D]); w_ps = din("w_ps", [512, D])
    w_glu = din("w_glu", [512, 512]); w_out = din("w_out", [D, D]); w_pg = din("w_pg", [D, D])
    w_ple = din("w_ple", [256, D])
    rows = din("rows", [4, D])
    colv8 = din("colv8", [128, 3, 8])
    colv4 = din("colv4", [128, 2, 4])
    convw = din("convw", [128, 8, CK])
    PA = din("PA", [128, 3, 256])
    BA = din("BA", [128, 2, 256])
    MA = din("MA", [128, 8])
    PB = din("PB", [128, 3, 16])
    CB = din("CB", [128, 2, 256])
    BB = din("BB", [128, 2, 256])
    MB = din("MB", [128, 4, 8])
    ident_d = din("ident", [128, 128])
    iota_d = din("iota", [128, SEQ])

    y_p = dout("y_p", [SEQ, D]); y_s = dout("y_s", [NS, D])
    conv_p = dout("conv_p", [30, D]); conv_s = dout("conv_s", [NS * 30, D])
    sre_p = dout("sre_p", [32, 64]); sim_p = dout("sim_p", [32, 64])
    sre_s = dout("sre_s", [NS, 2048]); sim_s = dout("sim_s", [NS, 2048])

    P = Prog(nc, es)
    stacks = []

    def ck(n):
        if stop is not None and n == stop:
            raise _Stop()

    def sb(name, free, dt=F32, stack=es):
        return stack.enter_context(nc.sbuf_tensor(name, [128] + list(free), dt))

    PS = [es.enter_context(nc.psum_tensor("ps%d" % i, [128, 512], F32)) for i in range(8)]
    psi = [0]

    def nextps():
        i = psi[0]
        psi[0] = (i + 1) % 8
        return i

    ident_b = sb("ident_b", [128], BF16)
    ident_f = sb("ident_f", [128], F32)
    ones_b = sb("ones_b", [128], BF16)
    cv8 = sb("cv8", [3, 8], F32)
    cv4 = sb("cv4", [2, 4], F32)
    m_t = sb("m_t", [8, TT], BF16)
    stW = ExitStack(); stacks.append(stW)
    NWB = 4
    WB = [sb("wb%d" % i, [8, 512], BF16, stW) for i in range(NWB)]

    def pipeline(stages, n, between=None):
        for t in range(n + len(stages) - 1):
            if between is not None:
                between()
            for si, st_fn in enumerate(stages):
                i = t - si
                if 0 <= i < n:
                    st_fn(i)
    wbi = [0]

    P.dma("sync", lambda e: e.dma_start(out=ident_f[:], in_=ident_d.ap()), W=["ident_f"])
    P.dma("gpsimd", lambda e: e.dma_start(out=ident_b[:], in_=ident_d.ap()), W=["ident_b"])
    P.dma("sync", lambda e: e.dma_start(out=cv8[:], in_=colv8.ap()), W=["cv8"])
    P.dma("sync", lambda e: e.dma_start(out=cv4[:], in_=colv4.ap()), W=["cv4"])
    P.op("gpsimd", lambda e: e.memset(ones_b[:], 1.0), W=["ones_b"])

    def load_w(wd, ncols_total, kc, c0, ncols):
        i = wbi[0]
        wbi[0] = (i + 1) % NWB
        buf = WB[i]
        P.dma("gpsimd", lambda e: e.dma_start(
            out=V(buf, 0, 128, 0, [[512, kc], [1, ncols]]),
            in_=DR(wd, c0, [[ncols_total, 128], [128 * ncols_total, kc], [1, ncols]])),
            W=[("wb", i)])
        return i

    pending = {}

    def prefetch(key, *args):
        pending[key] = load_w(*args)

    def getw(key, *args):
        if key in pending:
            return pending.pop(key)
        return load_w(*args)

    def mm_fm(wi, kc, col_in_buf, act, act_key, cs, cn, pbank):
        buf = WB[wi]
        actF = act.shape[2]
        for k in range(kc):
            P.op("tensor", lambda e, k=k: e.matmul(
                V(PS[pbank], 0, 128, 0, [[1, cn]]),
                V(buf, 0, 128, k * 512 + col_in_buf, [[1, 128]]),
                V(act, 0, 128, k * actF + cs, [[1, cn]]),
                start=(k == 0), stop=(k == kc - 1)),
                R=[("wb", wi), act_key], W=[("ps", pbank)])

    try:
        stH = ExitStack(); stacks.append(stH)
        hT = sb("hT", [8, TT], BF16, stH)
        stC = ExitStack(); stacks.append(stC)
        sx = sb("sx", [4, TT], BF16, stC)
        wi = load_w(w_in, 6144, 8, 3072, 512)
        TC = SEQ // 8
        stS = ExitStack(); stacks.append(stS)
        PBt = sb("PBt", [3, 16], F32, stS)
        tB = [sb("tB%d" % i, [16], F32, stS) for i in range(13)]
        LpB = sb("LpB", [2, 9, 16], F32, stS)
        Gc = sb("Gc", [8, 2, 256], F32, stS)
        Hc = sb("Hc", [2, 9, 256], BF16, stS)
        BbB = sb("BbB", [2, 256], BF16, stS)
        mA = sb("mA", [8], F32, stS)
        mB = sb("mB", [4, 8], BF16, stS)
        fin = sb("fin", [2, 16], F32, stS)
        sS = sb("sS", [2, 256], F32, stS)
        sN = sb("sN", [2, 256], F32, stS)
        sNb = sb("sNb", [2, 256], BF16, stS)
        sT1 = sb("sT1", [256], F32, stS)
        sT2 = sb("sT2", [256], F32, stS)
        P.dma("sync", lambda e: e.dma_start(out=PBt[:], in_=PB.ap()), W=["PBt"])
        P.dma("sync", lambda e: e.dma_start(out=mA[:], in_=MA.ap()), W=["mA"])
        P.dma("gpsimd", lambda e: e.dma_start(out=mB[:], in_=MB.ap()), W=["mB"])
        stP = ExitStack(); stacks.append(stP)
        PAt = sb("PAt", [3, 256], F32, stP)
        BAt = sb("BAt", [2, 256], F32, stP)
        CBt = sb("CBt", [2, 256], F32, stP)
        BBt = sb("BBt", [2, 256], F32, stP)
        tA = [sb("tA%d" % i, [256], F32, stP) for i in range(9)]
        big0 = sb("big0", [9 * 256], F32, stP)
        stage = big0
        big1 = sb("big1", [9 * 256], F32, stP)
        P.dma("sync", lambda e: e.dma_start(out=PAt[:], in_=PA.ap()), W=["PAt"])
        P.dma("sync", lambda e: e.dma_start(out=BAt[:], in_=BA.ap()), W=["BAt"])
        P.dma("sync", lambda e: e.dma_start(out=CBt[:], in_=CB.ap()), W=["CBt"])
        P.dma("sync", lambda e: e.dma_start(out=BBt[:], in_=BB.ap()), W=["BBt"])

        Gkeys = rho8 = tau8 = None

        def prep_gen():
            nonlocal Gkeys, rho8, tau8
            def lam_prep(par, n, T, pk, pre):
                a_re = V(par, 0, 128, 0, [[1, n]]); a_im = V(par, 0, 128, n, [[1, n]]); ldt = V(par, 0, 128, 2 * n, [[1, n]])
                dt_, mag, th, cs_, sn_, tmp = T[0], T[1], T[2], T[3], T[4], T[5]
                P.op("scalar", lambda e: e.activation(out=dt_[:], in_=ldt, func=AF.Exp), R=[pk], W=[pre + "dt"])
                P.op("vector", lambda e: e.tensor_tensor(mag[:], a_re, dt_[:], ALU.mult), R=[pk, pre + "dt"], W=[pre + "mag"])
                P.op("scalar", lambda e: e.activation(out=mag[:], in_=mag[:], func=AF.Exp), R=[pre + "mag"], W=[pre + "mag"])
                P.op("vector", lambda e: e.scalar_tensor_tensor(th[:], a_im, 1.0 / TWO_PI, dt_[:], ALU.mult, ALU.mult), R=[pk, pre + "dt"], W=[pre + "th"])
                P.op("vector", lambda e: e.tensor_scalar(tmp[:], th[:], MAGIC, -MAGIC, ALU.add, ALU.add), R=[pre + "th"], W=[pre + "tmp"])
                P.op("vector", lambda e: e.tensor_tensor(th[:], th[:], tmp[:], ALU.subtract), R=[pre + "th", pre + "tmp"], W=[pre + "th"])
                P.op("scalar", lambda e: e.activation(out=sn_[:], in_=th[:], func=AF.Sin, scale=TWO_PI * SHR), R=[pre + "th"], W=[pre + "sin"])
                P.op("scalar", lambda e: e.activation(out=tmp[:], in_=th[:], func=AF.Sin, scale=PI * SHR), R=[pre + "th"], W=[pre + "tmp"])
                P.op("scalar", lambda e: e.activation(out=tmp[:], in_=tmp[:], func=AF.Square, scale=math.sqrt(2.0)), R=[pre + "tmp"], W=[pre + "tmp"])
                P.op("scalar", lambda e: e.activation(out=cs_[:], in_=tmp[:], func=AF.Identity, scale=-1.0, bias=1.0), R=[pre + "tmp"], W=[pre + "cos"])
                return mag, th, cs_, sn_

            def f_prep(par, n, T, mag, cs_, sn_, pk, pre):
                a_re = V(par, 0, 128, 0, [[1, n]]); a_im = V(par, 0, 128, n, [[1, n]])
                lr, li, den, t7, nr = T[0], T[5], T[6], T[7], T[8]
                P.op("vector", lambda e: e.tensor_tensor(lr[:], mag[:], cs_[:], ALU.mult), R=[pre + "mag", pre + "cos", pre + "dt"], W=[pre + "dt"])
                P.op("vector", lambda e: e.tensor_tensor(li[:], mag[:], sn_[:], ALU.mult), R=[pre + "mag", pre + "sin"], W=[pre + "tmp"])
                P.op("vector", lambda e: e.tensor_scalar(nr[:], lr[:], -1.0, None, ALU.add), R=[pre + "dt"], W=[pre + "nr"])
                P.op("vector", lambda e: e.tensor_tensor(den[:], a_re, a_re, ALU.mult), R=[pk], W=[pre + "den"])
                P.op("vector", lambda e: e.tensor_tensor(t7[:], a_im, a_im, ALU.mult), R=[pk], W=[pre + "t7"])
                P.op("vector", lambda e: e.tensor_tensor(den[:], den[:], t7[:], ALU.add), R=[pre + "den", pre + "t7"], W=[pre + "den"])
                P.op("vector", lambda e: e.reciprocal(den[:], den[:]), R=[pre + "den"], W=[pre + "den"])
                fr, fi = T[3], T[4]
                P.op("vector", lambda e: e.tensor_tensor(fr[:], nr[:], a_re, ALU.mult), R=[pre + "nr", pk], W=[pre + "cos"])
                P.op("vector", lambda e: e.tensor_tensor(t7[:], li[:], a_im, ALU.mult), R=[pre + "tmp", pk, pre + "den"], W=[pre + "t7"])
                P.op("vector", lambda e: e.tensor_tensor(fr[:], fr[:], t7[:], ALU.add), R=[pre + "cos", pre + "t7"], W=[pre + "cos"])
                P.op("vector", lambda e: e.tensor_tensor(fr[:], fr[:], den[:], ALU.mult), R=[pre + "cos", pre + "den"], W=[pre + "cos"])
                P.op("vector", lambda e: e.tensor_tensor(fi[:], li[:], a_re, ALU.mult), R=[pre + "tmp", pk], W=[pre + "sin"])
                P.op("vector", lambda e: e.tensor_tensor(t7[:], nr[:], a_im, ALU.mult), R=[pre + "nr", pk, pre + "cos"], W=[pre + "t7"])
                P.op("vector", lambda e: e.tensor_tensor(fi[:], fi[:], t7[:], ALU.subtract), R=[pre + "sin", pre + "t7"], W=[pre + "sin"])
                P.op("vector", lambda e: e.tensor_tensor(fi[:], fi[:], den[:], ALU.mult), R=[pre + "sin", pre + "den"], W=[pre + "sin"])
                return lr, li, fr, fi

            def cmul(o_re, o_im, a_re, a_im, b_re, b_im, t0, t1, R, Wre, Wim, neg_im=False):
                P.op("vector", lambda e: e.tensor_tensor(t0, a_re, b_re, ALU.mult), R=R, W=["cm_t0"])
                P.op("vector", lambda e: e.tensor_tensor(t1, a_im, b_im, ALU.mult), R=R, W=["cm_t1"])
                P.op("vector", lambda e: e.tensor_tensor(o_re, t0, t1, ALU.subtract), R=["cm_t0", "cm_t1"], W=Wre)
                P.op("vector", lambda e: e.tensor_tensor(t0, a_re, b_im, ALU.mult), R=R + Wre, W=["cm_t0"])
                P.op("vector", lambda e: e.tensor_tensor(t1, a_im, b_re, ALU.mult), R=R + Wre, W=["cm_t1"])
                if neg_im:
                    P.op("vector", lambda e: e.scalar_tensor_tensor(o_im, t0, -1.0, t1, ALU.mult, ALU.subtract), R=["cm_t0", "cm_t1"], W=Wim)
                else:
                    P.op("vector", lambda e: e.tensor_tensor(o_im, t0, t1, ALU.add), R=["cm_t0", "cm_t1"], W=Wim)

            magA, tauA, cosA, sinA = lam_prep(PAt, 256, tA, "PAt", "A_")
            yield
            lrA, liA, frA, fiA = f_prep(PAt, 256, tA, magA, cosA, sinA, "PAt", "A_")
            yield

            def gk(k, comp):
                return V(Gc, 0, 128, (k * 2 + comp) * 256, [[1, 256]])
            b0 = V(big0, 0, 128, 0, [[1, 256]]); b1 = V(big1, 0, 128, 0, [[1, 256]])
            cmul(gk(0, 0), gk(0, 1), frA[:], fiA[:], V(BAt, 0, 128, 0, [[1, 256]]), V(BAt, 0, 128, 256, [[1, 256]]), b0, b1,
                 ["A_cos", "A_sin", "BAt"], [("Gc", 0, 0)], [("Gc", 0, 1)])
            for k in range(1, 8):
                cmul(gk(k, 0), gk(k, 1), gk(k - 1, 0), gk(k - 1, 1), lrA[:], liA[:], b0, b1,
                     [("Gc", k - 1, 0), ("Gc", k - 1, 1), "A_dt", "A_tmp"], [("Gc", k, 0)], [("Gc", k, 1)])
                yield
            Gkeys = [("Gc", k, c) for k in range(8) for c in range(2)]
            magB, tauB, cosB, sinB = lam_prep(PBt, 16, tB, "PBt", "B_")
            yield
            lrB, liB, frB, fiB = f_prep(PBt, 16, tB, magB, cosB, sinB, "PBt", "B_")
            yield

            def lp(k, comp):
                return V(LpB, 0, 128, (comp * 9 + k) * 16, [[1, 16]])
            P.op("vector", lambda e: e.memset(lp(0, 0), 1.0), W=[("LpB", 0)])
            P.op("vector", lambda e: e.memset(lp(0, 1), 0.0), R=[("LpB", 0)], W=[("LpB", 0)])
            P.op("vector", lambda e: e.tensor_copy(lp(1, 0), lrB[:]), R=["B_dt"], W=[("LpB", 1)])
            P.op("vector", lambda e: e.tensor_copy(lp(1, 1), liB[:]), R=["B_tmp", ("LpB", 1)], W=[("LpB", 1)])
            tb0 = tB[9][:]; tb1 = tB[10][:]
            for k in range(2, 9):
                cmul(lp(k, 0), lp(k, 1), lp(k - 1, 0), lp(k - 1, 1), lrB[:], liB[:], tb0, tb1,
                     [("LpB", k - 1), "B_dt", "B_tmp"], [("LpB", k)], [("LpB", k)])
                yield
            LpKeys = [("LpB", k) for k in range(9)]
            rho8 = tB[11]; tau8 = tB[12]
            P.op("vector", lambda e: e.tensor_tensor(rho8[:], magB[:], magB[:], ALU.mult), R=["B_mag"], W=["rho8"])
            P.op("vector", lambda e: e.tensor_tensor(rho8[:], rho8[:], rho8[:], ALU.mult), R=["rho8"], W=["rho8"])
            P.op("vector", lambda e: e.tensor_tensor(rho8[:], rho8[:], rho8[:], ALU.mult), R=["rho8"], W=["rho8"])
            P.op("vector", lambda e: e.tensor_scalar(tau8[:], tauB[:], 8.0, None, ALU.mult), R=["B_th"], W=["tau8"])
            P.op("vector", lambda e: e.tensor_scalar(tb0, tau8[:], MAGIC, -MAGIC, ALU.add, ALU.add), R=["tau8"] + LpKeys, W=["cm_t0"])
            P.op("vector", lambda e: e.tensor_tensor(tau8[:], tau8[:], tb0, ALU.subtract), R=["tau8", "cm_t0"], W=["tau8"])
            def bc16(t):
                return V(t, 0, 128, 0, [[1, 16], [0, 16]])

            def q16(t, comp):
                return V(t, 0, 128, comp * 256, [[16, 16], [1, 16]])
            g0 = V(big0, 0, 128, 0, [[16, 16], [1, 16]]); g1 = V(big1, 0, 128, 0, [[16, 16], [1, 16]])
            cmul(q16(BbB, 0), q16(BbB, 1), bc16(frB), bc16(fiB), q16(BBt, 0), q16(BBt, 1), g0, g1,
                 ["B_cos", "B_sin", "BBt"], ["BbB0"], ["BbB1"])
            def lpb(comp):
                return V(LpB, 0, 128, comp * 144, [[16, 9], [1, 16], [0, 16]])

            def cbb(comp):
                return V(CBt, 0, 128, comp * 256, [[0, 9], [16, 16], [1, 16]])

            def hcv(comp):
                return V(Hc, 0, 128, comp * 2304, [[256, 9], [16, 16], [1, 16]])
            h0 = V(big0, 0, 128, 0, [[256, 9], [16, 16], [1, 16]]); h1 = V(big1, 0, 128, 0, [[256, 9], [16, 16], [1, 16]])
            cmul(hcv(0), hcv(1), cbb(0), cbb(1), lpb(0), lpb(1), h0, h1, ["CBt"] + LpKeys, ["Hc0"], ["Hc1"], neg_im=True)
            yield

            for comp, sd in ((0, st_re), (1, st_im)):
                P.dma("sync", lambda e, sd=sd: e.dma_start(out=V(stage, 0, NS, 0, [[1, 2048]]), in_=sd.ap()), W=["cm_t0"])
                pb = nextps()
                for qg in range(16):
                    P.op("tensor", lambda e, qg=qg, pb=pb: e.matmul(
                        V(PS[pb], 0, 128, qg * NS, [[1, NS]]),
                        V(stage, 0, NS, qg * 128, [[1, 128]]),
                        V(ident_f, 0, NS, 0, [[1, NS]]), start=True, stop=True),
                        R=["cm_t0", "ident_f"], W=[("ps", pb)])
                P.op("vector", lambda e, comp=comp, pb=pb: e.tensor_copy(
                    V(sS, 0, 128, comp * 256, [[1, 256]]), V(PS[pb], 0, 128, 0, [[1, 256]])),
                    R=[("ps", pb)], W=["sS%d" % comp])

            def bcB(t):
                return V(t, 0, 128, 0, [[1, 16], [0, NS]])

            def s3(t, comp):
                return V(t, 0, 128, comp * 256, [[NS, 16], [1, NS]])

            def s3t(t):
                return V(t, 0, 128, 0, [[NS, 16], [1, NS]])
            cmul(s3(sN, 0), s3(sN, 1), s3(sS, 0), s3(sS, 1), bcB(lrB), bcB(liB), s3t(sT1), s3t(sT2),
                 ["sS0", "sS1", "B_dt", "B_tmp"], ["sN0"], ["sN1"])

        pgen = prep_gen()
        stA = ExitStack(); stacks.append(stA)
        xt = [sb("xt%d" % i, [D], F32, stA) for i in range(3)]
        hb = [sb("hb%d" % i, [D], BF16, stA) for i in range(2)]
        junk = sb("junk", [D], BF16, stA)
        ssA = sb("ssA", [40], F32, stA)
        gpre = sb("gpre", [D], F32, stA)
        P.dma("sync", lambda e: e.dma_start(out=gpre[:], in_=DR(rows, 0, [[0, 128], [1, D]])), W=["gpre"])

        def sA1(ti):
            r0, nt = ttiles[ti]; b = ti % 3
            src = DR(xp, r0 * D, [[D, nt], [1, D]]) if r0 < SEQ else DR(xs, 0, [[D, nt], [1, D]])
            P.dma("sync", lambda e: e.dma_start(out=V(xt[b], 0, nt, 0, [[1, D]]), in_=src), W=[("xt", b)])
            P.op("scalar", lambda e: e.activation(
                out=V(junk, 0, nt, 0, [[1, D]]), in_=V(xt[b], 0, nt, 0, [[1, D]]), func=AF.Square,
                accum_out=V(ssA, 0, nt, ti, [[1, 1]])), R=[("xt", b)], W=["junk", ("ssA", ti)])
            P.op("scalar", lambda e: e.activation(
                out=V(ssA, 0, nt, ti, [[1, 1]]), in_=V(ssA, 0, nt, ti, [[1, 1]]), func=AF.Ln, scale=1.0 / D, bias=EPS),
                R=[("ssA", ti)], W=[("ssA", ti)])

        def sA2(ti):
            r0, nt = ttiles[ti]; b = ti % 3; bh = ti % 2
            P.op("scalar", lambda e: e.activation(
                out=V(ssA, 0, nt, ti, [[1, 1]]), in_=V(ssA, 0, nt, ti, [[1, 1]]), func=AF.Exp, scale=-0.5),
                R=[("ssA", ti)], W=[("ssA", ti)])
            P.op("vector", lambda e: e.scalar_tensor_tensor(
                V(hb[bh], 0, nt, 0, [[1, D]]), V(xt[b], 0, nt, 0, [[1, D]]), V(ssA, 0, nt, ti, [[1, 1]]),
                V(gpre, 0, nt, 0, [[1, D]]), ALU.mult, ALU.mult),
                R=[("xt", b), ("ssA", ti), "gpre"], W=[("hb", bh)])

        def sA3(ti):
            r0, nt = ttiles[ti]; b = ti % 2
            for half in range(2):
                pb = nextps()
                for kk in range(4):
                    k = half * 4 + kk
                    P.op("tensor", lambda e, k=k, kk=kk, pb=pb: e.matmul(
                        V(PS[pb], 0, 128, kk * nt, [[1, nt]]),
                        V(hb[b], 0, nt, k * 128, [[1, 128]]),
                        V(ident_b, 0, nt, 0, [[1, nt]]), start=True, stop=True),
                        R=[("hb", b), "ident_b"], W=[("ps", pb)])
                P.op("scalar", lambda e, half=half, pb=pb: e.activation(
                    out=V(hT, 0, 128, half * 4 * TT + r0, [[TT, 4], [1, nt]]),
                    in_=V(PS[pb], 0, 128, 0, [[nt, 4], [1, nt]]), func=AF.Copy),
                    R=[("ps", pb)], W=[("hT", half, ti)])
        pipeline([sA1, sA2, sA3], len(ttiles), between=lambda: next(pgen, None))
        for _ in pgen:
            pass
        ck(1)

        for (cs, cn) in tblocks:
            tis = [ti for ti, (r0, nt) in enumerate(ttiles) if cs <= r0 < cs + cn]
            hkeys = [("hT", h, ti) for h in range(2) for ti in tis]
            for oc in range(4):
                pb = nextps()
                for k in range(8):
                    mov = V(hT, 0, 128, k * TT + cs, [[1, cn]])
                    P.op("tensor", lambda e, k=k, oc=oc, cn=cn, pb=pb, mov=mov: e.matmul(
                        V(PS[pb], 0, 128, 0, [[1, cn]]),
                        V(WB[wi], 0, 128, k * 512 + oc * 128, [[1, 128]]),
                        mov, start=(k == 0), stop=(k == 7)),
                        R=[("wb", wi)] + hkeys, W=[("ps", pb)])
                dst = (V(sx, 0, 128, oc * TT + cs // 8, [[SEQ // 8, 8], [1, cn // 8]]) if cs < SEQ
                       else V(sx, 0, 128, oc * TT + cs, [[1, cn]]))
                src_ = (V(PS[pb], 0, 128, 0, [[1, 8], [8, cn // 8]]) if cs < SEQ else V(PS[pb], 0, 128, 0, [[1, cn]]))
                P.op("scalar", lambda e, dst=dst, src_=src_: e.activation(out=dst, in_=src_, func=AF.Copy),
                     R=[("ps", pb)], W=[("sx", oc) if cs < SEQ else ("sxs", oc)])
        ck(2)
        ck(3)
        P.barrier()
        stA.close()
        stP.close()

        stL = ExitStack(); stacks.append(stL)
        SLi = sb("SLi", [8, 2, 512], BF16, stL)
        YSk = sb("YSk", [4, 2, 9, 128], BF16, stL)
        BBs = sb("BBs", [4, 2, 128], BF16, stL)
        BDk = sb("BDk", [8, 128], BF16, stL)
        Spv = sb("Spv", [2, 4, 2, TC], BF16, stL)
        iot = sb("iot", [TC], F32, stL)
        tcs = [sb("tcos%d" % i, [TC], F32, stL) for i in range(2)]
        tsn = [sb("tsin%d" % i, [TC], F32, stL) for i in range(2)]
        Tg = [sb("Tg%d" % i, [TC], F32, stL) for i in range(2)]
        Wk = [sb("Wk%d" % i, [TC], F32, stL) for i in range(5)]
        Pk = [Wk[0], Wk[1]]
        ytmp = sb("ytmp", [NS], F32, stL)
        P.dma("sync", lambda e: e.dma_start(out=iot[:], in_=DR(iota_d, 0, [[SEQ, 128], [1, TC]])), W=["iot"])
        P.op("gpsimd", lambda e: e.memset(V(Spv, 0, 128, 0, [[TC, 16], [1, 1]]), 0.0), W=["Spv_z"])
        P.op("gpsimd", lambda e: e.memset(YSk[:], 0.0), W=["YSk_z"])
        P.op("gpsimd", lambda e: e.memset(BBs[:], 0.0), W=["BBs_z"])
        ysg = sx
        YB = [0, 1, 2, 3]; SLB = [4, 5]; BUS = 6; YSB = 7
        slit = [0]
        def gen_tables(qg):
            sl = qg % 2
            t8col = V(tau8, 0, 128, qg, [[1, 1]])
            P.op("vector", lambda e: e.tensor_scalar(Tg[1][:], iot[:], t8col, None, ALU.mult), R=["iot", "tau8"], W=["tg1"])
            P.op("vector", lambda e: e.tensor_scalar(Tg[0][:], Tg[1][:], MAGIC, -MAGIC, ALU.add, ALU.add), R=["tg1"], W=["tg0"])
            P.op("vector", lambda e: e.tensor_tensor(Tg[1][:], Tg[1][:], Tg[0][:], ALU.subtract), R=["tg1", "tg0"], W=["tg1"])
            P.op("scalar", lambda e: e.activation(out=tsn[sl][:], in_=Tg[1][:], func=AF.Sin, scale=TWO_PI * SHR), R=["tg1"], W=[("tsin", sl)])
            P.op("scalar", lambda e: e.activation(out=Tg[0][:], in_=Tg[1][:], func=AF.Sin, scale=PI * SHR), R=["tg1"], W=["tg0"])
            P.op("scalar", lambda e: e.activation(out=Tg[0][:], in_=Tg[0][:], func=AF.Square, scale=math.sqrt(2.0)), R=["tg0"], W=["tg0"])
            P.op("scalar", lambda e: e.activation(out=tcs[sl][:], in_=Tg[0][:], func=AF.Identity, scale=-1.0, bias=1.0), R=["tg0"], W=[("tcos", sl)])

        def expand_sli(blk):
            for g2 in range(8):
                P.op("scalar", lambda e, g2=g2: e.activation(
                    out=V(SLi, 0, 128, g2 * 64, [[512, 16], [1, 64]]),
                    in_=V(Gc, 0, 128, blk * 64, [[256, 16], [1, 64]]), func=AF.Identity,
                    scale=V(mA, 0, 128, g2, [[1, 1]])),
                    R=Gkeys + ["mA"], W=[("SLi", k) for k in range(8)])

        slb_of = {}

        def s_local(blk, q):
            pbs = SLB[slit[0] % 2]; slit[0] += 1
            slb_of[(blk, q)] = pbs
            for comp in range(2):
                for i in range(8):
                    P.op("tensor", lambda e, comp=comp, i=i: e.matmul(
                        V(PS[pbs], 0, 128, comp * TC, [[1, TC]]),
                        V(SLi, 0, 128, ((7 - i) * 2 + comp) * 512 + q * 128, [[1, 128]]),
                        V(sx, 0, 128, blk * TT + i * TC, [[1, TC]]), start=(i == 0), stop=(i == 7)),
                        R=[("SLi", 7 - i), ("sx", blk)], W=[("ps", pbs)])

        def sample_bu(blk):
            for q in range(4):
                for comp in range(2):
                    P.op("tensor", lambda e, comp=comp, q=q: e.matmul(
                        V(PS[BUS], 0, 128, (blk % 2) * 128 + (q * 2 + comp) * NS, [[1, NS]]),
                        V(SLi, 0, 128, (0 * 2 + comp) * 512 + q * 128, [[1, 128]]),
                        V(sx, 0, 128, blk * TT + SEQ, [[1, NS]]), start=True, stop=True),
                        R=[("SLi", 0), ("sxs", blk)], W=[("ps", BUS)])

        expand_sli(0)
        sample_bu(0)
        s_local(0, 0)
        for blk in range(4):
            for q in range(4):
                qg = blk * 4 + q
                for comp in range(2):
                    for par in range(2):
                        P.op("gpsimd", lambda e, q=q, qg=qg, comp=comp, par=par: e.tensor_copy(
                            V(YSk, 64 * par, 64, (q * 2 + comp) * 9 * 128 + (2 * q + par) * 16, [[128, 9], [1, 16]]),
                            V(Hc, 64 * par, 64, comp * 2304 + qg * 16, [[256, 9], [1, 16]])),
                            R=["Hc%d" % comp, "YSk_z"], W=["YSk"])
                        P.op("gpsimd", lambda e, q=q, qg=qg, comp=comp, par=par: e.tensor_copy(
                            V(BBs, 64 * par, 64, (q * 2 + comp) * 128 + (2 * q + par) * 16, [[1, 16]]),
                            V(BbB, 64 * par, 64, comp * 256 + qg * 16, [[1, 16]])),
                            R=["BbB%d" % comp, "BBs_z"], W=["BBs"])
            ck(4 if blk == 0 else -1)
            dcol = V(cv4, 0, 128, 4 + blk, [[1, 1]])
            ck(5 if blk == 0 else -1)
            for q in range(4):
                qg = blk * 4 + q
                pbs = slb_of[(blk, q)]
                if q < 3:
                    s_local(blk, q + 1)
                elif blk < 3:
                    expand_sli(blk + 1)
                    sample_bu(blk + 1)
                    s_local(blk + 1, 0)
                if qg == 0:
                    gen_tables(0)
                if qg + 1 < 16:
                    gen_tables(qg + 1)
                sl = qg % 2
                tcos = tcs[sl]; tsin = tsn[sl]
                kc_ = ("tcos", sl); ks_ = ("tsin", sl)
                br = V(PS[pbs], 0, 128, 0, [[1, TC]]); bi = V(PS[pbs], 0, 128, TC, [[1, TC]])
                kp = ("ps", pbs)
                P.op("vector", lambda e, br=br, tcos=tcos, tsin=tsin: e.tensor_tensor(Wk[0][:], tcos[:], br, ALU.mult), R=[kc_, kp], W=["w0"])
                P.op("vector", lambda e, bi=bi, tcos=tcos, tsin=tsin: e.tensor_tensor(Wk[1][:], tsin[:], bi, ALU.mult), R=[ks_, kp], W=["w1"])
                P.op("vector", lambda e: e.tensor_tensor(Wk[0][:], Wk[0][:], Wk[1][:], ALU.add), R=["w0", "w1"], W=["w0"])
                P.op("vector", lambda e, bi=bi, tcos=tcos, tsin=tsin: e.tensor_tensor(Wk[2][:], tcos[:], bi, ALU.mult), R=[kc_, kp], W=["w2"])
                P.op("vector", lambda e, br=br, tcos=tcos, tsin=tsin: e.tensor_tensor(Wk[1][:], tsin[:], br, ALU.mult), R=[ks_, kp], W=["w1"])
                P.op("vector", lambda e: e.tensor_tensor(Wk[2][:], Wk[2][:], Wk[1][:], ALU.subtract), R=["w2", "w1"], W=["w2"])
                rbc = V(rho8, 0, 128, qg, [[0, TC]])
                P.op("vector", lambda e, rbc=rbc: e.tensor_tensor_scan(Wk[3][:], rbc, Wk[0][:], 0.0, ALU.mult, ALU.add), R=["w0", "rho8"], W=["wk3"])
                P.op("vector", lambda e, rbc=rbc: e.tensor_tensor_scan(Wk[4][:], rbc, Wk[2][:], 0.0, ALU.mult, ALU.add), R=["w2", "rho8"], W=["wk4"])
                P.op("vector", lambda e, tcos=tcos: e.tensor_tensor(Wk[0][:], tcos[:], Wk[3][:], ALU.mult), R=[kc_, "wk3"], W=["w0"])
                P.op("vector", lambda e, tsin=tsin: e.tensor_tensor(Wk[1][:], tsin[:], Wk[4][:], ALU.mult), R=[ks_, "wk4"], W=["w1"])
                P.op("vector", lambda e, q=q, blk=blk: e.tensor_tensor(V(Spv, 0, 128, ((blk % 2) * 8 + q * 2 + 0) * TC + 1, [[1, TC - 1]]),
                                                              V(Wk[0], 0, 128, 0, [[1, TC - 1]]), V(Wk[1], 0, 128, 0, [[1, TC - 1]]), ALU.subtract),
                     R=["w0", "w1", "Spv_z"], W=[("Spv", blk % 2, q, 0)])
                P.op("vector", lambda e, qg=qg: e.tensor_tensor(V(fin, 0, 128, qg, [[1, 1]]), V(Wk[0], 0, 128, TC - 1, [[1, 1]]),
                                                                V(Wk[1], 0, 128, TC - 1, [[1, 1]]), ALU.subtract),
                     R=["w0", "w1"], W=[("fin", 0, qg)])
                P.op("vector", lambda e, tsin=tsin: e.tensor_tensor(Pk[0][:], tsin[:], Wk[3][:], ALU.mult), R=[ks_, "wk3"], W=["w0"])
                P.op("vector", lambda e, tcos=tcos: e.tensor_tensor(Pk[1][:], tcos[:], Wk[4][:], ALU.mult), R=[kc_, "wk4"], W=["w1"])
                P.op("vector", lambda e, q=q, blk=blk: e.tensor_tensor(V(Spv, 0, 128, ((blk % 2) * 8 + q * 2 + 1) * TC + 1, [[1, TC - 1]]),
                                                              V(Pk[0], 0, 128, 0, [[1, TC - 1]]), V(Pk[1], 0, 128, 0, [[1, TC - 1]]), ALU.add),
                     R=["w0", "w1", "Spv_z"], W=[("Spv", blk % 2, q, 1)])
                P.op("vector", lambda e, qg=qg: e.tensor_tensor(V(fin, 0, 128, 16 + qg, [[1, 1]]), V(Pk[0], 0, 128, TC - 1, [[1, 1]]),
                                                                V(Pk[1], 0, 128, TC - 1, [[1, 1]]), ALU.add),
                     R=["w0", "w1"], W=[("fin", 1, qg)])
                ck(6 if (blk == 0 and q == 0) else -1)
                for comp in range(2):
                    P.op("vector", lambda e, comp=comp, qg=qg, q=q, blk=blk: e.tensor_tensor(
                        V(sN, 0, 128, comp * 256 + qg * NS, [[1, NS]]), V(sN, 0, 128, comp * 256 + qg * NS, [[1, NS]]),
                        V(PS[BUS], 0, 128, (blk % 2) * 128 + (q * 2 + comp) * NS, [[1, NS]]), ALU.add),
                        R=["sN%d" % comp, ("ps", BUS)], W=["sN%d" % comp])
                    P.op("vector", lambda e, comp=comp, qg=qg: e.tensor_copy(
                        V(sNb, 0, 128, comp * 256 + qg * NS, [[1, NS]]), V(sN, 0, 128, comp * 256 + qg * NS, [[1, NS]])),
                        R=["sN%d" % comp], W=["sNb%d" % comp])
                for comp in range(2):
                    P.op("tensor", lambda e, comp=comp, qg=qg, q=q: e.matmul(
                        V(PS[YSB], 0, 128, 0, [[1, NS]]),
                        V(YSk, 0, 128, ((q * 2 + comp) * 9 + 0) * 128, [[1, 128]]),
                        V(sNb, 0, 128, comp * 256 + qg * NS, [[1, NS]]),
                        start=(q == 0 and comp == 0), stop=(q == 3 and comp == 1)),
                        R=["YSk", "sNb%d" % comp], W=[("ps", YSB)])
            for hb_ in range(2):
                ck(43 if (blk == 0 and hb_ == 1) else -1)
                pbk = YB[hb_]
                for k4 in range(4):
                    k = hb_ * 4 + k4
                    for q in range(4):
                        for comp in range(2):
                            P.op("tensor", lambda e, k=k, k4=k4, q=q, comp=comp, pbk=pbk: e.matmul(
                                V(PS[pbk], 0, 128, k4 * 128, [[1, 128]]),
                                V(BBs, 0, 128, (q * 2 + comp) * 128, [[1, 128]]),
                                V(YSk, 0, 128, ((q * 2 + comp) * 9 + k) * 128, [[1, 128]]),
                                start=(q == 0 and comp == 0), stop=(q == 3 and comp == 1)),
                                R=["BBs", "YSk"], W=[("ps", pbk)])
                ck(41 if (blk == 0 and hb_ == 0) else -1)
                if hb_ == 0:
                    P.op("vector", lambda e, pbk=pbk, dcol=dcol: e.scalar_tensor_tensor(
                        V(BDk, 0, 128, 0, [[1, 128]]), ident_f[:], dcol, V(PS[pbk], 0, 128, 0, [[1, 128]]), ALU.mult, ALU.add),
                        R=[("ps", pbk), "ident_f", "cv4"], W=["BDk0"])
                    ck(42 if blk == 0 else -1)
                    P.op("vector", lambda e, pbk=pbk: e.tensor_copy(
                        V(BDk, 0, 128, 128, [[1, 384]]), V(PS[pbk], 0, 128, 128, [[1, 384]])),
                        R=[("ps", pbk)], W=["BDk1"])
                else:
                    ck(44 if blk == 0 else -1)
                    P.op("vector", lambda e, pbk=pbk: e.tensor_copy(
                        V(BDk, 0, 128, 512, [[1, 512]]), V(PS[pbk], 0, 128, 0, [[1, 512]])),
                        R=[("ps", pbk)], W=["BDk2"])
            ck(7 if blk == 0 else -1)
            spk = [("Spv", blk % 2, q, comp) for q in range(4) for comp in range(2)]
            for j in range(8):
                yb = YB[j // 2]
                nmm = (j + 1) + 8
                cnt = 0
                for i in range(j + 1):
                    P.op("tensor", lambda e, i=i, j=j, yb=yb, blk=blk, cnt=cnt, nmm=nmm: e.matmul(
                        V(PS[yb], 0, 128, (j % 2) * TC, [[1, TC]]),
                        V(BDk, 0, 128, (j - i) * 128, [[1, 128]]),
                        V(sx, 0, 128, blk * TT + i * TC, [[1, TC]]), start=(cnt == 0), stop=(cnt == nmm - 1)),
                        R=["BDk0", "BDk1", "BDk2", ("sx", blk)], W=[("ps", yb)])
                    cnt += 1
                for q in range(4):
                    for comp in range(2):
                        P.op("tensor", lambda e, q=q, comp=comp, j=j, yb=yb, cnt=cnt, nmm=nmm, blk=blk: e.matmul(
                            V(PS[yb], 0, 128, (j % 2) * TC, [[1, TC]]),
                            V(YSk, 0, 128, ((q * 2 + comp) * 9 + j + 1) * 128, [[1, 128]]),
                            V(Spv, 0, 128, ((blk % 2) * 8 + q * 2 + comp) * TC, [[1, TC]]), start=(cnt == 0), stop=(cnt == nmm - 1)),
                            R=["YSk"] + spk, W=[("ps", yb)])
                        cnt += 1
            ck(8 if blk == 0 else -1)
            for b4 in range(4):
                yb = YB[b4]
                P.op("scalar", lambda e, b4=b4, yb=yb, blk=blk: e.activation(
                    out=V(ysg, 0, 128, blk * TT + 2 * b4, [[1, 2], [8, TC]]), in_=V(PS[yb], 0, 128, 0, [[TC, 2], [1, TC]]), func=AF.Gelu),
                    R=[("ps", yb)], W=[("sx", blk)])
            P.op("vector", lambda e, blk=blk, dcol=dcol: e.scalar_tensor_tensor(
                V(ytmp, 0, 128, 0, [[1, NS]]), V(sx, 0, 128, blk * TT + SEQ, [[1, NS]]), dcol,
                V(PS[YSB], 0, 128, 0, [[1, NS]]), ALU.mult, ALU.add),
                R=[("sxs", blk), "cv4", ("ps", YSB)], W=["ytmp"])
            P.op("scalar", lambda e, blk=blk: e.activation(
                out=V(ysg, 0, 128, blk * TT + SEQ, [[1, NS]]), in_=V(ytmp, 0, 128, 0, [[1, NS]]), func=AF.Gelu),
                R=["ytmp"], W=[("sxs", blk)])
        ck(9)
        prefetch("glu", w_glu, 512, 4, 0, 512)
        prefetch("sz", w_in, 6144, 8, 3584, 512)
        for comp, dd in ((0, sre_p), (1, sim_p)):
            P.dma("sync", lambda e, comp=comp, dd=dd: e.dma_start(
                out=DR(dd, 0, [[1, 128], [128, 16], [1, 1]]), in_=V(fin, 0, 128, comp * 16, [[1, 16], [1, 1]]),
                allow_slow_non_contiguous=True),
                R=[("fin", comp, qg) for qg in range(16)], W=["out_fin%d" % comp])
        for comp, dd in ((0, sre_s), (1, sim_s)):
            pbs = [nextps() for _ in range(4)]
            for qg in range(16):
                pb = pbs[qg // 4]
                P.op("tensor", lambda e, comp=comp, qg=qg, pb=pb: e.matmul(
                    V(PS[pb], 0, NS, (qg % 4) * 128, [[1, 128]]),
                    V(sN, 0, 128, comp * 256 + qg * NS, [[1, NS]]),
                    V(ident_f, 0, 128, 0, [[1, 128]]), start=True, stop=True),
                    R=["sN%d" % comp, "ident_f"], W=[("ps", pb)])
            for i4 in range(4):
                P.op("scalar", lambda e, comp=comp, i4=i4, pb=pbs[i4]: e.activation(
                    out=V(Gc, 0, NS, comp * 2048 + i4 * 512, [[1, 512]]), in_=V(PS[pb], 0, NS, 0, [[1, 512]]), func=AF.Copy),
                    R=[("ps", pbs[i4])], W=Gkeys)
            P.dma("sync", lambda e, comp=comp, dd=dd: e.dma_start(out=dd.ap(), in_=V(Gc, 0, NS, comp * 2048, [[1, 2048]])),
                  R=Gkeys, W=["out_soT%d" % comp])
        P.barrier()
        stL.close()
        stS.close()

        ys2 = sb("ys2", [4, TT], BF16, stC)
        gt1 = [sb("gt1_%d" % i, [512], F32, stC) for i in range(2)]
        gt2 = [sb("gt2_%d" % i, [512], F32, stC) for i in range(2)]
        wi_glu = getw("glu", w_glu, 512, 4, 0, 512)
        wi_sz = getw("sz", w_in, 6144, 8, 3584, 512)
        it = 0
        for oc in range(4):
            for (cs, cn) in tblocks:
                b = it % 2; it += 1
                pb = nextps()
                mm_fm(wi_glu, 4, oc * 128, ysg, "ysgall", cs, cn, pb)
                P.op("scalar", lambda e, oc=oc, cn=cn, pb=pb, b=b: e.activation(
                    out=V(gt1[b], 0, 128, 0, [[1, cn]]), in_=V(PS[pb], 0, 128, 0, [[1, cn]]), func=AF.Sigmoid,
                    bias=V(cv4, 0, 128, oc, [[1, 1]])), R=[("ps", pb), "cv4"], W=[("gt1", b)])
                pb2 = nextps()
                mm_fm(wi_sz, 8, oc * 128, hT, "hTall", cs, cn, pb2)
                P.op("scalar", lambda e, cn=cn, pb2=pb2, b=b: e.activation(
                    out=V(gt2[b], 0, 128, 0, [[1, cn]]), in_=V(PS[pb2], 0, 128, 0, [[1, cn]]), func=AF.Sigmoid),
                    R=[("ps", pb2)], W=[("gt2", b)])
                P.op("vector", lambda e, cn=cn, pb2=pb2, b=b: e.tensor_tensor(
                    V(gt2[b], 0, 128, 0, [[1, cn]]), V(gt2[b], 0, 128, 0, [[1, cn]]), V(PS[pb2], 0, 128, 0, [[1, cn]]), ALU.mult),
                    R=[("gt2", b), ("ps", pb2)], W=[("gt2", b)])
                P.op("vector", lambda e, cn=cn, b=b, oc=oc, cs=cs: e.tensor_tensor(
                    V(gt1[b], 0, 128, 0, [[1, cn]]), V(gt1[b], 0, 128, 0, [[1, cn]]), V(ysg, 0, 128, oc * TT + cs, [[1, cn]]), ALU.mult),
                    R=[("gt1", b), "ysgall"], W=[("gt1", b)])
                P.op("vector", lambda e, cn=cn, b=b, oc=oc, cs=cs: e.tensor_tensor(
                    V(ys2, 0, 128, oc * TT + cs, [[1, cn]]), V(gt1[b], 0, 128, 0, [[1, cn]]), V(gt2[b], 0, 128, 0, [[1, cn]]), ALU.mult),
                    R=[("gt1", b), ("gt2", b)], W=["ys2all"])
        for half in range(2):
            wi_ps = load_w(w_ps, D, 4, half * 512, 512)
            wi_gs = load_w(w_in, 6144, 8, 5120 + half * 512, 512)
            for o4 in range(4):
                oc = half * 4 + o4
                for (cs, cn) in tblocks:
                    b = it % 2; it += 1
                    pb = nextps()
                    mm_fm(wi_gs, 8, o4 * 128, hT, "hTall", cs, cn, pb)
                    P.op("scalar", lambda e, cn=cn, pb=pb, b=b: e.activation(
                        out=V(gt1[b], 0, 128, 0, [[1, cn]]), in_=V(PS[pb], 0, 128, 0, [[1, cn]]), func=AF.Sigmoid),
                        R=[("ps", pb)], W=[("gt1", b)])
                    pb2 = nextps()
                    mm_fm(wi_ps, 4, o4 * 128, ys2, "ys2all", cs, cn, pb2)
                    P.op("vector", lambda e, cn=cn, pb2=pb2, b=b, oc=oc, cs=cs: e.tensor_tensor(
                        V(m_t, 0, 128, oc * TT + cs, [[1, cn]]), V(gt1[b], 0, 128, 0, [[1, cn]]), V(PS[pb2], 0, 128, 0, [[1, cn]]), ALU.mult),
                        R=[("gt1", b), ("ps", pb2)], W=[("m", oc)])
        prefetch("a0", w_in, 6144, 8, 0, 512)
        prefetch("b0", w_in, 6144, 8, 1024, 512)
        P.barrier()
        stC.close()

        stB = ExitStack(); stacks.append(stB)
        UW = 30 + SEQ
        u_t = sb("u_t", [8, UW], BF16, stB)
        vs_b = sb("vs_b", [8, NS], BF16, stB)
        us_f = sb("us_f", [8, NS], F32, stB)
        ul_f = sb("ul_f", [8, 30], F32, stB)
        vs_f = sb("vs_f", [8, NS], F32, stB)
        cwt = sb("cwt", [8, CK], F32, stB)
        bt1 = [sb("bt1_%d" % i, [512], F32, stB) for i in range(2)]
        bt2 = [sb("bt2_%d" % i, [512], F32, stB) for i in range(2)]
        acc1 = sb("acc1", [TT], F32, stB)
        acc2 = sb("acc2", [TT], F32, stB)
        bt3 = [sb("bt3_%d" % i, [512], F32, stB) for i in range(3)]
        cz1 = [sb("cz1_%d" % i, [512], F32, stB) for i in range(3)]

        def vcol(c, cs, cn):
            if cs < SEQ:
                return V(u_t, 0, 128, c * UW + 30 + cs, [[1, cn]])
            return V(vs_b, 0, 128, c * NS, [[1, cn]])
        P.dma("sync", lambda e: e.dma_start(out=cwt[:], in_=convw.ap()), W=["cwt"])
        P.op("gpsimd", lambda e: e.memset(V(u_t, 0, 128, 0, [[UW, 8], [1, 30]]), 0.0), W=[("u", c, -1) for c in range(8)])
        P.op("gpsimd", lambda e: e.memset(acc1[:], 0.0), W=["acc1"])
        P.op("gpsimd", lambda e: e.memset(acc2[:], 0.0), W=["acc2"])
        for half in range(2):
            wi_a = getw("a%d" % half, w_in, 6144, 8, half * 512, 512)
            wi_b = getw("b%d" % half, w_in, 6144, 8, 1024 + half * 512, 512)
            for o4 in range(4):
                c = half * 4 + o4
                for tbi, (cs, cn) in enumerate(tblocks):
                    b = it % 2; it += 1
                    pb = nextps()
                    mm_fm(wi_b, 8, o4 * 128, hT, "hTall", cs, cn, pb)
                    P.op("scalar", lambda e, cn=cn, pb=pb, b=b: e.activation(
                        out=V(bt1[b], 0, 128, 0, [[1, cn]]), in_=V(PS[pb], 0, 128, 0, [[1, cn]]), func=AF.Sigmoid),
                        R=[("ps", pb)], W=[("bt1", b)])
                    pb2 = nextps()
                    mm_fm(wi_a, 8, o4 * 128, hT, "hTall", cs, cn, pb2)
                    if cs < SEQ:
                        P.op("vector", lambda e, cn=cn, pb2=pb2, b=b, c=c, cs=cs: e.tensor_tensor(
                            V(u_t, 0, 128, c * UW + 30 + cs, [[1, cn]]), V(bt1[b], 0, 128, 0, [[1, cn]]), V(PS[pb2], 0, 128, 0, [[1, cn]]), ALU.mult),
                            R=[("bt1", b), ("ps", pb2)], W=[("u", c, tbi)])
                        if tbi == NTB - 1:
                            P.op("vector", lambda e, pb2=pb2, b=b, c=c: e.tensor_tensor(
                                V(ul_f, 0, 128, c * 30, [[1, 30]]), V(bt1[b], 0, 128, 482, [[1, 30]]), V(PS[pb2], 0, 128, 482, [[1, 30]]), ALU.mult),
                                R=[("bt1", b), ("ps", pb2)], W=[("ul", c)])
                    else:
                        P.op("vector", lambda e, cn=cn, pb2=pb2, b=b, c=c: e.tensor_tensor(
                            V(us_f, 0, 128, c * NS, [[1, NS]]), V(bt1[b], 0, 128, 0, [[1, cn]]), V(PS[pb2], 0, 128, 0, [[1, cn]]), ALU.mult),
                            R=[("bt1", b), ("ps", pb2)], W=[("us", c)])
        stB1 = ExitStack(); stacks.append(stB1)
        nrows = NS * 30
        cacheT = sb("cacheT", [8, nrows], F32, stB1)
        crow = [sb("crow%d" % i, [D], F32, stB1) for i in range(2)]
        ctmp = sb("ctmp", [nrows], F32, stB1)
        otm = sb("otm", [D], F32, stB1)
        for half in range(2):
            pb = nextps()
            for kk in range(4):
                c = half * 4 + kk
                P.op("tensor", lambda e, c=c, kk=kk, pb=pb: e.matmul(
                    V(PS[pb], 0, 30, kk * 128, [[1, 128]]), V(ul_f, 0, 128, c * 30, [[1, 30]]),
                    V(ident_f, 0, 128, 0, [[1, 128]]), start=True, stop=True),
                    R=[("ul", c), "ident_f"], W=[("ps", pb)])
            P.op("scalar", lambda e, half=half, pb=pb: e.activation(
                out=V(otm, 0, 30, half * 512, [[1, 512]]), in_=V(PS[pb], 0, 30, 0, [[1, 512]]), func=AF.Copy),
                R=[("ps", pb)], W=[("otm", half)])
        P.dma("sync", lambda e: e.dma_start(out=conv_p.ap(), in_=V(otm, 0, 30, 0, [[1, D]])),
              R=[("otm", 0), ("otm", 1)], W=["otm_out"])
        P.dma("sync", lambda e: e.dma_start(out=DR(conv_s, 0, [[30 * D, NS], [1, 29 * D]]),
                                            in_=DR(cache, D, [[30 * D, NS], [1, 29 * D]])), W=["conv_s_a"])
        for half in range(2):
            pb = nextps()
            for kk in range(4):
                c = half * 4 + kk
                P.op("tensor", lambda e, c=c, kk=kk, pb=pb: e.matmul(
                    V(PS[pb], 0, NS, kk * 128, [[1, 128]]), V(us_f, 0, 128, c * NS, [[1, NS]]),
                    V(ident_f, 0, 128, 0, [[1, 128]]), start=True, stop=True),
                    R=[("us", c), "ident_f"], W=[("ps", pb)])
            P.op("scalar", lambda e, half=half, pb=pb: e.activation(
                out=V(otm, 32, NS, half * 512, [[1, 512]]), in_=V(PS[pb], 0, NS, 0, [[1, 512]]), func=AF.Copy),
                R=[("ps", pb)], W=[("otm2", half)])
        P.dma("sync", lambda e: e.dma_start(out=DR(conv_s, 29 * D, [[30 * D, NS], [1, D]]), in_=V(otm, 32, NS, 0, [[1, D]])),
              R=[("otm2", 0), ("otm2", 1)], W=["conv_s_b"])
        rtiles = [(r, min(128, nrows - r)) for r in range(0, nrows, 128)]
        for ri, (r0, nr) in enumerate(rtiles):
            b = ri % 2
            P.dma("sync", lambda e, r0=r0, nr=nr, b=b: e.dma_start(out=V(crow[b], 0, nr, 0, [[1, D]]),
                                                                  in_=DR(cache, r0 * D, [[D, nr], [1, D]])), W=[("crow", b)])
            for c in range(8):
                pb = nextps()
                P.op("tensor", lambda e, c=c, pb=pb, nr=nr, b=b: e.matmul(
                    V(PS[pb], 0, 128, 0, [[1, nr]]), V(crow[b], 0, nr, c * 128, [[1, 128]]),
                    V(ident_f, 0, nr, 0, [[1, nr]]), start=True, stop=True),
                    R=[("crow", b), "ident_f"], W=[("ps", pb)])
                P.op("scalar", lambda e, c=c, pb=pb, nr=nr, r0=r0: e.activation(
                    out=V(cacheT, 0, 128, c * nrows + r0, [[1, nr]]), in_=V(PS[pb], 0, 128, 0, [[1, nr]]), func=AF.Copy),
                    R=[("ps", pb)], W=[("cacheT", c)])
        for c in range(8):
            P.op("vector", lambda e, c=c: e.tensor_tensor(
                V(ctmp, 0, 128, 0, [[30, NS], [1, 30]]), V(cacheT, 0, 128, c * nrows, [[30, NS], [1, 30]]),
                V(cwt, 0, 128, c * CK, [[0, NS], [1, 30]]), ALU.mult), R=[("cacheT", c), "cwt"], W=["ctmp"])
            P.op("vector", lambda e, c=c: e.tensor_reduce(
                V(vs_f, 0, 128, c * NS, [[1, NS]]), V(ctmp, 0, 128, 0, [[30, NS], [1, 30]]), mybir.AxisListType.X, ALU.add),
                R=["ctmp"], W=[("vs", c)])
            P.op("vector", lambda e, c=c: e.scalar_tensor_tensor(
                V(vs_f, 0, 128, c * NS, [[1, NS]]), V(us_f, 0, 128, c * NS, [[1, NS]]), V(cwt, 0, 128, c * CK + 30, [[1, 1]]),
                V(vs_f, 0, 128, c * NS, [[1, NS]]), ALU.mult, ALU.add), R=[("vs", c), ("us", c), "cwt"], W=[("vs", c)])
        P.barrier()
        stB1.close()
        stB2 = ExitStack(); stacks.append(stB2)
        dg = [sb("dg%d" % i, [CK, 128], BF16, stB2) for i in range(2)]
        vsq = [sb("vsq%d" % i, [512], BF16, stB2) for i in range(2)]
        msq = sb("msq", [512], F32, stB2)
        ND = 5
        accD = [sb("accD%d" % i, [SEQ], F32, stB2) for i in range(2)]
        conv_items = []
        for c in range(8):
            for tbi, (cs, cn) in reversed(list(enumerate(tblocks))):
                conv_items.append((c, tbi, cs, cn))
        conv_state = {}

        def build_dg(c):
            d_ = dg[c % 2]
            for k in range(ND, CK):
                P.op("scalar", lambda e, k=k: e.activation(
                    out=V(d_, 0, 128, k * 128, [[1, 128]]), in_=ident_f[:], func=AF.Identity, scale=V(cwt, 0, 128, c * CK + k, [[1, 1]])),
                    R=["ident_f", "cwt"], W=[("dg", c % 2, k)])

        def dve_taps(c, ks):
            a_ = accD[c % 2]
            ukeys = [("u", c, t) for t in range(-1, NTB)]
            for k in ks:
                srcu = V(u_t, 0, 128, c * UW + k, [[1, SEQ]])
                wcol = V(cwt, 0, 128, c * CK + k, [[1, 1]])
                if k == 0:
                    P.op("vector", lambda e, srcu=srcu, wcol=wcol: e.tensor_scalar(a_[:], srcu, wcol, None, ALU.mult),
                         R=ukeys + ["cwt"], W=[("accD", c % 2)])
                else:
                    P.op("vector", lambda e, srcu=srcu, wcol=wcol: e.scalar_tensor_tensor(a_[:], srcu, wcol, a_[:], ALU.mult, ALU.add),
                         R=ukeys + ["cwt", ("accD", c % 2)], W=[("accD", c % 2)])

        def sC1(ii):
            c, tbi, cs, cn = conv_items[ii]
            d_ = dg[c % 2]
            if ii == 0:
                build_dg(0)
                dve_taps(0, range(ND))
            if ii % len(tblocks) == 1 and c + 1 < 8:
                build_dg(c + 1)
            slots = list(range(1, len(tblocks)))[:3]
            per = -(-ND // len(slots))
            if c + 1 < 8 and (ii % len(tblocks)) in slots:
                j = slots.index(ii % len(tblocks))
                dve_taps(c + 1, range(per * j, min(ND, per * j + per)))
            b = ii % 2
            bias = V(cv8, 0, 128, c, [[1, 1]])
            if cs < SEQ:
                pb = nextps()
                for k in range(ND, CK):
                    P.op("tensor", lambda e, k=k, pb=pb: e.matmul(
                        V(PS[pb], 0, 128, 0, [[1, 512]]), V(d_, 0, 128, k * 128, [[1, 128]]),
                        V(u_t, 0, 128, c * UW + cs + k, [[1, 512]]), start=(k == ND), stop=(k == CK - 1)),
                        R=[("dg", c % 2, k), ("u", c, tbi), ("u", c, tbi - 1)], W=[("ps", pb)])
                P.op("vector", lambda e, pb=pb: e.tensor_tensor(
                    V(bt2[b], 0, 128, 0, [[1, cn]]), V(PS[pb], 0, 128, 0, [[1, cn]]), V(accD[c % 2], 0, 128, cs, [[1, cn]]), ALU.add),
                    R=[("ps", pb), ("accD", c % 2)], W=[("bt2", b)])
                src = V(bt2[b], 0, 128, 0, [[1, cn]]); rk = ("bt2", b)
            else:
                src = V(vs_f, 0, 128, c * NS, [[1, NS]]); rk = ("vs", c)
            P.op("scalar", lambda e: e.activation(
                out=vcol(c, cs, cn), in_=src, func=AF.Identity, bias=bias),
                R=[rk, "cv8"], W=[("u", c, tbi)])
            P.op("scalar", lambda e: e.activation(
                out=V(vsq[b], 0, 128, 0, [[1, cn]]), in_=src, func=AF.Square, bias=bias),
                R=[rk, "cv8"], W=[("vsq", b)])

        def sC2(ii):
            c, tbi, cs, cn = conv_items[ii]
            b = ii % 2
            p1 = nextps(); p2 = nextps()
            P.op("tensor", lambda e: e.matmul(
                V(PS[p1], 0, 128, 0, [[1, cn]]), ones_b[:], vcol(c, cs, cn), start=True, stop=True),
                R=["ones_b", ("u", c, tbi)], W=[("ps", p1)])
            P.op("tensor", lambda e: e.matmul(
                V(PS[p2], 0, 128, 0, [[1, cn]]), ones_b[:], V(vsq[b], 0, 128, 0, [[1, cn]]), start=True, stop=True),
                R=["ones_b", ("vsq", b)], W=[("ps", p2)])
            P.op("vector", lambda e: e.tensor_tensor(
                V(acc1, 0, 128, cs, [[1, cn]]), V(acc1, 0, 128, cs, [[1, cn]]), V(PS[p1], 0, 128, 0, [[1, cn]]), ALU.add),
                R=[("ps", p1), ("acc1", tbi)], W=[("acc1", tbi)])
            P.op("vector", lambda e: e.tensor_tensor(
                V(acc2, 0, 128, cs, [[1, cn]]), V(acc2, 0, 128, cs, [[1, cn]]), V(PS[p2], 0, 128, 0, [[1, cn]]), ALU.add),
                R=[("ps", p2), ("acc2", tbi)], W=[("acc2", tbi)])
        pipeline([sC1, sC2], len(conv_items))
        for tbi, (cs, cn) in enumerate(tblocks):
            a1 = V(acc1, 0, 128, cs, [[1, cn]]); a2 = V(acc2, 0, 128, cs, [[1, cn]]); mq = V(msq, 0, 128, 0, [[1, cn]])
            k1 = ("acc1", tbi); k2 = ("acc2", tbi)
            P.op("vector", lambda e, a1=a1: e.tensor_scalar(a1, a1, 1.0 / D, None, ALU.mult), R=[k1, "acc1"], W=[k1])
            P.op("vector", lambda e, a2=a2: e.tensor_scalar(a2, a2, 1.0 / D, EPS, ALU.mult, ALU.add), R=[k2, "acc2"], W=[k2])
            P.op("vector", lambda e, a1=a1, mq=mq: e.tensor_tensor(mq, a1, a1, ALU.mult), R=[k1], W=["msq"])
            P.op("vector", lambda e, a2=a2, mq=mq: e.tensor_tensor(a2, a2, mq, ALU.subtract), R=[k2, "msq"], W=[k2])
            P.op("scalar", lambda e, a2=a2: e.activation(out=a2, in_=a2, func=AF.Ln), R=[k2], W=[k2])
            P.op("scalar", lambda e, a2=a2: e.activation(out=a2, in_=a2, func=AF.Exp, scale=-0.5), R=[k2], W=[k2])
            P.op("vector", lambda e, a1=a1, a2=a2: e.scalar_tensor_tensor(a1, a1, -1.0, a2, ALU.mult, ALU.mult), R=[k1, k2], W=[k1])
        v2_items = []
        for half in range(2):
            for o4 in range(4):
                for tbi, (cs, cn) in enumerate(tblocks):
                    v2_items.append((half, o4, tbi, cs, cn))
        wz = {}

        def sV1(ii):
            half, o4, tbi, cs, cn = v2_items[ii]
            c = half * 4 + o4
            if o4 == 0 and tbi == 0:
                wz[half] = getw("z%d" % half, w_in, 6144, 8, 2048 + half * 512, 512)
            b = ii % 3
            pb = nextps()
            mm_fm(wz[half], 8, o4 * 128, hT, "hTall", cs, cn, pb)
            P.op("scalar", lambda e: e.activation(
                out=V(cz1[b], 0, 128, 0, [[1, cn]]), in_=V(PS[pb], 0, 128, 0, [[1, cn]]), func=AF.Silu),
                R=[("ps", pb)], W=[("cz1", b)])
            P.op("vector", lambda e: e.tensor_tensor(
                V(bt3[b], 0, 128, 0, [[1, cn]]), vcol(c, cs, cn), V(acc2, 0, 128, cs, [[1, cn]]), ALU.mult),
                R=[("u", c, tbi), ("acc2", tbi)], W=[("bt3", b)])
            P.op("vector", lambda e: e.tensor_tensor(
                V(bt3[b], 0, 128, 0, [[1, cn]]), V(bt3[b], 0, 128, 0, [[1, cn]]), V(acc1, 0, 128, cs, [[1, cn]]), ALU.add),
                R=[("bt3", b), ("acc1", tbi)], W=[("bt3", b)])

        def sV2(ii):
            half, o4, tbi, cs, cn = v2_items[ii]
            c = half * 4 + o4
            b = ii % 3
            P.op("scalar", lambda e: e.activation(
                out=V(bt3[b], 0, 128, 0, [[1, cn]]), in_=V(bt3[b], 0, 128, 0, [[1, cn]]), func=AF.Silu,
                scale=V(cv8, 0, 128, 8 + c, [[1, 1]]), bias=V(cv8, 0, 128, 16 + c, [[1, 1]])),
                R=[("bt3", b), "cv8"], W=[("bt3", b)])
            P.op("vector", lambda e: e.tensor_tensor(
                vcol(c, cs, cn), V(bt3[b], 0, 128, 0, [[1, cn]]), V(cz1[b], 0, 128, 0, [[1, cn]]), ALU.mult),
                R=[("bt3", b), ("cz1", b)], W=[("u", c, tbi)])
        pipeline([sV1, sV2], len(v2_items))
        v2keys_all = {tbi: [("u", c, tbi) for c in range(8)] for tbi in range(len(tblocks))}
        for half in range(2):
            wi_pc = load_w(w_pc, D, 8, half * 512, 512)
            wi_gc = load_w(w_in, 6144, 8, 4096 + half * 512, 512)
            for o4 in range(4):
                oc = half * 4 + o4
                for tbi_, (cs, cn) in enumerate(tblocks):
                    v2keys = v2keys_all[tbi_]
                    b = it % 2; it += 1
                    pb = nextps()
                    mm_fm(wi_gc, 8, o4 * 128, hT, "hTall", cs, cn, pb)
                    P.op("scalar", lambda e, cn=cn, pb=pb, b=b: e.activation(
                        out=V(bt1[b], 0, 128, 0, [[1, cn]]), in_=V(PS[pb], 0, 128, 0, [[1, cn]]), func=AF.Sigmoid),
                        R=[("ps", pb)], W=[("bt1", b)])
                    pb2 = nextps()
                    buf = WB[wi_pc]
                    for k in range(8):
                        P.op("tensor", lambda e, k=k, buf=buf, o4=o4, cs=cs, cn=cn, pb2=pb2: e.matmul(
                            V(PS[pb2], 0, 128, 0, [[1, cn]]), V(buf, 0, 128, k * 512 + o4 * 128, [[1, 128]]),
                            vcol(k, cs, cn), start=(k == 0), stop=(k == 7)),
                            R=[("wb", wi_pc)] + v2keys, W=[("ps", pb2)])
                    P.op("vector", lambda e, cn=cn, pb2=pb2, b=b: e.tensor_tensor(
                        V(bt1[b], 0, 128, 0, [[1, cn]]), V(bt1[b], 0, 128, 0, [[1, cn]]), V(PS[pb2], 0, 128, 0, [[1, cn]]), ALU.mult),
                        R=[("bt1", b), ("ps", pb2)], W=[("bt1", b)])
                    P.op("vector", lambda e, cn=cn, b=b, oc=oc, cs=cs: e.tensor_tensor(
                        V(m_t, 0, 128, oc * TT + cs, [[1, cn]]), V(m_t, 0, 128, oc * TT + cs, [[1, cn]]), V(bt1[b], 0, 128, 0, [[1, cn]]), ALU.add),
                        R=[("bt1", b), ("m", oc)], W=[("m", oc)])
        wo = [load_w(w_out, D, 8, h2 * 512, 512) for h2 in range(2)]
        wg = [load_w(w_pg, D, 8, h2 * 512, 512) for h2 in range(2)]
        P.barrier()
        stB2.close()
        stB.close()

        stH.close()
        stD = ExitStack(); stacks.append(stD)
        wple_b = sb("wple_b", [2, D], BF16, stD)
        bpg_b = sb("bpg_b", [D], BF16, stD)
        P.dma("gpsimd", lambda e: e.dma_start(out=V(bpg_b, 0, 1, 0, [[1, D]]),
                                              in_=DR(rows, 3 * D, [[0, 1], [1, D]])), W=["bpg_b"])
        rows2 = sb("rows2", [2, D], F32, stD)
        P.dma("sync", lambda e: e.dma_start(out=V(rows2, 0, 128, 0, [[1, 2 * D]]), in_=DR(rows, D, [[0, 128], [1, 2 * D]])), W=["rows2"])
        for (dst, wd, kc, nm) in ((wple_b, w_ple, 2, "wple"),):
            for h2 in range(2):
                P.dma("gpsimd", lambda e, dst=dst, wd=wd, kc=kc, h2=h2: e.dma_start(
                    out=V(dst, 0, 128, h2 * 512, [[D, kc], [1, 512]]),
                    in_=DR(wd, h2 * 512, [[D, 128], [128 * D, kc], [1, 512]])), W=[(nm, h2)])
        RX, RE, R1, RB = 4, 5, 3, 3
        xd = [sb("xd%d" % i, [D], F32, stD) for i in range(RX)]
        osb = [sb("osb%d" % i, [D], F32, stD) for i in range(RX)]
        en = [sb("en%d" % i, [D], F32, stD) for i in range(RE)]
        x1 = [sb("x1_%d" % i, [D], F32, stD) for i in range(R1)]
        pd = [sb("pd%d" % i, [256], BF16, stD) for i in range(RB)]
        pT = [sb("pT%d" % i, [2, 128], BF16, stD) for i in range(RB)]
        x1b = [sb("x1b%d" % i, [D], BF16, stD) for i in range(RB)]
        x1T = [sb("x1T%d" % i, [8, 128], BF16, stD) for i in range(RB)]
        gsig = [sb("gsig%d" % i, [D], F32, stD) for i in range(2)]
        yo = [sb("yo%d" % i, [D], F32, stD) for i in range(2)]
        jks = [sb("jk%d" % i, [512], BF16, stD) for i in range(4)]
        ssD = sb("ssD", [len(ttiles), 8], F32, stD)
        mkeys = [("m", oc) for oc in range(8)]
        tst = {}

        def col(ti, nt, c0):
            return V(ssD, 0, nt, ti * 8 + c0, [[1, 1]])

        def tl1(ti):
            r0, nt = ttiles[ti]
            srcx = DR(xp, r0 * D, [[D, nt], [1, D]]) if r0 < SEQ else DR(xs, 0, [[D, nt], [1, D]])
            srcp = DR(pp, r0 * 256, [[256, nt], [1, 256]]) if r0 < SEQ else DR(psm, 0, [[256, nt], [1, 256]])
            bx = ti % RX; bb = ti % RB
            P.dma("sync", lambda e: e.dma_start(out=V(xd[bx], 0, nt, 0, [[1, D]]), in_=srcx), W=[("xd", bx)])
            P.dma("gpsimd", lambda e: e.dma_start(out=V(pd[bb], 0, nt, 0, [[1, 256]]), in_=srcp), W=[("pd", bb)])
            ob = [nextps(), nextps()]
            for h2 in range(2):
                for k in range(8):
                    P.op("tensor", lambda e, k=k, h2=h2: e.matmul(
                        V(PS[ob[h2]], 0, nt, 0, [[1, 512]]), V(m_t, 0, 128, k * TT + r0, [[1, nt]]),
                        V(WB[wo[h2]], 0, 128, k * 512, [[1, 512]]), start=(k == 0), stop=(k == 7)),
                        R=mkeys + [("wb", wo[h2])], W=[("ps", ob[h2])])
                P.op("scalar", lambda e, h2=h2: e.activation(
                    out=V(jks[h2], 0, nt, 0, [[1, 512]]), in_=V(PS[ob[h2]], 0, nt, 0, [[1, 512]]), func=AF.Square,
                    accum_out=col(ti, nt, h2)), R=[("ps", ob[h2])], W=[("jk", h2), ("ssD", ti, h2)])
                P.op("scalar", lambda e, h2=h2: e.activation(
                    out=V(osb[bx], 0, nt, h2 * 512, [[1, 512]]), in_=V(PS[ob[h2]], 0, nt, 0, [[1, 512]]), func=AF.Copy),
                    R=[("ps", ob[h2])], W=[("osb", bx, h2)])
            pb = nextps()
            for k in range(2):
                P.op("tensor", lambda e, k=k: e.matmul(
                    V(PS[pb], 0, 128, k * nt, [[1, nt]]), V(pd[bb], 0, nt, k * 128, [[1, 128]]),
                    V(ident_b, 0, nt, 0, [[1, nt]]), start=True, stop=True),
                    R=[("pd", bb), "ident_b"], W=[("ps", pb)])
            P.op("vector", lambda e: e.tensor_copy(
                V(pT[bb], 0, 128, 0, [[128, 2], [1, nt]]), V(PS[pb], 0, 128, 0, [[nt, 2], [1, nt]])),
                R=[("ps", pb)], W=[("pT", bb)])

        def tl2(ti):
            r0, nt = ttiles[ti]
            bb = ti % RB; be = ti % RE
            eb = [nextps(), nextps()]
            for h2 in range(2):
                for k in range(2):
                    P.op("tensor", lambda e, k=k, h2=h2: e.matmul(
                        V(PS[eb[h2]], 0, nt, 0, [[1, 512]]), V(pT[bb], 0, 128, k * 128, [[1, nt]]),
                        V(wple_b, 0, 128, k * D + h2 * 512, [[1, 512]]), start=(k == 0), stop=(k == 1)),
                        R=[("pT", bb), ("wple", h2)], W=[("ps", eb[h2])])
                P.op("scalar", lambda e, h2=h2: e.activation(
                    out=V(jks[2 + h2], 0, nt, 0, [[1, 512]]), in_=V(PS[eb[h2]], 0, nt, 0, [[1, 512]]), func=AF.Square,
                    accum_out=col(ti, nt, 2 + h2)), R=[("ps", eb[h2])], W=[("jk", 2 + h2), ("ssD", ti, 2 + h2)])
                P.op("scalar", lambda e, h2=h2: e.activation(
                    out=V(en[be], 0, nt, h2 * 512, [[1, 512]]), in_=V(PS[eb[h2]], 0, nt, 0, [[1, 512]]), func=AF.Copy),
                    R=[("ps", eb[h2])], W=[("en", be, h2)])
            a = col(ti, nt, 0); a2 = col(ti, nt, 1)
            P.op("vector", lambda e: e.tensor_tensor(a, a, a2, ALU.add), R=[("ssD", ti, 0), ("ssD", ti, 1)], W=[("ssD", ti, 0)])

        def tl3(ti):
            r0, nt = ttiles[ti]
            a = col(ti, nt, 0)
            P.op("scalar", lambda e: e.activation(out=a, in_=a, func=AF.Ln, scale=1.0 / D, bias=EPS), R=[("ssD", ti, 0)], W=[("ssD", ti, 0)])
            P.op("scalar", lambda e: e.activation(out=a, in_=a, func=AF.Exp, scale=-0.5), R=[("ssD", ti, 0)], W=[("ssD", ti, 0)])
            c = col(ti, nt, 2); c2 = col(ti, nt, 3)
            P.op("vector", lambda e: e.tensor_tensor(c, c, c2, ALU.add), R=[("ssD", ti, 2), ("ssD", ti, 3)], W=[("ssD", ti, 2)])

        def tl4(ti):
            r0, nt = ttiles[ti]
            bx = ti % RX; b1 = ti % R1; bb = ti % RB
            a = col(ti, nt, 0); c = col(ti, nt, 2)
            P.op("scalar", lambda e: e.activation(out=c, in_=c, func=AF.Ln, scale=1.0 / D, bias=EPS), R=[("ssD", ti, 2)], W=[("ssD", ti, 2)])
            P.op("scalar", lambda e: e.activation(out=c, in_=c, func=AF.Exp, scale=-0.5), R=[("ssD", ti, 2)], W=[("ssD", ti, 2)])
            P.op("vector", lambda e: e.scalar_tensor_tensor(
                V(x1[b1], 0, nt, 0, [[1, D]]), V(osb[bx], 0, nt, 0, [[1, D]]), a,
                V(rows2, 0, nt, 0, [[1, D]]), ALU.mult, ALU.mult),
                R=[("osb", bx, 0), ("osb", bx, 1), ("ssD", ti, 0), "rows2"], W=[("x1", b1)])
            P.op("vector", lambda e: e.tensor_tensor(
                V(x1[b1], 0, nt, 0, [[1, D]]), V(x1[b1], 0, nt, 0, [[1, D]]), V(xd[bx], 0, nt, 0, [[1, D]]), ALU.add),
                R=[("x1", b1), ("xd", bx)], W=[("x1", b1)])
            P.op("vector", lambda e: e.tensor_copy(V(x1b[bb], 0, nt, 0, [[1, D]]), V(x1[b1], 0, nt, 0, [[1, D]])),
                 R=[("x1", b1)], W=[("x1b", bb)])

        def tl5(ti):
            r0, nt = ttiles[ti]
            bb = ti % RB; be = ti % RE
            for half in range(2):
                pb = nextps()
                for kk in range(4):
                    k = half * 4 + kk
                    P.op("tensor", lambda e, k=k, kk=kk, pb=pb: e.matmul(
                        V(PS[pb], 0, 128, kk * nt, [[1, nt]]), V(x1b[bb], 0, nt, k * 128, [[1, 128]]),
                        V(ident_b, 0, nt, 0, [[1, nt]]), start=True, stop=True),
                        R=[("x1b", bb), "ident_b"], W=[("ps", pb)])
                P.op("vector", lambda e, half=half, pb=pb: e.tensor_copy(
                    V(x1T[bb], 0, 128, half * 4 * 128, [[128, 4], [1, nt]]),
                    V(PS[pb], 0, 128, 0, [[nt, 4], [1, nt]])),
                    R=[("ps", pb)], W=[("x1T", bb, half)])
            c = col(ti, nt, 2)
            P.op("vector", lambda e: e.scalar_tensor_tensor(
                V(en[be], 0, nt, 0, [[1, D]]), V(en[be], 0, nt, 0, [[1, D]]), c,
                V(rows2, 0, nt, D, [[1, D]]), ALU.mult, ALU.mult),
                R=[("en", be, 0), ("en", be, 1), ("ssD", ti, 2), "rows2"], W=[("en", be, 0), ("en", be, 1)])

        def tl6(ti):
            r0, nt = ttiles[ti]
            bb = ti % RB; be = ti % RE; b1 = ti % R1; b2 = ti % 2
            gb = [nextps(), nextps()]
            for h2 in range(2):
                P.op("tensor", lambda e, h2=h2: e.matmul(
                    V(PS[gb[h2]], 0, nt, 0, [[1, 512]]), V(ones_b, 0, 1, 0, [[1, nt]]),
                    V(bpg_b, 0, 1, h2 * 512, [[1, 512]]), start=True, stop=False),
                    R=["ones_b", "bpg_b"], W=[("ps", gb[h2])])
                for k in range(8):
                    P.op("tensor", lambda e, k=k, h2=h2: e.matmul(
                        V(PS[gb[h2]], 0, nt, 0, [[1, 512]]), V(x1T[bb], 0, 128, k * 128, [[1, nt]]),
                        V(WB[wg[h2]], 0, 128, k * 512, [[1, 512]]), start=False, stop=(k == 7)),
                        R=[("x1T", bb, 0), ("x1T", bb, 1), ("wb", wg[h2])], W=[("ps", gb[h2])])
                P.op("scalar", lambda e, h2=h2: e.activation(
                    out=V(gsig[b2], 0, nt, h2 * 512, [[1, 512]]), in_=V(PS[gb[h2]], 0, nt, 0, [[1, 512]]), func=AF.Sigmoid),
                    R=[("ps", gb[h2])], W=[("gsig", b2, h2)])
            P.op("vector", lambda e: e.tensor_tensor(
                V(en[be], 0, nt, 0, [[1, D]]), V(en[be], 0, nt, 0, [[1, D]]), V(gsig[b2], 0, nt, 0, [[1, D]]), ALU.mult),
                R=[("en", be, 0), ("en", be, 1), ("gsig", b2, 0), ("gsig", b2, 1)], W=[("en", be, 0), ("en", be, 1)])
            P.op("vector", lambda e: e.tensor_tensor(
                V(yo[b2], 0, nt, 0, [[1, D]]), V(en[be], 0, nt, 0, [[1, D]]), V(x1[b1], 0, nt, 0, [[1, D]]), ALU.add),
                R=[("en", be, 0), ("en", be, 1), ("x1", b1)], W=[("yo", b2)])
            dsty = DR(y_p, r0 * D, [[D, nt], [1, D]]) if r0 < SEQ else DR(y_s, 0, [[D, nt], [1, D]])
            P.dma("sync", lambda e: e.dma_start(out=dsty, in_=V(yo[b2], 0, nt, 0, [[1, D]])),
                  R=[("yo", b2)], W=[("yout", ti)])
        pipeline([tl1, tl2, tl3, tl4, tl5, tl6], len(ttiles))
        P.barrier(final=True)


    except _Stop:
        for st_ in reversed(stacks):
            st_.close()
        P.barrier(final=True)
    with nc.Block() as block:
        P.replay(block)
    for st_ in reversed(stacks):
        st_.close()
    es.close()
    return nc


def _host_layouts(inp, SEQ):
    f = lambda a: np.ascontiguousarray(np.asarray(a, dtype=np.float32))
    out = {}
    out["w_in"] = f(inp["w_in"][0]); out["w_pc"] = f(inp["w_pc"][0]); out["w_ps"] = f(inp["w_ps"][0])
    out["w_glu"] = f(inp["w_glu"][0]); out["w_out"] = f(inp["w_out"][0]); out["w_pg"] = f(inp["w_pg"][0])
    out["w_ple"] = f(inp["w_ple"][0])
    out["rows"] = f(np.stack([inp["g_pre"][0], inp["g_post"][0], inp["g_ple"][0], inp["b_pg"][0]]))
    col8 = lambda v: np.asarray(v).reshape(8, 128).T
    col4 = lambda v: np.asarray(v).reshape(4, 128).T
    out["colv8"] = f(np.stack([col8(inp["conv_b"][0]), col8(inp["ln_g"][0]), col8(inp["ln_b"][0])], axis=1))
    out["colv4"] = f(np.stack([col4(inp["b_glu"][0]), col4(inp["ssm_d"][0])], axis=1))
    out["convw"] = f(np.asarray(inp["conv_w"][0]).T.reshape(8, 128, CK).transpose(1, 0, 2))
    a_re = np.asarray(inp["ssm_a_re"][0]); a_im = np.asarray(inp["ssm_a_im"][0]); ldt = np.asarray(inp["ssm_log_dt"][0])
    b_re = np.asarray(inp["ssm_b_re"][0]); b_im = np.asarray(inp["ssm_b_im"][0])
    c_re = np.asarray(inp["ssm_c_re"][0]); c_im = np.asarray(inp["ssm_c_im"][0])
    PA = np.zeros((128, 3, 4, 64), np.float32)
    BA = np.zeros((128, 2, 4, 64), np.float32)
    MA = np.zeros((128, 8), np.float32)
    PB = np.zeros((128, 3, 16), np.float32)
    CBm = np.zeros((128, 2, 16, 16), np.float32)
    BBm = np.zeros((128, 2, 16, 16), np.float32)
    MB = np.zeros((128, 4, 8), np.float32)
    for blk in range(4):
        for gl in range(8):
            g = blk * 8 + gl
            q, par = gl // 2, gl % 2
            rows = slice(gl * 16, gl * 16 + 16)
            PA[rows, 0, blk, :] = a_re[g][None, :]
            PA[rows, 1, blk, :] = a_im[g][None, :]
            PA[rows, 2, blk, :] = ldt[g]
            BA[rows, 0, blk, :] = b_re[g].T
            BA[rows, 1, blk, :] = b_im[g].T
            MA[rows, gl] = 1.0
            qg = blk * 4 + q
            prow = slice(par * 64, (par + 1) * 64)
            PB[prow, 0, qg] = a_re[g]; PB[prow, 1, qg] = a_im[g]; PB[prow, 2, qg] = ldt[g]
            CBm[prow, 0, qg, :] = c_re[g].T
            CBm[prow, 1, qg, :] = c_im[g].T
            BBm[prow, 0, qg, :] = b_re[g]
            BBm[prow, 1, qg, :] = b_im[g]
            MB[prow, q, gl] = 1.0
    out["PA"] = PA.reshape(128, 3, 256); out["BA"] = BA.reshape(128, 2, 256); out["MA"] = MA
    out["PB"] = PB; out["CB"] = CBm.reshape(128, 2, 256); out["BB"] = BBm.reshape(128, 2, 256); out["MB"] = MB
    out["ident"] = np.eye(128, dtype=np.float32)
    out["iota"] = np.ascontiguousarray(np.broadcast_to(np.arange(SEQ, dtype=np.float32), (128, SEQ)))
    return out


def make_in_maps(inp, SEQ, ncores):
    shared = _host_layouts(inp, SEQ)
    f = lambda a: np.ascontiguousarray(np.asarray(a, dtype=np.float32))
    maps = []
    for i in range(ncores):
        m = dict(shared)
        m["xp"] = f(inp["x_prompt"][i]); m["xs"] = f(inp["x_sample"][i * NS:(i + 1) * NS, 0])
        m["pp"] = f(inp["p_prompt"][0, i]); m["psm"] = f(inp["p_sample"][0, i * NS:(i + 1) * NS, 0])
        m["cache"] = f(inp["cache_conv"][0, i * NS:(i + 1) * NS]).reshape(NS * 30, D)
        m["st_re"] = f(inp["state_ssm_re"][0, i * NS:(i + 1) * NS]).reshape(NS, 2048)
        m["st_im"] = f(inp["state_ssm_im"][0, i * NS:(i + 1) * NS]).reshape(NS, 2048)
        maps.append(m)
    return maps


def assemble(results, SEQ, ncores):
    y_p = np.stack([r["y_p"] for r in results]).reshape(ncores, SEQ, D)
    y_s = np.concatenate([r["y_s"] for r in results]).reshape(ncores * NS, 1, D)
    conv_p = np.stack([r["conv_p"] for r in results]).reshape(1, ncores, 30, D)
    conv_s = np.concatenate([r["conv_s"].reshape(NS, 30, D) for r in results]).reshape(1, ncores * NS, 30, D)
    sre_p = np.stack([r["sre_p"] for r in results]).reshape(1, ncores, 32, 64)
    sim_p = np.stack([r["sim_p"] for r in results]).reshape(1, ncores, 32, 64)
    sre_s = np.concatenate([r["sre_s"] for r in results]).reshape(1, ncores * NS, 32, 64)
    sim_s = np.concatenate([r["sim_s"] for r in results]).reshape(1, ncores * NS, 32, 64)
    return tuple(np.ascontiguousarray(a, dtype=np.float32) for a in
                 (y_p, y_s, conv_p, conv_s, sre_p, sim_p, sre_s, sim_s))


def kernel(**inputs):
    SEQ = 2048
    n = 8
    nc = build_program(SEQ)
    in_maps = make_in_maps(inputs, SEQ, n)
    res = run_bass_kernel_spmd(nc, in_maps, core_ids=list(range(n)))
    return assemble(res.results, SEQ, n)
```
